# Optimizing a Trainium2 kernel written in Bass

```python
import math
import jax, jax.numpy as jnp
from jax import lax
import numpy as np

D_MODEL = 1024
BATCH = 4
SEQ = 8192
DEPTH = 1

HEAD_DIM = 64
V_DIM = 2 * HEAD_DIM
N_HEADS = D_MODEL // V_DIM
D_QK = 2 * N_HEADS * HEAD_DIM
D_V = N_HEADS * V_DIM
D_CONV = D_MODEL
CONV_WIDTH = 31
D_FF = 4 * D_MODEL
N_BRANCH = 2
N_MOD = 6
ROPE_THETA = 10000.0
Q_BLOCK = 128
EPS = 1e-6
D_IN = 2 * D_CONV + 2 * D_QK + D_V + N_BRANCH * D_MODEL

kernel_name = "hybrid_conformer_diffattn_block"


def rms_norm(x, g):
    xf = x.astype(jnp.float32)
    y = xf * lax.rsqrt(jnp.mean(xf * xf, axis=-1, keepdims=True) + EPS)
    return (y * g.astype(jnp.float32)).astype(x.dtype)


def layer_norm(x, g, b):
    xf = x.astype(jnp.float32)
    mu = jnp.mean(xf, axis=-1, keepdims=True)
    xc = xf - mu
    y = xc * lax.rsqrt(jnp.mean(xc * xc, axis=-1, keepdims=True) + EPS)
    return (y * g.astype(jnp.float32) + b.astype(jnp.float32)).astype(x.dtype)


def rope(x, pos):
    half = HEAD_DIM // 2
    inv_freq = ROPE_THETA ** (-jnp.arange(0, HEAD_DIM, 2, dtype=jnp.float32) / HEAD_DIM)
    ang = pos.astype(jnp.float32)[:, None] * inv_freq[None, :]
    cos = jnp.cos(ang)[None, :, None, :]
    sin = jnp.sin(ang)[None, :, None, :]
    xf = x.astype(jnp.float32)
    x1, x2 = xf[..., :half], xf[..., half:]
    return jnp.concatenate([x1 * cos - x2 * sin, x2 * cos + x1 * sin], axis=-1).astype(x.dtype)


def lambda_init(layer_idx):
    return 0.8 - 0.6 * math.exp(-0.3 * (layer_idx - 1))


def causal_diff_attention(q1, q2, k1, k2, v, lam):
    B, T, H, Dh = q1.shape
    nb = T // Q_BLOCK
    scale = 1.0 / math.sqrt(HEAD_DIM)
    kpos = jnp.arange(T)

    def to_blocks(q):
        return q.reshape(B, nb, Q_BLOCK, H, Dh).transpose(1, 0, 2, 3, 4)

    def one_block(args):
        i, qb1, qb2 = args
        qpos = i * Q_BLOCK + jnp.arange(Q_BLOCK)
        mask = kpos[None, :] <= qpos[:, None]

        def probs(qb, k):
            s = jnp.einsum('bqhd,bkhd->bhqk', qb, k, preferred_element_type=jnp.float32) * scale
            s = jnp.where(mask[None, None], s, -jnp.inf)
            return jax.nn.softmax(s, axis=-1)

        a = probs(qb1, k1) - lam * probs(qb2, k2)
        return jnp.einsum('bhqk,bkhe->bqhe', a.astype(v.dtype), v)

    out = lax.map(one_block, (jnp.arange(nb), to_blocks(q1), to_blocks(q2)))
    return out.transpose(1, 0, 2, 3, 4).reshape(B, T, H, V_DIM)


def setup_inputs(seed: int = 0) -> dict:
    key = jax.random.key(seed)
    ks = jax.random.split(key, 24)
    f32 = jnp.float32
    L = DEPTH

    def nrm(k, shape, s):
        return jax.random.normal(k, shape, f32) * s

    def gain(k, shape):
        return 1.0 + 0.02 * jax.random.normal(k, shape, f32)

    return {
        "x": jax.random.normal(ks[0], (BATCH, SEQ, D_MODEL), f32),
        "c": jax.random.normal(ks[1], (BATCH, D_MODEL), f32),
        "w_ada": nrm(ks[2], (L, D_MODEL, N_MOD * D_MODEL), D_MODEL ** -0.5),
        "b_ada": nrm(ks[3], (L, N_MOD * D_MODEL), 0.01),
        "pre_norm1": gain(ks[4], (L, D_MODEL)),
        "post_norm1": gain(ks[5], (L, D_MODEL)),
        "w_in": nrm(ks[6], (L, D_MODEL, D_IN), D_MODEL ** -0.5),
        "conv_w": nrm(ks[7], (L, CONV_WIDTH, D_CONV), CONV_WIDTH ** -0.5),
        "conv_b": nrm(ks[8], (L, D_CONV), 0.01),
        "conv_ln_g": gain(ks[9], (L, D_CONV)),
        "conv_ln_b": nrm(ks[10], (L, D_CONV), 0.01),
        "conv_w_out": nrm(ks[11], (L, D_CONV, D_MODEL), D_CONV ** -0.5),
        "conv_b_out": nrm(ks[12], (L, D_MODEL), 0.01),
        "lambda_q1": nrm(ks[13], (L, HEAD_DIM), 0.1),
        "lambda_k1": nrm(ks[14], (L, HEAD_DIM), 0.1),
        "lambda_q2": nrm(ks[15], (L, HEAD_DIM), 0.1),
        "lambda_k2": nrm(ks[16], (L, HEAD_DIM), 0.1),
        "head_norm": gain(ks[17], (L, V_DIM)),
        "w_o": nrm(ks[18], (L, D_MODEL, D_MODEL), D_MODEL ** -0.5),
        "pre_norm2": gain(ks[19], (L, D_MODEL)),
        "post_norm2": gain(ks[20], (L, D_MODEL)),
        "w_ff1": nrm(ks[21], (L, D_MODEL, D_FF), D_MODEL ** -0.5),
        "w_ff2": nrm(ks[22], (L, D_FF, D_MODEL), D_FF ** -0.5),
    }


def reference(x, c, w_ada, b_ada, pre_norm1, post_norm1, w_in, conv_w, conv_b, conv_ln_g,
              conv_ln_b, conv_w_out, conv_b_out, lambda_q1, lambda_k1, lambda_q2, lambda_k2,
              head_norm, w_o, pre_norm2, post_norm2, w_ff1, w_ff2):
    B, T, _ = x.shape
    pos = jnp.arange(T)
    cs = jax.nn.silu(c)
    for l in range(DEPTH):
        mod = (cs @ w_ada[l] + b_ada[l])[:, None, :]
        sh1, sc1, g1, sh2, sc2, g2 = jnp.split(mod, N_MOD, axis=-1)

        h = rms_norm(x, pre_norm1[l]) * (1.0 + sc1) + sh1
        proj = h @ w_in[l]
        o1 = 2 * D_CONV
        o2 = o1 + D_QK
        o3 = o2 + D_QK
        o4 = o3 + D_V
        u_glu, q, k, v, gate_logits = proj[..., :o1], proj[..., o1:o2], proj[..., o2:o3], proj[..., o3:o4], proj[..., o4:]

        ua, ub = jnp.split(u_glu, 2, axis=-1)
        u = ua * jax.nn.sigmoid(ub)
        u = lax.conv_general_dilated(
            u, conv_w[l][:, None, :].astype(u.dtype), window_strides=(1,),
            padding=[(CONV_WIDTH - 1, 0)], dimension_numbers=('NWC', 'WIO', 'NWC'),
            feature_group_count=D_CONV) + conv_b[l]
        u = jax.nn.silu(layer_norm(u, conv_ln_g[l], conv_ln_b[l]))
        y_conv = u @ conv_w_out[l] + conv_b_out[l]

        q = q.reshape(B, T, 2, N_HEADS, HEAD_DIM)
        k = k.reshape(B, T, 2, N_HEADS, HEAD_DIM)
        v = v.reshape(B, T, N_HEADS, V_DIM)
        q1, q2 = rope(q[:, :, 0], pos), rope(q[:, :, 1], pos)
        k1, k2 = rope(k[:, :, 0], pos), rope(k[:, :, 1], pos)
        lam_init = lambda_init(l + 1)
        lam = (jnp.exp(jnp.sum(lambda_q1[l].astype(jnp.float32) * lambda_k1[l].astype(jnp.float32)))
               - jnp.exp(jnp.sum(lambda_q2[l].astype(jnp.float32) * lambda_k2[l].astype(jnp.float32)))
               + lam_init)
        att = causal_diff_attention(q1, q2, k1, k2, v, lam)
        att = rms_norm(att, head_norm[l]) * (1.0 - lam_init)
        y_att = att.reshape(B, T, D_V)

        g_conv, g_att = jnp.split(jax.nn.sigmoid(gate_logits), N_BRANCH, axis=-1)
        y = (g_conv * y_conv + g_att * y_att) @ w_o[l]
        x = x + g1 * rms_norm(y, post_norm1[l])

        h = rms_norm(x, pre_norm2[l]) * (1.0 + sc2) + sh2
        f = jnp.square(jax.nn.relu(h @ w_ff1[l])) @ w_ff2[l]
        x = x + g2 * rms_norm(f, post_norm2[l])
    return x
```

```python
import numpy as np
import concourse.bass as bass
import concourse.mybir as mybir

F32 = mybir.dt.float32
BF16 = mybir.dt.bfloat16
AF = mybir.ActivationFunctionType
ALU = mybir.AluOpType

ENGS = ("tensor", "vector", "scalar", "gpsimd", "sync")
N_DMA_SEMS = 32
SAME_ENGINE_SYNC = True


class Res:
    __slots__ = ("t", "last_w", "readers", "name", "parent", "kids")

    def __init__(self, t, name, parent=None):
        self.t = t
        self.last_w = None
        self.readers = []
        self.name = name
        self.parent = parent
        self.kids = {}

    def s(self, k):
        if k not in self.kids:
            self.kids[k] = Res(self.t, f"{self.name}.{k}", parent=self)
        return self.kids[k]

    def holders(self):
        h = [self]
        if self.parent is not None:
            h.append(self.parent)
        h.extend(self.kids.values())
        return h

    def __getitem__(self, idx):
        return self.t[idx]


class Op:
    __slots__ = ("eng", "fn", "deps", "signal", "count", "is_dma", "sem", "semval", "barrier", "emitted")

    def __init__(self, eng, fn, is_dma=False):
        self.eng = eng
        self.fn = fn
        self.deps = []
        self.signal = False
        self.count = None
        self.is_dma = is_dma
        self.sem = None
        self.semval = None
        self.barrier = False
        self.emitted = False


class Prog:
    def __init__(self, nc):
        self.nc = nc
        self.ops = {e: [] for e in ENGS}
        self.esem = {e: nc.alloc_semaphore(name=f"s_{e}") for e in ENGS}
        self.ecount = {e: 0 for e in ENGS}
        self.waited = {e: {} for e in ENGS}
        self.nd = 2 * N_DMA_SEMS
        self.dsems = [nc.alloc_semaphore(name=f"d_{i}") for i in range(self.nd)]
        self.dsem_uses = [0] * self.nd
        self.dsem_last = [None] * self.nd
        self.dma_rr = {"sync": 0, "gpsimd": 0}
        self.last_op = {e: None for e in ENGS}
        self.n_ops = 0

    def sb(self, ctx, name, shape, dtype):
        t = ctx.enter_context(self.nc.sbuf_tensor(name, list(shape), dtype))
        return Res(t, name)

    def ps(self, ctx, name, shape, dtype=F32):
        t = ctx.enter_context(self.nc.psum_tensor(name, list(shape), dtype))
        return Res(t, name)

    def _deps(self, op, r, w):
        deps = []
        for x0 in r:
            for x in x0.holders():
                if x.last_w is not None:
                    deps.append(x.last_w)
        for x0 in w:
            for x in x0.holders():
                if x.last_w is not None:
                    deps.append(x.last_w)
                deps.extend(x.readers)
        seen = set()
        for d in deps:
            if id(d) in seen or d is op or d.emitted:
                continue
            seen.add(id(d))
            if (not d.is_dma) and (not op.is_dma) and d.eng == op.eng and (d.eng == "tensor" or not SAME_ENGINE_SYNC):
                continue
            op.deps.append(d)
            if not d.is_dma:
                d.signal = True
        for x in r:
            if op.is_dma:
                x.readers.append(op)
            else:
                x.readers = [o for o in x.readers if o.is_dma or o.eng != op.eng]
                x.readers.append(op)
        for x in w:
            x.last_w = op
            x.readers = []

    def add(self, eng, fn, r=(), w=()):
        op = Op(eng, fn)
        self._deps(op, r, w)
        self.ops[eng].append(op)
        self.last_op[eng] = op
        self.n_ops += 1
        return op

    def dma(self, fn, r=(), w=(), q="sync"):
        op = Op(q, fn, is_dma=True)
        k = self.dma_rr[q] % N_DMA_SEMS + (N_DMA_SEMS if q == "gpsimd" else 0)
        self.dma_rr[q] += 1
        prev = self.dsem_last[k]
        self.dsem_uses[k] += 1
        op.sem = k
        op.semval = 16 * self.dsem_uses[k]
        if prev is not None and not prev.emitted:
            op.deps.append(prev)
        self._deps(op, r, w)
        self.dsem_last[k] = op
        self.ops[q].append(op)
        self.n_ops += 1
        return op

    def barrier(self):
        targets = []
        for e in ENGS:
            lo = self.last_op[e]
            if lo is not None:
                lo.signal = True
                targets.append(lo)
        for k in range(self.nd):
            if self.dsem_last[k] is not None:
                targets.append(self.dsem_last[k])
        for e in ENGS:
            op = Op(e, None)
            op.barrier = True
            op.deps = [t for t in targets if t.is_dma or t.eng != e]
            self.ops[e].append(op)

    def emit(self, name=None):
        nc = self.nc
        for e in ENGS:
            c = self.ecount[e]
            for op in self.ops[e]:
                if op.is_dma or op.barrier:
                    continue
                if op.signal:
                    c += 1
                    op.count = c
            self.ecount[e] = c

        def emit_engine(e, eng):
            waited = self.waited[e]
            for op in self.ops[e]:
                for d in op.deps:
                    if d.is_dma:
                        key, val, sem = ("d", d.sem), d.semval, self.dsems[d.sem]
                    else:
                        assert d.count is not None, "dependency on non-signalling op"
                        key, val, sem = ("e", d.eng), d.count, self.esem[d.eng]
                    if waited.get(key, 0) >= val:
                        continue
                    eng.wait_ge(sem, val)
                    waited[key] = val
                if op.barrier:
                    continue
                ins = op.fn(eng)
                if op.is_dma:
                    ins.then_inc(self.dsems[op.sem], 16)
                elif op.signal:
                    ins.then_inc(self.esem[e], 1)

        with nc.Block() as block:
            @block.tensor
            def _(eng):
                emit_engine("tensor", eng)

            @block.vector
            def _(eng):
                emit_engine("vector", eng)

            @block.scalar
            def _(eng):
                emit_engine("scalar", eng)

            @block.gpsimd
            def _(eng):
                emit_engine("gpsimd", eng)

            @block.sync
            def _(eng):
                emit_engine("sync", eng)
        for e in ENGS:
            for op in self.ops[e]:
                op.emitted = True
        self.ops = {e: [] for e in ENGS}

    def mm(self, out, lhsT, rhs, start, stop, r, w, **kw):
        return self.add("tensor", lambda e: e.matmul(out, lhsT, rhs, start=start, stop=stop, **kw), r=r, w=w)

    def act(self, out, in_, func, r, w, bias=None, scale=None, eng="scalar"):
        kw = {}
        if bias is not None:
            kw["bias"] = bias
        if scale is not None:
            kw["scale"] = scale
        return self.add(eng, lambda e: e.activation(out, in_, func, **kw), r=r, w=w)

    def tt(self, out, in0, in1, op, r, w, eng="vector"):
        return self.add(eng, lambda e: e.tensor_tensor(out, in0, in1, op), r=r, w=w)

    def ts(self, out, in0, s1, s2, op0, op1, r, w, eng="vector"):
        if op1 is None:
            return self.add(eng, lambda e: e.tensor_scalar(out, in0, s1, None, op0), r=r, w=w)
        return self.add(eng, lambda e: e.tensor_scalar(out, in0, s1, s2, op0, op1), r=r, w=w)

    def stt(self, out, in0, scalar, in1, op0, op1, r, w):
        return self.add("vector", lambda e: e.scalar_tensor_tensor(out, in0, scalar, in1, op0, op1), r=r, w=w)

    def copy(self, out, in_, r, w, eng="vector"):
        return self.add(eng, lambda e: e.tensor_copy(out, in_), r=r, w=w)

    def recip(self, out, in_, r, w):
        return self.add("vector", lambda e: e.reciprocal(out, in_), r=r, w=w)

    def memset(self, ap, val, w, eng="vector"):
        return self.add(eng, lambda e: e.memset(ap, val), r=(), w=w)

    def load(self, out, in_, w, r=(), q="sync", **kw):
        return self.dma(lambda e: e.dma_start(out=out, in_=in_, **kw), r=r, w=w, q=q)

    def store(self, out, in_, r, w=(), q="sync", **kw):
        return self.dma(lambda e: e.dma_start(out=out, in_=in_, **kw), r=r, w=w, q=q)


from contextlib import ExitStack
from concourse.bass_utils import run_bass_kernel_spmd

D = 1024
T = 8192
NB = 4
NCH = 16
NSL = 8
CH = 512
HALO = 32
XW = CH + HALO
EPS = 1e-6
LAM_INIT = 0.2
OWN = {0: [0, 3, 4, 7, 8, 11, 12, 15], 1: [1, 2, 5, 6, 9, 10, 13, 14]}

SM_BADA = 0
SM_PN1 = 48
SM_PO1 = 56
SM_PN2 = 64
SM_PO2 = 72
SM_CB = 80
SM_LNG = 88
SM_LNB = 96
SM_BOUT = 104
SM_HN = 112
SM_FLAG = 113
SM_CT = 121
SM_LAM = 137
SM_CW = 393
NS = 393 + 248


class Pool:
    def __init__(self, items):
        self.items = items
        self.i = 0

    def next(self):
        x = self.items[self.i % len(self.items)]
        self.i += 1
        return x


def build_program(phases="ABDC", dbg=False):
    nc = bass.Bass("TRN2", target_bir_lowering=False)

    def din(name, shape, dt=F32):
        return nc.dram_tensor(name, list(shape), dt, kind="ExternalInput").ap()

    xT_all = din("xT_all", [D, T])
    xT_own = din("xT_own", [D, NSL * XW])
    w_ada = din("w_ada", [D, 6 * D])
    w_in = din("w_in", [D, 7 * D])
    w_co = din("w_co", [D, D])
    w_o = din("w_o", [D, D])
    w_ff1 = din("w_ff1", [D, 4 * D])
    w_ff2 = din("w_ff2", [4 * D, D])
    small = din("small", [128, NS])
    cos_all = din("cos_all", [128, T])
    sin_all = din("sin_all", [128, T])
    cos_own = din("cos_own", [128, NSL * CH])
    sin_own = din("sin_own", [128, NSL * CH])
    masks = din("masks", [128, 16 * CH])
    rmat = din("rmat", [128, 128])
    outT = nc.dram_tensor("outT", [D, NSL * CH], F32, kind="ExternalOutput").ap()

    def dscr(name, shape, dt=BF16):
        return nc.dram_tensor(name, list(shape), dt, kind="ExternalOutput" if dbg else "Internal").ap()

    dbg_small = dscr("dbg_small", [128, 88], F32) if dbg else None

    Kscr = dscr("Kscr", [8, 128, T])
    Vscr = dscr("Vscr", [T, D])
    Qscr = dscr("Qscr", [8, 128, NSL * CH])
    ZCscr = dscr("ZCscr", [D, NSL * CH])
    GAscr = dscr("GAscr", [D, NSL * CH])
    Zscr = dscr("Zscr", [D, NSL * CH])

    P = Prog(nc)
    MUL, ADD, SUB, MAX = ALU.mult, ALU.add, ALU.subtract, ALU.max

    def fm(ap):
        return ap.rearrange("(j p) t -> p j t", p=128)

    with ExitStack() as g:
        SM = P.sb(g, "SM", [128, NS], F32)
        MODV = P.sb(g, "MODV", [128, 48], F32)
        DV = P.sb(g, "DV", [128, 40], F32)
        ONESB = P.sb(g, "ONESB", [128, 128], BF16)
        RM = P.sb(g, "RM", [128, 128], BF16)

        def smc(off, n=1):
            return SM[:, off:off + n]

        SH1 = lambda j: MODV[:, 0 + j:1 + j]
        SH2 = lambda j: MODV[:, 24 + j:25 + j]
        A1 = lambda j: DV[:, 0 + j:1 + j]
        A2 = lambda j: DV[:, 8 + j:9 + j]
        GP1 = lambda j: DV[:, 16 + j:17 + j]
        GP2 = lambda j: DV[:, 24 + j:25 + j]
        NEGLAM = DV[:, 32:33]
        HNV = DV[:, 33:34]

        with ExitStack() as c0:
            WA = [P.sb(c0, f"WA{i}", [128, 8, 768], F32) for i in range(2)]
            CS = P.sb(c0, "CS", [128, 16], F32)
            TMPL = P.sb(c0, "TMPL", [128, 64], F32)
            SS = P.sb(c0, "SSl", [128, 4], F32)
            MODP = P.ps(c0, "MODP", [128, 96], F32)
            P.load(SM[:], small, w=[SM])
            P.load(RM[:], rmat, w=[RM], q="gpsimd")
            P.memset(ONESB[:], 1.0, w=[ONESB])
            P.act(CS[:], smc(SM_CT, 16), AF.Silu, r=[SM], w=[CS])
            for gi in range(8):
                Wt = WA[gi % 2]
                P.load(Wt[:], fm(w_ada[:, gi * 768:(gi + 1) * 768]), w=[Wt])
                for nn in range(6):
                    n = gi * 6 + nn
                    for j in range(8):
                        P.mm(MODP[:, 2 * n:2 * n + 2], Wt[:, j, nn * 128:(nn + 1) * 128], CS[:, 2 * j:2 * j + 2],
                             j == 0, j == 7, r=[Wt, CS], w=[MODP])
            modp_v = MODP[:].rearrange("p (n two) -> p n two", two=2)[:, :, 0]
            P.tt(MODV[:], modp_v, smc(SM_BADA, 48), ADD, r=[MODP, SM], w=[MODV])
            P.stt(DV[:, 0:8], MODV[:, 8:16], 1.0, smc(SM_PN1, 8), ADD, MUL, r=[MODV, SM], w=[DV])
            P.stt(DV[:, 8:16], MODV[:, 32:40], 1.0, smc(SM_PN2, 8), ADD, MUL, r=[MODV, SM], w=[DV])
            P.tt(DV[:, 16:24], MODV[:, 16:24], smc(SM_PO1, 8), MUL, r=[MODV, SM], w=[DV])
            P.tt(DV[:, 24:32], MODV[:, 40:48], smc(SM_PO2, 8), MUL, r=[MODV, SM], w=[DV])
            for k in range(2):
                P.tt(TMPL[:], smc(SM_LAM + 128 * k, 64), smc(SM_LAM + 128 * k + 64, 64), MUL, r=[SM], w=[TMPL])
                P.add("vector", lambda e, k=k: e.reduce_sum(SS[:, k:k + 1], TMPL[:], mybir.AxisListType.X),
                      r=[TMPL], w=[SS])
            P.act(SS[:, 2:4], SS[:, 0:2], AF.Exp, r=[SS], w=[SS])
            P.stt(DV[:, 32:33], SS[:, 3:4], -LAM_INIT, SS[:, 2:3], ADD, SUB, r=[SS], w=[DV])
            P.ts(DV[:, 33:34], smc(SM_HN, 1), 1.0 - LAM_INIT, None, MUL, None, r=[SM], w=[DV])
            if dbg:
                P.store(dbg_small[:, 0:48], MODV[:], r=[MODV])
                P.store(dbg_small[:, 48:82], DV[:, 0:34], r=[DV])
            P.barrier()
            P.emit()

        def norm_H(X, H, t0, t1, avec, shvec, SQ, SSb, SR, RS, TMPs):
            n = t1 - t0
            P.act(SQ[:, :, t0:t1], X[:, :, t0:t1], AF.Square, r=[X], w=[SQ])
            for j in range(8):
                P.mm(SSb[:, 0:n], ONESB[:], SQ[:, j, t0:t1], j == 0, j == 7, r=[ONESB, SQ], w=[SSb])
            P.act(SR[:, 0:n], SSb[:, 0:n], AF.Sqrt, r=[SSb], w=[SR], bias=EPS, scale=1.0 / D)
            P.recip(RS[:, 0:n], SR[:, 0:n], r=[SR], w=[RS])
            for j in range(8):
                Tm = TMPs.next()
                P.stt(Tm[:, 0:n], X[:, j, t0:t1], avec(j), RS[:, 0:n], MUL, MUL, r=[X, DV, RS], w=[Tm])
                P.act(H[:, j, t0:t1], Tm[:, 0:n], AF.Identity, r=[Tm, MODV], w=[H.s((j, t0))], bias=shvec(j), scale=1.0)

        def load_w(Wt, src, ncols, c0col=0):
            for cc in range(0, ncols, 512):
                P.load(Wt[:, :, cc:cc + 512], fm(src[:, c0col + cc:c0col + cc + 512]), w=[Wt], q="gpsimd")

        if "A" in phases:
          with ExitStack() as ca:
            WK = P.sb(ca, "WK", [128, 8, 1024], BF16)
            WV = P.sb(ca, "WV", [128, 8, 1024], BF16)
            load_w(WK, w_in, 1024, 3072)
            load_w(WV, w_in, 1024, 4096)
            X = [P.sb(ca, f"XA{i}", [128, 8, CH], F32) for i in range(2)]
            H = [P.sb(ca, f"HA{i}", [128, 8, CH], BF16) for i in range(2)]
            SQ = P.sb(ca, "SQA", [128, 8, CH], BF16)
            SR = P.sb(ca, "SRA", [128, CH], F32)
            RS = P.sb(ca, "RSA", [128, CH], F32)
            TMPs = Pool([P.sb(ca, f"TMA{i}", [128, CH], F32) for i in range(3)])
            COS = [P.sb(ca, f"COSA{i}", [128, CH], F32) for i in range(2)]
            SIN = [P.sb(ca, f"SINA{i}", [128, CH], F32) for i in range(2)]
            KRAW = Pool([P.sb(ca, f"KRAW{i}", [128, CH], BF16) for i in range(2)])
            T1 = Pool([P.sb(ca, f"T1A{i}", [128, CH], F32) for i in range(2)])
            T2 = Pool([P.sb(ca, f"T2A{i}", [128, CH], F32) for i in range(2)])
            KRO = Pool([P.sb(ca, f"KRO{i}", [128, CH], BF16) for i in range(3)])
            VSB = Pool([P.sb(ca, f"VSB{i}", [128, 1024], BF16) for i in range(2)])
            SSb = P.ps(ca, "SSbA", [128, CH])
            PP = Pool([P.ps(ca, f"PPA{i}", [128, CH]) for i in range(6)])

            def loads_a(c):
                P.load(X[c % 2][:], fm(xT_all[:, c * CH:(c + 1) * CH]), w=[X[c % 2]])
                P.load(COS[c % 2][:], cos_all[:, c * CH:(c + 1) * CH], w=[COS[c % 2]])
                P.load(SIN[c % 2][:], sin_all[:, c * CH:(c + 1) * CH], w=[SIN[c % 2]])

            loads_a(0)
            for c in range(NCH_RUN):
                if c + 1 < NCH_RUN:
                    loads_a(c + 1)
                Xc, Hc, Cc, Sc = X[c % 2], H[c % 2], COS[c % 2], SIN[c % 2]
                if "n" in A_PARTS:
                    norm_H(Xc, Hc, 0, CH, A1, SH1, SQ, SSb, SR, RS, TMPs)
                for n in range(8 if "k" in A_PARTS else 0):
                    KP = PP.next()
                    for j in range(8):
                        P.mm(KP[:], WK[:, j, n * 128:(n + 1) * 128], Hc[:, j, :], j == 0, j == 7, r=[WK, Hc.s((j, 0))], w=[KP])
                    Kr = KRAW.next()
                    P.act(Kr[:], KP[:], AF.Identity, r=[KP], w=[Kr])
                    KR = PP.next()
                    P.mm(KR[:], RM[:], Kr[:], True, True, r=[RM, Kr], w=[KR])
                    t1, t2, ko = T1.next(), T2.next(), KRO.next()
                    P.tt(t1[:], Kr[:], Cc[:], MUL, r=[Kr, Cc], w=[t1])
                    P.tt(t2[:], KR[:], Sc[:], MUL, r=[KR, Sc], w=[t2])
                    P.tt(ko[:], t1[:], t2[:], ADD, r=[t1, t2], w=[ko])
                    P.store(Kscr[n, :, c * CH:(c + 1) * CH], ko[:], r=[ko], q=STQ)
                for tb in range(4 if "v" in A_PARTS else 0):
                    Vs = VSB.next()
                    for cg in range(2):
                        VP = PP.next()
                        for j in range(8):
                            P.mm(VP[:], Hc[:, j, tb * 128:(tb + 1) * 128], WV[:, j, cg * 512:(cg + 1) * 512],
                                 j == 0, j == 7, r=[Hc.s((j, 0)), WV], w=[VP])
                        if cg == 0:
                            P.act(Vs[:, 0:512], VP[:], AF.Identity, r=[VP], w=[Vs.s(0)])
                        else:
                            P.copy(Vs[:, 512:1024], VP[:], r=[VP], w=[Vs.s(1)])
                    r0 = (c * 4 + tb) * 128
                    P.store(Vscr[r0:r0 + 128, :], Vs[:], r=[Vs], q=STQ)
            P.barrier()
            P.emit()

        if "B" in phases:
          with ExitStack() as cb:
            WUA = P.sb(cb, "WUA", [128, 8, 1024], BF16)
            WUB = P.sb(cb, "WUB", [128, 8, 1024], BF16)
            WQ = P.sb(cb, "WQ", [128, 8, 1024], BF16)
            WG = P.sb(cb, "WG", [128, 8, 2048], BF16)
            WCO = P.sb(cb, "WCO", [128, 8, 1024], BF16)
            load_w(WUA, w_in, 1024, 0)
            load_w(WUB, w_in, 1024, 1024)
            load_w(WQ, w_in, 1024, 2048)
            load_w(WG, w_in, 2048, 5120)
            load_w(WCO, w_co, 1024, 0)
            X = P.sb(cb, "XB", [128, 8, XW], F32)
            H = P.sb(cb, "HB", [128, 8, XW], BF16)
            U = P.sb(cb, "UB", [128, 8, XW], F32)
            CV = P.sb(cb, "CVB", [128, 8, CH], F32)
            S = P.sb(cb, "SB", [128, 8, XW], BF16)
            SQ = S
            SR = P.sb(cb, "SRB", [128, CH], F32)
            RS = P.sb(cb, "RSB", [128, CH], F32)
            MEAN = P.sb(cb, "MEANB", [128, CH], F32)
            MSQ = SR
            VAR = RS
            NBt = P.sb(cb, "NBB", [128, CH], F32)
            TMPs = Pool([P.sb(cb, f"TMB{i}", [128, CH], F32) for i in range(4)])
            TMP2 = TMPs
            SG = Pool([P.sb(cb, f"SGB{i}", [128, CH], F32) for i in range(2)])
            CB16 = Pool([P.sb(cb, f"CB16{i}", [128, CH], BF16) for i in range(2)])
            CQ16 = Pool([P.sb(cb, f"CQ16{i}", [128, CH], BF16) for i in range(2)])
            OB16 = Pool([P.sb(cb, f"OB16{i}", [128, CH], BF16) for i in range(3)])
            COS = [P.sb(cb, f"COSB{i}", [128, CH], F32) for i in range(2)]
            SIN = [P.sb(cb, f"SINB{i}", [128, CH], F32) for i in range(2)]
            SSb = P.ps(cb, "SSbB", [128, CH])
            ACC2 = P.ps(cb, "ACC2B", [128, CH])
            PP = Pool([P.ps(cb, f"PPB{i}", [128, CH]) for i in range(6)])
            CW = lambda n, j: SM[:, SM_CW + n * 31 + j:SM_CW + n * 31 + j + 1]

            def loads_b_tabs(i):
                P.load(COS[i % 2][:], cos_own[:, i * CH:(i + 1) * CH], w=[COS[i % 2]])
                P.load(SIN[i % 2][:], sin_own[:, i * CH:(i + 1) * CH], w=[SIN[i % 2]])

            P.load(X[:], fm(xT_own[:, 0:XW]), w=[X])
            loads_b_tabs(0)
            for i in range(NSL):
                if i + 1 < NSL:
                    loads_b_tabs(i + 1)
                Cc, Sc = COS[i % 2], SIN[i % 2]
                norm_H(X, H, HALO, XW, A1, SH1, SQ, SSb, SR, RS, TMPs)
                norm_H(X, H, 0, HALO, A1, SH1, SQ, SSb, SR, RS, TMPs)
                if i + 1 < NSL:
                    P.load(X[:], fm(xT_own[:, (i + 1) * XW:(i + 2) * XW]), w=[X])
                for n in range(8):
                    for (t0, t1) in ((HALO, XW), (0, HALO)):
                        nn = t1 - t0
                        PA, PB = PP.next(), PP.next()
                        for j in range(8):
                            P.mm(PA[:, 0:nn], WUA[:, j, n * 128:(n + 1) * 128], H[:, j, t0:t1], j == 0, j == 7,
                                 r=[WUA, H.s((j, t0))], w=[PA])
                        for j in range(8):
                            P.mm(PB[:, 0:nn], WUB[:, j, n * 128:(n + 1) * 128], H[:, j, t0:t1], j == 0, j == 7,
                                 r=[WUB, H.s((j, t0))], w=[PB])
                        sg = SG.next()
                        P.act(sg[:, 0:nn], PB[:, 0:nn], AF.Sigmoid, r=[PB], w=[sg])
                        if t0 == 0:
                            P.stt(U[:, n, t0:t1], PA[:, 0:nn], SM[:, SM_FLAG + i:SM_FLAG + i + 1], sg[:, 0:nn], MUL, MUL,
                                  r=[PA, sg, SM], w=[U.s((n, t0))])
                        else:
                            P.tt(U[:, n, t0:t1], PA[:, 0:nn], sg[:, 0:nn], MUL, r=[PA, sg], w=[U.s((n, t0))])
                for n in range(8):
                    P.ts(CV[:, n, :], U[:, n, 2:2 + CH], CW(n, 0), SM[:, SM_CB + n:SM_CB + n + 1], MUL, ADD,
                         r=[U.s((n, 0)), U.s((n, HALO)), SM], w=[CV.s(n)])
                for j in range(1, 31):
                    for n in range(8):
                        P.stt(CV[:, n, :], U[:, n, 2 + j:2 + j + CH], CW(n, j), CV[:, n, :], MUL, ADD,
                              r=[U.s((n, 0)), U.s((n, HALO)), SM, CV.s(n)], w=[CV.s(n)])
                for n in range(8):
                    cb16, cq16 = CB16.next(), CQ16.next()
                    P.act(cb16[:], CV[:, n, :], AF.Identity, r=[CV.s(n)], w=[cb16])
                    P.act(cq16[:], CV[:, n, :], AF.Square, r=[CV.s(n)], w=[cq16])
                    P.mm(SSb[:], ONESB[:], cb16[:], n == 0, n == 7, r=[ONESB, cb16], w=[SSb])
                    P.mm(ACC2[:], ONESB[:], cq16[:], n == 0, n == 7, r=[ONESB, cq16], w=[ACC2])
                P.ts(MEAN[:], SSb[:], 1.0 / D, None, MUL, None, r=[SSb], w=[MEAN])
                P.tt(MSQ[:], MEAN[:], MEAN[:], MUL, r=[MEAN], w=[MSQ])
                P.stt(VAR[:], ACC2[:], 1.0 / D, MSQ[:], MUL, SUB, r=[ACC2, MSQ], w=[VAR])
                P.act(SR[:], VAR[:], AF.Sqrt, r=[VAR], w=[SR], bias=EPS, scale=1.0)
                P.recip(RS[:], SR[:], r=[SR], w=[RS])
                P.stt(NBt[:], MEAN[:], -1.0, RS[:], MUL, MUL, r=[MEAN, RS], w=[NBt])
                for n in range(8):
                    ta, tb_ = TMPs.next(), TMP2.next()
                    P.tt(ta[:], CV[:, n, :], RS[:], MUL, r=[CV.s(n), RS], w=[ta])
                    P.tt(tb_[:], ta[:], NBt[:], ADD, r=[ta, NBt], w=[tb_])
                    P.act(S[:, n, 0:CH], tb_[:], AF.Silu, r=[tb_, SM], w=[S.s(n)],
                          bias=SM[:, SM_LNB + n:SM_LNB + n + 1], scale=SM[:, SM_LNG + n:SM_LNG + n + 1])
                for n in range(8):
                    PY, PG = PP.next(), PP.next()
                    for j in range(8):
                        P.mm(PY[:], WCO[:, j, n * 128:(n + 1) * 128], S[:, j, 0:CH], j == 0, j == 7, r=[WCO, S.s(j)], w=[PY])
                    for j in range(8):
                        P.mm(PG[:], WG[:, j, n * 128:(n + 1) * 128], H[:, j, HALO:XW], j == 0, j == 7, r=[WG, H.s((j, HALO))], w=[PG])
                    sg = SG.next()
                    P.act(sg[:], PG[:], AF.Sigmoid, r=[PG], w=[sg])
                    ob = OB16.next()
                    P.stt(ob[:], PY[:], SM[:, SM_BOUT + n:SM_BOUT + n + 1], sg[:], ADD, MUL, r=[PY, sg, SM], w=[ob])
                    P.store(ZCscr[n * 128:(n + 1) * 128, i * CH:(i + 1) * CH], ob[:], r=[ob], q=STQ)
                for n in range(8):
                    PG = PP.next()
                    for j in range(8):
                        P.mm(PG[:], WG[:, j, 1024 + n * 128:1024 + (n + 1) * 128], H[:, j, HALO:XW], j == 0, j == 7,
                             r=[WG, H.s((j, HALO))], w=[PG])
                    ob = OB16.next()
                    P.act(ob[:], PG[:], AF.Sigmoid, r=[PG], w=[ob])
                    P.store(GAscr[n * 128:(n + 1) * 128, i * CH:(i + 1) * CH], ob[:], r=[ob], q=STQ)
                for n in range(8):
                    PQ = PP.next()
                    for j in range(8):
                        P.mm(PQ[:], WQ[:, j, n * 128:(n + 1) * 128], H[:, j, HALO:XW], j == 0, j == 7, r=[WQ, H.s((j, HALO))], w=[PQ])
                    qr = CB16.next()
                    P.act(qr[:], PQ[:], AF.Identity, r=[PQ], w=[qr])
                    PR = PP.next()
                    P.mm(PR[:], RM[:], qr[:], True, True, r=[RM, qr], w=[PR])
                    t1, t2, ob = TMPs.next(), TMP2.next(), OB16.next()
                    P.tt(t1[:], qr[:], Cc[:], MUL, r=[qr, Cc], w=[t1])
                    P.tt(t2[:], PR[:], Sc[:], MUL, r=[PR, Sc], w=[t2])
                    P.tt(ob[:], t1[:], t2[:], ADD, r=[t1, t2], w=[ob])
                    P.store(Qscr[n, :, i * CH:(i + 1) * CH], ob[:], r=[ob], q=STQ)
            P.barrier()
            P.emit()

        if "D" in phases:
          with ExitStack() as cd:
            KH = [P.sb(cd, f"KH{i}", [128, T], BF16) for i in range(2)]
            VH = [P.sb(cd, f"VH{i}", [128, 64, 128], BF16) for i in range(2)]
            QH = [P.sb(cd, f"QH{i}", [128, NSL * CH], BF16) for i in range(2)]
            GAH = [P.sb(cd, f"GAH{i}", [128, NSL * CH], BF16) for i in range(2)]
            ZCH = [P.sb(cd, f"ZCH{i}", [128, NSL * CH], BF16) for i in range(2)]
            MK = P.sb(cd, "MK", [128, 16, CH], BF16)
            ONESF = P.sb(cd, "ONESF", [128, 128], F32)
            PT = Pool([P.sb(cd, f"PT{i}", [128, 2 * CH], BF16) for i in range(6)])
            ACCD = P.sb(cd, "ACCD", [128, 2 * CH], F32)
            ACCP = P.sb(cd, "ACCP", [128, 2 * CH], F32)
            ACCS = Pool([P.sb(cd, f"ACCS{i}", [128, 2 * CH], F32) for i in range(2)])
            R1 = P.sb(cd, "R1", [128, CH], F32)
            R2 = P.sb(cd, "R2", [128, CH], F32)
            O1S = Pool([P.sb(cd, f"O1S{i}", [128, CH], F32) for i in range(2)])
            O2S = Pool([P.sb(cd, f"O2S{i}", [128, CH], F32) for i in range(2)])
            AT1 = P.sb(cd, "AT1", [128, CH], F32)
            AT2 = P.sb(cd, "AT2", [128, CH], F32)
            ATT = Pool([P.sb(cd, f"ATT{i}", [128, CH], F32) for i in range(2)])
            SQ16 = Pool([P.sb(cd, f"SQ16{i}", [128, CH], BF16) for i in range(2)])
            LNd = P.sb(cd, "LNd", [128, CH], F32)
            RSd = P.sb(cd, "RSd", [128, CH], F32)
            YA = P.sb(cd, "YA", [128, CH], F32)
            ZA = P.sb(cd, "ZA", [128, CH], F32)
            ZO = Pool([P.sb(cd, f"ZO{i}", [128, CH], BF16) for i in range(2)])
            SP = Pool([P.ps(cd, f"SP{i}", [128, 2 * CH]) for i in range(2)])
            O1 = P.ps(cd, "O1", [128, CH])
            O2 = P.ps(cd, "O2", [128, CH])
            MSB = P.ps(cd, "MSB", [128, 2 * CH])
            P.memset(ONESF[:], 1.0, w=[ONESF])
            for mi in range(16):
                P.load(MK[:, mi, :], masks[:, mi * CH:(mi + 1) * CH], w=[MK], q="gpsimd")

            def loads_d(h):
                b = h % 2
                P.load(KH[b][:], Kscr[h], w=[KH[b]])
                vsrc = Vscr[:, h * 128:(h + 1) * 128].rearrange("(kb p) e -> p kb e", p=128)
                for q4 in range(4):
                    P.load(VH[b][:, q4 * 16:(q4 + 1) * 16, :], vsrc[:, q4 * 16:(q4 + 1) * 16, :], w=[VH[b]])
                P.load(QH[b][:], Qscr[h], w=[QH[b]])
                P.load(GAH[b][:], GAscr[h * 128:(h + 1) * 128, :], w=[GAH[b]])
                P.load(ZCH[b][:], ZCscr[h * 128:(h + 1) * 128, :], w=[ZCH[b]])

            pending = []

            def run_pending(kb):
                while pending and (kb is None or pending[0][0] <= kb):
                    pending.pop(0)[1]()

            def make_finalize(h, i, Gh, Zh, qs):
                accs, o1s, o2s = ACCS.next(), O1S.next(), O2S.next()
                att, sq = ATT.next(), SQ16.next()

                def st1():
                    P.tt(accs[:], ACCD[:], ACCP[:], ADD, r=[ACCD, ACCP], w=[accs])
                    P.copy(o1s[:], O1[:], r=[O1], w=[o1s])
                    P.copy(o2s[:], O2[:], r=[O2], w=[o2s])

                def st2():
                    P.mm(MSB[:, 0:CH], ONESF[:], accs[:, 0:CH], True, True, r=[ONESF, accs], w=[MSB.s(0)])
                    P.mm(MSB[:, CH:2 * CH], ONESF[:], accs[:, CH:2 * CH], True, True, r=[ONESF, accs], w=[MSB.s(1)])

                def st3():
                    P.recip(R1[:], MSB[:, 0:CH], r=[MSB.s(0)], w=[R1])
                    P.recip(R2[:], MSB[:, CH:2 * CH], r=[MSB.s(1)], w=[R2])
                    P.tt(AT1[:], o1s[:], R1[:], MUL, r=[o1s, R1], w=[AT1])
                    P.tt(AT2[:], o2s[:], R2[:], MUL, r=[o2s, R2], w=[AT2])
                    P.stt(att[:], AT2[:], NEGLAM, AT1[:], MUL, ADD, r=[AT2, AT1, DV], w=[att])

                def st6():
                    P.act(sq[:], att[:], AF.Square, r=[att], w=[sq])

                def st7():
                    P.mm(MSB[:, 0:CH], ONESB[:], sq[:], True, True, r=[ONESB, sq], w=[MSB.s(0)])

                def st8():
                    P.act(LNd[:], MSB[:, 0:CH], AF.Ln, r=[MSB.s(0)], w=[LNd], bias=EPS, scale=1.0 / 128)
                    P.act(RSd[:], LNd[:], AF.Exp, r=[LNd], w=[RSd], scale=-0.5)

                def st9():
                    P.tt(YA[:], att[:], RSd[:], MUL, r=[att, RSd], w=[YA])
                    P.stt(ZA[:], YA[:], HNV, Gh[:, qs], MUL, MUL, r=[YA, DV, Gh], w=[ZA])
                    zo = ZO.next()
                    P.tt(zo[:], ZA[:], Zh[:, qs], ADD, r=[ZA, Zh], w=[zo])
                    P.store(Zscr[h * 128:(h + 1) * 128, qs], zo[:], r=[zo], q=STQ)

                st1()
                pending.extend([(1, st2), (2, st3), (3, st6), (4, st7), (5, st8), (6, st9)])

            loads_d(0)
            for h in range(8):
                if h + 1 < 8:
                    loads_d(h + 1)
                b = h % 2
                Kh, Vh, Qh, Gh, Zh = KH[b], VH[b], QH[b], GAH[b], ZCH[b]
                for i in range(NSL):
                    nk = 8 * (i + 1)
                    qs = slice(i * CH, (i + 1) * CH)

                    def smm(kb):
                        Sx = SP.next()
                        ks = slice(kb * 128, (kb + 1) * 128)
                        P.mm(Sx[:, 0:CH], Kh[0:64, ks], Qh[0:64, qs], True, True, r=[Kh, Qh], w=[Sx.s(0)])
                        P.mm(Sx[:, CH:2 * CH], Kh[64:128, ks], Qh[64:128, qs], True, True, r=[Kh, Qh], w=[Sx.s(1)])
                        return Sx

                    Snext = smm(0)
                    for kb in range(nk):
                        Sx = Snext
                        pt = PT.next()
                        P.act(pt[:], Sx[:], AF.Exp, r=[Sx], w=[pt], scale=0.125)
                        if kb >= nk - 8:
                            jj = kb - (nk - 8)
                            midx = (i % 2) * 8 + jj
                            P.tt(pt[:, 0:CH], pt[:, 0:CH], MK[:, midx, :], MUL, r=[pt.s(0), MK], w=[pt.s(0)])
                            P.tt(pt[:, CH:2 * CH], pt[:, CH:2 * CH], MK[:, midx, :], MUL, r=[pt.s(1), MK], w=[pt.s(1)])
                        if kb + 1 < nk:
                            Snext = smm(kb + 1)
                        st, sp_ = (kb == 0), (kb == nk - 1)
                        P.mm(O1[:], Vh[:, kb, :], pt[:, 0:CH], st, sp_, r=[Vh, pt.s(0)], w=[O1])
                        P.mm(O2[:], Vh[:, kb, :], pt[:, CH:2 * CH], st, sp_, r=[Vh, pt.s(1)], w=[O2])
                        use_dve = (kb % 9) % 2 == 0
                        acc, eng = (ACCD, "vector") if use_dve else (ACCP, "gpsimd")
                        if kb < 2:
                            P.copy(acc[:], pt[:], r=[pt], w=[acc], eng=eng)
                        else:
                            P.tt(acc[:], acc[:], pt[:], ADD, r=[acc, pt], w=[acc], eng=eng)
                        run_pending(kb)
                    make_finalize(h, i, Gh, Zh, qs)
                run_pending(None)
            P.barrier()
            P.emit()

        if "C" in phases:
          with ExitStack() as cc:
            SC = 256
            WO = P.sb(cc, "WO", [128, 8, 1024], BF16)
            WF1 = P.sb(cc, "WF1", [128, 8, 4096], BF16)
            WF2 = P.sb(cc, "WF2", [128, 32, 1024], BF16)
            load_w(WO, w_o, 1024, 0)
            load_w(WF1, w_ff1, 4096, 0)
            for q4 in range(4):
                for cg in range(2):
                    P.load(WF2[:, q4 * 8:(q4 + 1) * 8, cg * 512:(cg + 1) * 512],
                           fm(w_ff2[q4 * 1024:(q4 + 1) * 1024, cg * 512:(cg + 1) * 512]), w=[WF2], q="gpsimd")
            Z = [P.sb(cc, f"ZC{i}", [128, 8, SC], BF16) for i in range(2)]
            X = P.sb(cc, "XC", [128, 8, SC], F32)
            Y = P.sb(cc, "YC", [128, 8, SC], F32)
            H2 = P.sb(cc, "H2C", [128, 8, SC], BF16)
            F1 = P.sb(cc, "F1C", [128, 32, SC], BF16)
            SQ = P.sb(cc, "SQC", [128, 8, SC], BF16)
            SR = P.sb(cc, "SRC", [128, SC], F32)
            RS = P.sb(cc, "RSC", [128, SC], F32)
            TMPs = Pool([P.sb(cc, f"TMC{i}", [128, SC], F32) for i in range(3)])
            RL = Pool([P.sb(cc, f"RLC{i}", [128, SC], F32) for i in range(3)])
            Q16 = Pool([P.sb(cc, f"Q16C{i}", [128, SC], BF16) for i in range(2)])
            ACC = P.ps(cc, "ACCC", [128, CH])
            SSb = P.ps(cc, "SSbC", [128, CH])
            PP = Pool([P.ps(cc, f"PPC{i}", [128, CH]) for i in range(6)])
            NSC = NSL * CH // SC

            def xcols(s):
                slot, hf = divmod(s, CH // SC)
                o = slot * XW + HALO + hf * SC
                return slice(o, o + SC)

            def load_z(s):
                P.load(Z[s % 2][:], fm(Zscr[:, s * SC:(s + 1) * SC]), w=[Z[s % 2]])

            load_z(0)
            P.load(X[:], fm(xT_own[:, xcols(0)]), w=[X])
            for s in range(NSC):
                if s + 1 < NSC:
                    load_z(s + 1)
                Zs = Z[s % 2]
                for n in range(8):
                    PY = PP.next()
                    for j in range(8):
                        P.mm(PY[:, 0:SC], WO[:, j, n * 128:(n + 1) * 128], Zs[:, j, :], j == 0, j == 7, r=[WO, Zs], w=[PY])
                    P.act(Y[:, n, :], PY[:, 0:SC], AF.Identity, r=[PY], w=[Y.s(n)])
                    q16 = Q16.next()
                    P.act(q16[:], PY[:, 0:SC], AF.Square, r=[PY], w=[q16])
                    P.mm(ACC[:, 0:SC], ONESB[:], q16[:], n == 0, n == 7, r=[ONESB, q16], w=[ACC])
                P.act(SR[:], ACC[:, 0:SC], AF.Sqrt, r=[ACC], w=[SR], bias=EPS, scale=1.0 / D)
                P.recip(RS[:], SR[:], r=[SR], w=[RS])
                for n in range(8):
                    tm = TMPs.next()
                    P.stt(tm[:], Y[:, n, :], GP1(n), RS[:], MUL, MUL, r=[Y.s(n), DV, RS], w=[tm])
                    P.tt(X[:, n, :], X[:, n, :], tm[:], ADD, r=[X.s(n), tm], w=[X.s(n)])
                P.act(SQ[:], X[:], AF.Square, r=[X], w=[SQ])
                for j in range(8):
                    P.mm(SSb[:, 0:SC], ONESB[:], SQ[:, j, :], j == 0, j == 7, r=[ONESB, SQ], w=[SSb])
                P.act(SR[:], SSb[:, 0:SC], AF.Sqrt, r=[SSb], w=[SR], bias=EPS, scale=1.0 / D)
                P.recip(RS[:], SR[:], r=[SR], w=[RS])
                for j in range(8):
                    tm = TMPs.next()
                    P.stt(tm[:], X[:, j, :], A2(j), RS[:], MUL, MUL, r=[X.s(j), DV, RS], w=[tm])
                    P.act(H2[:, j, :], tm[:], AF.Identity, r=[tm, MODV], w=[H2.s(j)], bias=SH2(j), scale=1.0)
                for m in range(32):
                    PF = PP.next()
                    for j in range(8):
                        P.mm(PF[:, 0:SC], WF1[:, j, m * 128:(m + 1) * 128], H2[:, j, :], j == 0, j == 7, r=[WF1, H2.s(j)], w=[PF])
                    rl = RL.next()
                    P.ts(rl[:], PF[:, 0:SC], 0.0, None, MAX, None, r=[PF], w=[rl])
                    P.act(F1[:, m, :], rl[:], AF.Square, r=[rl], w=[F1.s(m)])
                for n in range(8):
                    PO = PP.next()
                    for m in range(32):
                        P.mm(PO[:, 0:SC], WF2[:, m, n * 128:(n + 1) * 128], F1[:, m, :], m == 0, m == 31, r=[WF2, F1.s(m)], w=[PO])
                    P.act(Y[:, n, :], PO[:, 0:SC], AF.Identity, r=[PO], w=[Y.s(n)])
                    q16 = Q16.next()
                    P.act(q16[:], PO[:, 0:SC], AF.Square, r=[PO], w=[q16])
                    P.mm(ACC[:, 0:SC], ONESB[:], q16[:], n == 0, n == 7, r=[ONESB, q16], w=[ACC])
                P.act(SR[:], ACC[:, 0:SC], AF.Sqrt, r=[ACC], w=[SR], bias=EPS, scale=1.0 / D)
                P.recip(RS[:], SR[:], r=[SR], w=[RS])
                for n in range(8):
                    tm = TMPs.next()
                    P.stt(tm[:], Y[:, n, :], GP2(n), RS[:], MUL, MUL, r=[Y.s(n), DV, RS], w=[tm])
                    P.tt(X[:, n, :], X[:, n, :], tm[:], ADD, r=[X.s(n), tm], w=[X.s(n)])
                P.store(fm(outT[:, s * SC:(s + 1) * SC]), X[:], r=[X], q=STQ)
                if s + 1 < NSC:
                    P.load(X[:], fm(xT_own[:, xcols(s + 1)]), w=[X])
            P.barrier()
            P.emit()
    return nc


_DBG = None
STQ = "sync"
import os
NCH_RUN = int(os.environ.get("NCH_RUN", "16"))
A_PARTS = os.environ.get("A_PARTS", "nkv")
DBG_OUT = os.environ.get("DBG_OUT", "1") == "1"


def _fmv(v, n):
    return np.ascontiguousarray(np.asarray(v, np.float32).reshape(n, 128).T)


def _host_constants():
    inv_freq = (np.float32(10000.0) ** (-np.arange(0, 64, 2, dtype=np.float32) / np.float32(64))).astype(np.float32)
    pos = np.arange(T, dtype=np.float32)
    ang = (pos[:, None] * inv_freq[None, :]).astype(np.float32)
    cos = np.cos(ang).astype(np.float32).T
    sin = np.sin(ang).astype(np.float32).T
    cos_all = np.ascontiguousarray(np.tile(cos, (4, 1)))
    sin_all = np.ascontiguousarray(np.tile(sin, (4, 1)))
    rm = np.zeros((128, 128), np.float32)
    for blk in (0, 64):
        for d in range(32):
            rm[blk + d + 32, blk + d] = -1.0
            rm[blk + d, blk + d + 32] = 1.0
    k = np.arange(128)[:, None]
    q = np.arange(CH)[None, :]
    diag = [((j * 128 + k) <= q).astype(np.float32) for j in range(4)]
    ones = [np.ones((128, CH), np.float32)] * 4
    zeros = [np.zeros((128, CH), np.float32)] * 4
    return cos_all, sin_all, rm, diag, ones, zeros


def kernel(x, c, w_ada, b_ada, pre_norm1, post_norm1, w_in, conv_w, conv_b, conv_ln_g, conv_ln_b,
           conv_w_out, conv_b_out, lambda_q1, lambda_k1, lambda_q2, lambda_k2, head_norm, w_o,
           pre_norm2, post_norm2, w_ff1, w_ff2):
    x = np.asarray(x, np.float32)
    c = np.asarray(c, np.float32)
    cos_all, sin_all, rm, diag, ones, zeros = _host_constants()
    w_in0 = np.asarray(w_in, np.float32)[0]
    perm = np.arange(7168)
    for base in (2048, 3072):
        for h in range(8):
            for m in range(2):
                perm[base + h * 128 + m * 64: base + h * 128 + m * 64 + 64] = base + m * 512 + h * 64 + np.arange(64)
    w_in_p = np.ascontiguousarray(w_in0[:, perm])
    w_ada0 = np.ascontiguousarray(np.asarray(w_ada, np.float32)[0])
    w_co0 = np.ascontiguousarray(np.asarray(conv_w_out, np.float32)[0])
    w_o0 = np.ascontiguousarray(np.asarray(w_o, np.float32)[0])
    w_ff10 = np.ascontiguousarray(np.asarray(w_ff1, np.float32)[0])
    w_ff20 = np.ascontiguousarray(np.asarray(w_ff2, np.float32)[0])

    in_maps = []
    for core in range(8):
        b, jh = divmod(core, 2)
        own = OWN[jh]
        xT = np.ascontiguousarray(x[b].T)
        xo = np.zeros((D, NSL, XW), np.float32)
        flag = np.ones((128, NSL), np.float32)
        for i, ch in enumerate(own):
            lo = ch * CH - HALO
            if lo < 0:
                xo[:, i, HALO:] = xT[:, 0:CH]
                flag[:, i] = 0.0
            else:
                xo[:, i, :] = xT[:, lo:lo + XW]
        small = np.zeros((128, NS), np.float32)
        small[:, SM_BADA:SM_BADA + 48] = _fmv(b_ada[0], 48)
        small[:, SM_PN1:SM_PN1 + 8] = _fmv(pre_norm1[0], 8)
        small[:, SM_PO1:SM_PO1 + 8] = _fmv(post_norm1[0], 8)
        small[:, SM_PN2:SM_PN2 + 8] = _fmv(pre_norm2[0], 8)
        small[:, SM_PO2:SM_PO2 + 8] = _fmv(post_norm2[0], 8)
        small[:, SM_CB:SM_CB + 8] = _fmv(conv_b[0], 8)
        small[:, SM_LNG:SM_LNG + 8] = _fmv(conv_ln_g[0], 8)
        small[:, SM_LNB:SM_LNB + 8] = _fmv(conv_ln_b[0], 8)
        small[:, SM_BOUT:SM_BOUT + 8] = _fmv(conv_b_out[0], 8)
        small[:, SM_HN] = np.asarray(head_norm, np.float32)[0]
        small[:, SM_FLAG:SM_FLAG + 8] = flag
        small[:, SM_CT:SM_CT + 16] = np.repeat(_fmv(c[b], 8), 2, axis=1)
        for k, v in enumerate((lambda_q1, lambda_k1, lambda_q2, lambda_k2)):
            small[:, SM_LAM + 64 * k:SM_LAM + 64 * (k + 1)] = np.asarray(v, np.float32)[0][None, :]
        cw = np.asarray(conv_w, np.float32)[0]
        small[:, SM_CW:SM_CW + 248] = cw.T.reshape(8, 128, 31).transpose(1, 0, 2).reshape(128, 248)
        idx = np.concatenate([np.arange(ch * CH, (ch + 1) * CH) for ch in own])
        mk = []
        for par in range(2):
            own_is_2i = (par == 0) if jh == 0 else (par == 1)
            mk += (diag + zeros) if own_is_2i else (ones + diag)
        in_maps.append({
            "xT_all": xT, "xT_own": xo.reshape(D, NSL * XW), "w_ada": w_ada0, "w_in": w_in_p, "w_co": w_co0,
            "w_o": w_o0, "w_ff1": w_ff10, "w_ff2": w_ff20, "small": small,
            "cos_all": cos_all, "sin_all": sin_all,
            "cos_own": np.ascontiguousarray(cos_all[:, idx]), "sin_own": np.ascontiguousarray(sin_all[:, idx]),
            "masks": np.ascontiguousarray(np.concatenate(mk, axis=1)), "rmat": rm,
        })
    if _DBG is not None:
        nc = build_program(_DBG, dbg=DBG_OUT)
        res = run_bass_kernel_spmd(nc, in_maps, core_ids=list(range(8)))
        return res, in_maps
    nc = build_program()
    res = run_bass_kernel_spmd(nc, in_maps, core_ids=list(range(8)))
    out = np.empty((NB, T, D), np.float32)
    for core in range(8):
        b, jh = divmod(core, 2)
        oT = np.asarray(res.results[core]["outT"], np.float32)
        for i, ch in enumerate(OWN[jh]):
            out[b, ch * CH:(ch + 1) * CH, :] = oT[:, i * CH:(i + 1) * CH].T
    return out
```

```python
import numpy as np
import concourse.bass as bass
import concourse.mybir as mybir

F32 = mybir.dt.float32
BF16 = mybir.dt.bfloat16
AF = mybir.ActivationFunctionType
ALU = mybir.AluOpType

ENGS = ("tensor", "vector", "scalar", "gpsimd", "sync")
N_DMA_SEMS = 32
SAME_ENGINE_SYNC = True


class Res:
    __slots__ = ("t", "last_w", "readers", "name", "parent", "kids")

    def __init__(self, t, name, parent=None):
        self.t = t
        self.last_w = None
        self.readers = []
        self.name = name
        self.parent = parent
        self.kids = {}

    def s(self, k):
        if k not in self.kids:
            self.kids[k] = Res(self.t, f"{self.name}.{k}", parent=self)
        return self.kids[k]

    def holders(self):
        h = [self]
        if self.parent is not None:
            h.append(self.parent)
        h.extend(self.kids.values())
        return h

    def __getitem__(self, idx):
        return self.t[idx]


class Op:
    __slots__ = ("eng", "fn", "deps", "signal", "count", "is_dma", "sem", "semval", "barrier", "emitted")

    def __init__(self, eng, fn, is_dma=False):
        self.eng = eng
        self.fn = fn
        self.deps = []
        self.signal = False
        self.count = None
        self.is_dma = is_dma
        self.sem = None
        self.semval = None
        self.barrier = False
        self.emitted = False


class Prog:
    def __init__(self, nc):
        self.nc = nc
        self.ops = {e: [] for e in ENGS}
        self.esem = {e: nc.alloc_semaphore(name=f"s_{e}") for e in ENGS}
        self.ecount = {e: 0 for e in ENGS}
        self.waited = {e: {} for e in ENGS}
        self.nd = 2 * N_DMA_SEMS
        self.dsems = [nc.alloc_semaphore(name=f"d_{i}") for i in range(self.nd)]
        self.dsem_uses = [0] * self.nd
        self.dsem_last = [None] * self.nd
        self.dma_rr = {"sync": 0, "gpsimd": 0}
        self.last_op = {e: None for e in ENGS}
        self.n_ops = 0

    def sb(self, ctx, name, shape, dtype):
        t = ctx.enter_context(self.nc.sbuf_tensor(name, list(shape), dtype))
        return Res(t, name)

    def ps(self, ctx, name, shape, dtype=F32):
        t = ctx.enter_context(self.nc.psum_tensor(name, list(shape), dtype))
        return Res(t, name)

    def _deps(self, op, r, w):
        deps = []
        for x0 in r:
            for x in x0.holders():
                if x.last_w is not None:
                    deps.append(x.last_w)
        for x0 in w:
            for x in x0.holders():
                if x.last_w is not None:
                    deps.append(x.last_w)
                deps.extend(x.readers)
        seen = set()
        for d in deps:
            if id(d) in seen or d is op or d.emitted:
                continue
            seen.add(id(d))
            if (not d.is_dma) and (not op.is_dma) and d.eng == op.eng and (d.eng == "tensor" or not SAME_ENGINE_SYNC):
                continue
            op.deps.append(d)
            if not d.is_dma:
                d.signal = True
        for x in r:
            if op.is_dma:
                x.readers.append(op)
            else:
                x.readers = [o for o in x.readers if o.is_dma or o.eng != op.eng]
                x.readers.append(op)
        for x in w:
            x.last_w = op
            x.readers = []

    def add(self, eng, fn, r=(), w=()):
        op = Op(eng, fn)
        self._deps(op, r, w)
        self.ops[eng].append(op)
        self.last_op[eng] = op
        self.n_ops += 1
        return op

    def dma(self, fn, r=(), w=(), q="sync"):
        op = Op(q, fn, is_dma=True)
        k = self.dma_rr[q] % N_DMA_SEMS + (N_DMA_SEMS if q == "gpsimd" else 0)
        self.dma_rr[q] += 1
        prev = self.dsem_last[k]
        self.dsem_uses[k] += 1
        op.sem = k
        op.semval = 16 * self.dsem_uses[k]
        if prev is not None and not prev.emitted:
            op.deps.append(prev)
        self._deps(op, r, w)
        self.dsem_last[k] = op
        self.ops[q].append(op)
        self.n_ops += 1
        return op

    def barrier(self):
        targets = []
        for e in ENGS:
            lo = self.last_op[e]
            if lo is not None:
                lo.signal = True
                targets.append(lo)
        for k in range(self.nd):
            if self.dsem_last[k] is not None:
                targets.append(self.dsem_last[k])
        for e in ENGS:
            op = Op(e, None)
            op.barrier = True
            op.deps = [t for t in targets if t.is_dma or t.eng != e]
            self.ops[e].append(op)

    def emit(self, name=None):
        nc = self.nc
        for e in ENGS:
            c = self.ecount[e]
            for op in self.ops[e]:
                if op.is_dma or op.barrier:
                    continue
                if op.signal:
                    c += 1
                    op.count = c
            self.ecount[e] = c

        def emit_engine(e, eng):
            waited = self.waited[e]
            for op in self.ops[e]:
                for d in op.deps:
                    if d.is_dma:
                        key, val, sem = ("d", d.sem), d.semval, self.dsems[d.sem]
                    else:
                        assert d.count is not None, "dependency on non-signalling op"
                        key, val, sem = ("e", d.eng), d.count, self.esem[d.eng]
                    if waited.get(key, 0) >= val:
                        continue
                    eng.wait_ge(sem, val)
                    waited[key] = val
                if op.barrier:
                    continue
                ins = op.fn(eng)
                if op.is_dma:
                    ins.then_inc(self.dsems[op.sem], 16)
                elif op.signal:
                    ins.then_inc(self.esem[e], 1)

        with nc.Block() as block:
            @block.tensor
            def _(eng):
                emit_engine("tensor", eng)

            @block.vector
            def _(eng):
                emit_engine("vector", eng)

            @block.scalar
            def _(eng):
                emit_engine("scalar", eng)

            @block.gpsimd
            def _(eng):
                emit_engine("gpsimd", eng)

            @block.sync
            def _(eng):
                emit_engine("sync", eng)
        for e in ENGS:
            for op in self.ops[e]:
                op.emitted = True
        self.ops = {e: [] for e in ENGS}

    def mm(self, out, lhsT, rhs, start, stop, r, w, **kw):
        return self.add("tensor", lambda e: e.matmul(out, lhsT, rhs, start=start, stop=stop, **kw), r=r, w=w)

    def act(self, out, in_, func, r, w, bias=None, scale=None, eng="scalar"):
        kw = {}
        if bias is not None:
            kw["bias"] = bias
        if scale is not None:
            kw["scale"] = scale
        return self.add(eng, lambda e: e.activation(out, in_, func, **kw), r=r, w=w)

    def tt(self, out, in0, in1, op, r, w, eng="vector"):
        return self.add(eng, lambda e: e.tensor_tensor(out, in0, in1, op), r=r, w=w)

    def ts(self, out, in0, s1, s2, op0, op1, r, w, eng="vector"):
        if op1 is None:
            return self.add(eng, lambda e: e.tensor_scalar(out, in0, s1, None, op0), r=r, w=w)
        return self.add(eng, lambda e: e.tensor_scalar(out, in0, s1, s2, op0, op1), r=r, w=w)

    def stt(self, out, in0, scalar, in1, op0, op1, r, w):
        return self.add("vector", lambda e: e.scalar_tensor_tensor(out, in0, scalar, in1, op0, op1), r=r, w=w)

    def copy(self, out, in_, r, w, eng="vector"):
        return self.add(eng, lambda e: e.tensor_copy(out, in_), r=r, w=w)

    def recip(self, out, in_, r, w):
        return self.add("vector", lambda e: e.reciprocal(out, in_), r=r, w=w)

    def memset(self, ap, val, w, eng="vector"):
        return self.add(eng, lambda e: e.memset(ap, val), r=(), w=w)

    def load(self, out, in_, w, r=(), q="sync", **kw):
        return self.dma(lambda e: e.dma_start(out=out, in_=in_, **kw), r=r, w=w, q=q)

    def store(self, out, in_, r, w=(), q="sync", **kw):
        return self.dma(lambda e: e.dma_start(out=out, in_=in_, **kw), r=r, w=w, q=q)


from contextlib import ExitStack
from concourse.bass_utils import run_bass_kernel_spmd

D = 1024
T = 8192
NB = 4
NCH = 16
NSL = 8
CH = 512
HALO = 32
XW = CH + HALO
EPS = 1e-6
LAM_INIT = 0.2
OWN = {0: [0, 3, 4, 7, 8, 11, 12, 15], 1: [1, 2, 5, 6, 9, 10, 13, 14]}

SM_BADA = 0
SM_PN1 = 48
SM_PO1 = 56
SM_PN2 = 64
SM_PO2 = 72
SM_CB = 80
SM_LNG = 88
SM_LNB = 96
SM_BOUT = 104
SM_HN = 112
SM_FLAG = 113
SM_CT = 121
SM_LAM = 137
SM_CW = 393
NS = 393 + 248


class Pool:
    def __init__(self, items):
        self.items = items
        self.i = 0

    def next(self):
        x = self.items[self.i % len(self.items)]
        self.i += 1
        return x


def build_program(phases="ABDC", dbg=False):
    nc = bass.Bass("TRN2", target_bir_lowering=False)

    def din(name, shape, dt=F32):
        return nc.dram_tensor(name, list(shape), dt, kind="ExternalInput").ap()

    xT_all = din("xT_all", [D, T])
    xT_own = din("xT_own", [D, NSL * XW])
    w_ada = din("w_ada", [D, 6 * D])
    w_in = din("w_in", [D, 7 * D])
    w_co = din("w_co", [D, D])
    w_o = din("w_o", [D, D])
    w_ff1 = din("w_ff1", [D, 4 * D])
    w_ff2 = din("w_ff2", [4 * D, D])
    small = din("small", [128, NS])
    cos_all = din("cos_all", [128, T])
    sin_all = din("sin_all", [128, T])
    cos_own = din("cos_own", [128, NSL * CH])
    sin_own = din("sin_own", [128, NSL * CH])
    masks = din("masks", [128, 16 * CH])
    rmat = din("rmat", [128, 128])
    outT = nc.dram_tensor("outT", [D, NSL * CH], F32, kind="ExternalOutput").ap()

    def dscr(name, shape, dt=BF16):
        return nc.dram_tensor(name, list(shape), dt, kind="ExternalOutput" if dbg else "Internal").ap()

    dbg_small = dscr("dbg_small", [128, 88], F32) if dbg else None

    Kscr = dscr("Kscr", [8, 128, T])
    Vscr = dscr("Vscr", [T, D])
    Qscr = dscr("Qscr", [8, 128, NSL * CH])
    ZCscr = dscr("ZCscr", [D, NSL * CH])
    GAscr = dscr("GAscr", [D, NSL * CH])
    Zscr = dscr("Zscr", [D, NSL * CH])

    P = Prog(nc)
    MUL, ADD, SUB, MAX = ALU.mult, ALU.add, ALU.subtract, ALU.max

    def fm(ap):
        return ap.rearrange("(j p) t -> p j t", p=128)

    with ExitStack() as g:
        SM = P.sb(g, "SM", [128, NS], F32)
        MODV = P.sb(g, "MODV", [128, 48], F32)
        DV = P.sb(g, "DV", [128, 40], F32)
        ONESB = P.sb(g, "ONESB", [128, 128], BF16)
        RM = P.sb(g, "RM", [128, 128], BF16)

        def smc(off, n=1):
            return SM[:, off:off + n]

        SH1 = lambda j: MODV[:, 0 + j:1 + j]
        SH2 = lambda j: MODV[:, 24 + j:25 + j]
        A1 = lambda j: DV[:, 0 + j:1 + j]
        A2 = lambda j: DV[:, 8 + j:9 + j]
        GP1 = lambda j: DV[:, 16 + j:17 + j]
        GP2 = lambda j: DV[:, 24 + j:25 + j]
        NEGLAM = DV[:, 32:33]
        HNV = DV[:, 33:34]

        with ExitStack() as c0:
            WA = [P.sb(c0, f"WA{i}", [128, 8, 768], F32) for i in range(2)]
            CS = P.sb(c0, "CS", [128, 16], F32)
            TMPL = P.sb(c0, "TMPL", [128, 64], F32)
            SS = P.sb(c0, "SSl", [128, 4], F32)
            MODP = P.ps(c0, "MODP", [128, 96], F32)
            P.load(SM[:], small, w=[SM])
            P.load(RM[:], rmat, w=[RM], q="gpsimd")
            P.memset(ONESB[:], 1.0, w=[ONESB])
            P.act(CS[:], smc(SM_CT, 16), AF.Silu, r=[SM], w=[CS])
            for gi in range(8):
                Wt = WA[gi % 2]
                P.load(Wt[:], fm(w_ada[:, gi * 768:(gi + 1) * 768]), w=[Wt])
                for nn in range(6):
                    n = gi * 6 + nn
                    for j in range(8):
                        P.mm(MODP[:, 2 * n:2 * n + 2], Wt[:, j, nn * 128:(nn + 1) * 128], CS[:, 2 * j:2 * j + 2],
                             j == 0, j == 7, r=[Wt, CS], w=[MODP])
            modp_v = MODP[:].rearrange("p (n two) -> p n two", two=2)[:, :, 0]
            P.tt(MODV[:], modp_v, smc(SM_BADA, 48), ADD, r=[MODP, SM], w=[MODV])
            P.stt(DV[:, 0:8], MODV[:, 8:16], 1.0, smc(SM_PN1, 8), ADD, MUL, r=[MODV, SM], w=[DV])
            P.stt(DV[:, 8:16], MODV[:, 32:40], 1.0, smc(SM_PN2, 8), ADD, MUL, r=[MODV, SM], w=[DV])
            P.tt(DV[:, 16:24], MODV[:, 16:24], smc(SM_PO1, 8), MUL, r=[MODV, SM], w=[DV])
            P.tt(DV[:, 24:32], MODV[:, 40:48], smc(SM_PO2, 8), MUL, r=[MODV, SM], w=[DV])
            for k in range(2):
                P.tt(TMPL[:], smc(SM_LAM + 128 * k, 64), smc(SM_LAM + 128 * k + 64, 64), MUL, r=[SM], w=[TMPL])
                P.add("vector", lambda e, k=k: e.reduce_sum(SS[:, k:k + 1], TMPL[:], mybir.AxisListType.X),
                      r=[TMPL], w=[SS])
            P.act(SS[:, 2:4], SS[:, 0:2], AF.Exp, r=[SS], w=[SS])
            P.stt(DV[:, 32:33], SS[:, 3:4], -LAM_INIT, SS[:, 2:3], ADD, SUB, r=[SS], w=[DV])
            P.ts(DV[:, 33:34], smc(SM_HN, 1), 1.0 - LAM_INIT, None, MUL, None, r=[SM], w=[DV])
            if dbg:
                P.store(dbg_small[:, 0:48], MODV[:], r=[MODV])
                P.store(dbg_small[:, 48:82], DV[:, 0:34], r=[DV])
            P.barrier()
            P.emit()

        def norm_H(X, H, t0, t1, avec, shvec, SQ, SSb, SR, RS, TMPs):
            n = t1 - t0
            P.act(SQ[:, :, t0:t1], X[:, :, t0:t1], AF.Square, r=[X], w=[SQ])
            for j in range(8):
                P.mm(SSb[:, 0:n], ONESB[:], SQ[:, j, t0:t1], j == 0, j == 7, r=[ONESB, SQ], w=[SSb])
            P.act(SR[:, 0:n], SSb[:, 0:n], AF.Sqrt, r=[SSb], w=[SR], bias=EPS, scale=1.0 / D)
            P.recip(RS[:, 0:n], SR[:, 0:n], r=[SR], w=[RS])
            for j in range(8):
                Tm = TMPs.next()
                P.stt(Tm[:, 0:n], X[:, j, t0:t1], avec(j), RS[:, 0:n], MUL, MUL, r=[X, DV, RS], w=[Tm])
                P.act(H[:, j, t0:t1], Tm[:, 0:n], AF.Identity, r=[Tm, MODV], w=[H.s((j, t0))], bias=shvec(j), scale=1.0)

        def load_w(Wt, src, ncols, c0col=0):
            for cc in range(0, ncols, 512):
                P.load(Wt[:, :, cc:cc + 512], fm(src[:, c0col + cc:c0col + cc + 512]), w=[Wt], q="gpsimd")

        if "A" in phases:
          with ExitStack() as ca:
            WK = P.sb(ca, "WK", [128, 8, 1024], BF16)
            WV = P.sb(ca, "WV", [128, 8, 1024], BF16)
            load_w(WK, w_in, 1024, 3072)
            load_w(WV, w_in, 1024, 4096)
            X = [P.sb(ca, f"XA{i}", [128, 8, CH], F32) for i in range(2)]
            H = [P.sb(ca, f"HA{i}", [128, 8, CH], BF16) for i in range(2)]
            SQ = P.sb(ca, "SQA", [128, 8, CH], BF16)
            SR = P.sb(ca, "SRA", [128, CH], F32)
            RS = P.sb(ca, "RSA", [128, CH], F32)
            TMPs = Pool([P.sb(ca, f"TMA{i}", [128, CH], F32) for i in range(3)])
            COS = [P.sb(ca, f"COSA{i}", [128, CH], F32) for i in range(2)]
            SIN = [P.sb(ca, f"SINA{i}", [128, CH], F32) for i in range(2)]
            KRAW = Pool([P.sb(ca, f"KRAW{i}", [128, CH], BF16) for i in range(2)])
            T1 = Pool([P.sb(ca, f"T1A{i}", [128, CH], F32) for i in range(2)])
            T2 = Pool([P.sb(ca, f"T2A{i}", [128, CH], F32) for i in range(2)])
            KRO = Pool([P.sb(ca, f"KRO{i}", [128, CH], BF16) for i in range(3)])
            VSB = Pool([P.sb(ca, f"VSB{i}", [128, 1024], BF16) for i in range(2)])
            SSb = P.ps(ca, "SSbA", [128, CH])
            PP = Pool([P.ps(ca, f"PPA{i}", [128, CH]) for i in range(6)])

            def loads_a(c):
                P.load(X[c % 2][:], fm(xT_all[:, c * CH:(c + 1) * CH]), w=[X[c % 2]])
                P.load(COS[c % 2][:], cos_all[:, c * CH:(c + 1) * CH], w=[COS[c % 2]])
                P.load(SIN[c % 2][:], sin_all[:, c * CH:(c + 1) * CH], w=[SIN[c % 2]])

            loads_a(0)
            for c in range(NCH_RUN):
                if c + 1 < NCH_RUN:
                    loads_a(c + 1)
                Xc, Hc, Cc, Sc = X[c % 2], H[c % 2], COS[c % 2], SIN[c % 2]
                if "n" in A_PARTS:
                    norm_H(Xc, Hc, 0, CH, A1, SH1, SQ, SSb, SR, RS, TMPs)
                for n in range(8 if "k" in A_PARTS else 0):
                    KP = PP.next()
                    for j in range(8):
                        P.mm(KP[:], WK[:, j, n * 128:(n + 1) * 128], Hc[:, j, :], j == 0, j == 7, r=[WK, Hc.s((j, 0))], w=[KP])
                    Kr = KRAW.next()
                    P.act(Kr[:], KP[:], AF.Identity, r=[KP], w=[Kr])
                    KR = PP.next()
                    P.mm(KR[:], RM[:], Kr[:], True, True, r=[RM, Kr], w=[KR])
                    t1, t2, ko = T1.next(), T2.next(), KRO.next()
                    P.tt(t1[:], Kr[:], Cc[:], MUL, r=[Kr, Cc], w=[t1])
                    P.tt(t2[:], KR[:], Sc[:], MUL, r=[KR, Sc], w=[t2])
                    P.tt(ko[:], t1[:], t2[:], ADD, r=[t1, t2], w=[ko])
                    P.store(Kscr[n, :, c * CH:(c + 1) * CH], ko[:], r=[ko], q=STQ)
                for tb in range(4 if "v" in A_PARTS else 0):
                    Vs = VSB.next()
                    for cg in range(2):
                        VP = PP.next()
                        for j in range(8):
                            P.mm(VP[:], Hc[:, j, tb * 128:(tb + 1) * 128], WV[:, j, cg * 512:(cg + 1) * 512],
                                 j == 0, j == 7, r=[Hc.s((j, 0)), WV], w=[VP])
                        if cg == 0:
                            P.act(Vs[:, 0:512], VP[:], AF.Identity, r=[VP], w=[Vs.s(0)])
                        else:
                            P.copy(Vs[:, 512:1024], VP[:], r=[VP], w=[Vs.s(1)])
                    r0 = (c * 4 + tb) * 128
                    P.store(Vscr[r0:r0 + 128, :], Vs[:], r=[Vs], q=STQ)
            P.barrier()
            P.emit()

        if "B" in phases:
          with ExitStack() as cb:
            WUA = P.sb(cb, "WUA", [128, 8, 1024], BF16)
            WUB = P.sb(cb, "WUB", [128, 8, 1024], BF16)
            WQ = P.sb(cb, "WQ", [128, 8, 1024], BF16)
            WG = P.sb(cb, "WG", [128, 8, 2048], BF16)
            WCO = P.sb(cb, "WCO", [128, 8, 1024], BF16)
            load_w(WUA, w_in, 1024, 0)
            load_w(WUB, w_in, 1024, 1024)
            load_w(WQ, w_in, 1024, 2048)
            load_w(WG, w_in, 2048, 5120)
            load_w(WCO, w_co, 1024, 0)
            X = P.sb(cb, "XB", [128, 8, XW], F32)
            H = P.sb(cb, "HB", [128, 8, XW], BF16)
            U = P.sb(cb, "UB", [128, 8, XW], F32)
            CV = P.sb(cb, "CVB", [128, 8, CH], F32)
            S = P.sb(cb, "SB", [128, 8, XW], BF16)
            SQ = S
            SR = P.sb(cb, "SRB", [128, CH], F32)
            RS = P.sb(cb, "RSB", [128, CH], F32)
            MEAN = P.sb(cb, "MEANB", [128, CH], F32)
            MSQ = SR
            VAR = RS
            NBt = P.sb(cb, "NBB", [128, CH], F32)
            TMPs = Pool([P.sb(cb, f"TMB{i}", [128, CH], F32) for i in range(4)])
            TMP2 = TMPs
            SG = Pool([P.sb(cb, f"SGB{i}", [128, CH], F32) for i in range(2)])
            CB16 = Pool([P.sb(cb, f"CB16{i}", [128, CH], BF16) for i in range(2)])
            CQ16 = Pool([P.sb(cb, f"CQ16{i}", [128, CH], BF16) for i in range(2)])
            OB16 = Pool([P.sb(cb, f"OB16{i}", [128, CH], BF16) for i in range(3)])
            COS = [P.sb(cb, f"COSB{i}", [128, CH], F32) for i in range(2)]
            SIN = [P.sb(cb, f"SINB{i}", [128, CH], F32) for i in range(2)]
            SSb = P.ps(cb, "SSbB", [128, CH])
            ACC2 = P.ps(cb, "ACC2B", [128, CH])
            PP = Pool([P.ps(cb, f"PPB{i}", [128, CH]) for i in range(6)])
            CW = lambda n, j: SM[:, SM_CW + n * 31 + j:SM_CW + n * 31 + j + 1]

            def loads_b_tabs(i):
                P.load(COS[i % 2][:], cos_own[:, i * CH:(i + 1) * CH], w=[COS[i % 2]])
                P.load(SIN[i % 2][:], sin_own[:, i * CH:(i + 1) * CH], w=[SIN[i % 2]])

            P.load(X[:], fm(xT_own[:, 0:XW]), w=[X])
            loads_b_tabs(0)
            for i in range(NSL):
                if i + 1 < NSL:
                    loads_b_tabs(i + 1)
                Cc, Sc = COS[i % 2], SIN[i % 2]
                norm_H(X, H, HALO, XW, A1, SH1, SQ, SSb, SR, RS, TMPs)
                norm_H(X, H, 0, HALO, A1, SH1, SQ, SSb, SR, RS, TMPs)
                if i + 1 < NSL:
                    P.load(X[:], fm(xT_own[:, (i + 1) * XW:(i + 2) * XW]), w=[X])
                for n in range(8):
                    for (t0, t1) in ((HALO, XW), (0, HALO)):
                        nn = t1 - t0
                        PA, PB = PP.next(), PP.next()
                        for j in range(8):
                            P.mm(PA[:, 0:nn], WUA[:, j, n * 128:(n + 1) * 128], H[:, j, t0:t1], j == 0, j == 7,
                                 r=[WUA, H.s((j, t0))], w=[PA])
                        for j in range(8):
                            P.mm(PB[:, 0:nn], WUB[:, j, n * 128:(n + 1) * 128], H[:, j, t0:t1], j == 0, j == 7,
                                 r=[WUB, H.s((j, t0))], w=[PB])
                        sg = SG.next()
                        P.act(sg[:, 0:nn], PB[:, 0:nn], AF.Sigmoid, r=[PB], w=[sg])
                        if t0 == 0:
                            P.stt(U[:, n, t0:t1], PA[:, 0:nn], SM[:, SM_FLAG + i:SM_FLAG + i + 1], sg[:, 0:nn], MUL, MUL,
                                  r=[PA, sg, SM], w=[U.s((n, t0))])
                        else:
                            P.tt(U[:, n, t0:t1], PA[:, 0:nn], sg[:, 0:nn], MUL, r=[PA, sg], w=[U.s((n, t0))])
                for n in range(8):
                    P.ts(CV[:, n, :], U[:, n, 2:2 + CH], CW(n, 0), SM[:, SM_CB + n:SM_CB + n + 1], MUL, ADD,
                         r=[U.s((n, 0)), U.s((n, HALO)), SM], w=[CV.s(n)])
                for j in range(1, 31):
                    for n in range(8):
                        P.stt(CV[:, n, :], U[:, n, 2 + j:2 + j + CH], CW(n, j), CV[:, n, :], MUL, ADD,
                              r=[U.s((n, 0)), U.s((n, HALO)), SM, CV.s(n)], w=[CV.s(n)])
                for n in range(8):
                    cb16, cq16 = CB16.next(), CQ16.next()
                    P.act(cb16[:], CV[:, n, :], AF.Identity, r=[CV.s(n)], w=[cb16])
                    P.act(cq16[:], CV[:, n, :], AF.Square, r=[CV.s(n)], w=[cq16])
                    P.mm(SSb[:], ONESB[:], cb16[:], n == 0, n == 7, r=[ONESB, cb16], w=[SSb])
                    P.mm(ACC2[:], ONESB[:], cq16[:], n == 0, n == 7, r=[ONESB, cq16], w=[ACC2])
                P.ts(MEAN[:], SSb[:], 1.0 / D, None, MUL, None, r=[SSb], w=[MEAN])
                P.tt(MSQ[:], MEAN[:], MEAN[:], MUL, r=[MEAN], w=[MSQ])
                P.stt(VAR[:], ACC2[:], 1.0 / D, MSQ[:], MUL, SUB, r=[ACC2, MSQ], w=[VAR])
                P.act(SR[:], VAR[:], AF.Sqrt, r=[VAR], w=[SR], bias=EPS, scale=1.0)
                P.recip(RS[:], SR[:], r=[SR], w=[RS])
                P.stt(NBt[:], MEAN[:], -1.0, RS[:], MUL, MUL, r=[MEAN, RS], w=[NBt])
                for n in range(8):
                    ta, tb_ = TMPs.next(), TMP2.next()
                    P.tt(ta[:], CV[:, n, :], RS[:], MUL, r=[CV.s(n), RS], w=[ta])
                    P.tt(tb_[:], ta[:], NBt[:], ADD, r=[ta, NBt], w=[tb_])
                    P.act(S[:, n, 0:CH], tb_[:], AF.Silu, r=[tb_, SM], w=[S.s(n)],
                          bias=SM[:, SM_LNB + n:SM_LNB + n + 1], scale=SM[:, SM_LNG + n:SM_LNG + n + 1])
                for n in range(8):
                    PY, PG = PP.next(), PP.next()
                    for j in range(8):
                        P.mm(PY[:], WCO[:, j, n * 128:(n + 1) * 128], S[:, j, 0:CH], j == 0, j == 7, r=[WCO, S.s(j)], w=[PY])
                    for j in range(8):
                        P.mm(PG[:], WG[:, j, n * 128:(n + 1) * 128], H[:, j, HALO:XW], j == 0, j == 7, r=[WG, H.s((j, HALO))], w=[PG])
                    sg = SG.next()
                    P.act(sg[:], PG[:], AF.Sigmoid, r=[PG], w=[sg])
                    ob = OB16.next()
                    P.stt(ob[:], PY[:], SM[:, SM_BOUT + n:SM_BOUT + n + 1], sg[:], ADD, MUL, r=[PY, sg, SM], w=[ob])
                    P.store(ZCscr[n * 128:(n + 1) * 128, i * CH:(i + 1) * CH], ob[:], r=[ob], q=STQ)
                for n in range(8):
                    PG = PP.next()
                    for j in range(8):
                        P.mm(PG[:], WG[:, j, 1024 + n * 128:1024 + (n + 1) * 128], H[:, j, HALO:XW], j == 0, j == 7,
                             r=[WG, H.s((j, HALO))], w=[PG])
                    ob = OB16.next()
                    P.act(ob[:], PG[:], AF.Sigmoid, r=[PG], w=[ob])
                    P.store(GAscr[n * 128:(n + 1) * 128, i * CH:(i + 1) * CH], ob[:], r=[ob], q=STQ)
                for n in range(8):
                    PQ = PP.next()
                    for j in range(8):
                        P.mm(PQ[:], WQ[:, j, n * 128:(n + 1) * 128], H[:, j, HALO:XW], j == 0, j == 7, r=[WQ, H.s((j, HALO))], w=[PQ])
                    qr = CB16.next()
                    P.act(qr[:], PQ[:], AF.Identity, r=[PQ], w=[qr])
                    PR = PP.next()
                    P.mm(PR[:], RM[:], qr[:], True, True, r=[RM, qr], w=[PR])
                    t1, t2, ob = TMPs.next(), TMP2.next(), OB16.next()
                    P.tt(t1[:], qr[:], Cc[:], MUL, r=[qr, Cc], w=[t1])
                    P.tt(t2[:], PR[:], Sc[:], MUL, r=[PR, Sc], w=[t2])
                    P.tt(ob[:], t1[:], t2[:], ADD, r=[t1, t2], w=[ob])
                    P.store(Qscr[n, :, i * CH:(i + 1) * CH], ob[:], r=[ob], q=STQ)
            P.barrier()
            P.emit()

        if "D" in phases:
          with ExitStack() as cd:
            KH = [P.sb(cd, f"KH{i}", [128, T], BF16) for i in range(2)]
            VH = [P.sb(cd, f"VH{i}", [128, 64, 128], BF16) for i in range(2)]
            QH = [P.sb(cd, f"QH{i}", [128, NSL * CH], BF16) for i in range(2)]
            GAH = [P.sb(cd, f"GAH{i}", [128, NSL * CH], BF16) for i in range(2)]
            ZCH = [P.sb(cd, f"ZCH{i}", [128, NSL * CH], BF16) for i in range(2)]
            MK = P.sb(cd, "MK", [128, 16, CH], BF16)
            ONESF = P.sb(cd, "ONESF", [128, 128], F32)
            PT = Pool([P.sb(cd, f"PT{i}", [128, 2 * CH], BF16) for i in range(6)])
            ACC = [P.sb(cd, f"ACCD{i}", [128, 2 * CH], F32) for i in range(2)]
            R1 = P.sb(cd, "R1", [128, CH], F32)
            R2 = P.sb(cd, "R2", [128, CH], F32)
            O1S = Pool([P.sb(cd, f"O1S{i}", [128, CH], F32) for i in range(2)])
            O2S = Pool([P.sb(cd, f"O2S{i}", [128, CH], F32) for i in range(2)])
            AT1 = P.sb(cd, "AT1", [128, CH], F32)
            AT2 = P.sb(cd, "AT2", [128, CH], F32)
            ATT = Pool([P.sb(cd, f"ATT{i}", [128, CH], F32) for i in range(2)])
            SQ16 = Pool([P.sb(cd, f"SQ16{i}", [128, CH], BF16) for i in range(2)])
            LNd = P.sb(cd, "LNd", [128, CH], F32)
            RSd = P.sb(cd, "RSd", [128, CH], F32)
            YA = P.sb(cd, "YA", [128, CH], F32)
            ZA = P.sb(cd, "ZA", [128, CH], F32)
            ZO = Pool([P.sb(cd, f"ZO{i}", [128, CH], BF16) for i in range(2)])
            SP = Pool([P.ps(cd, f"SP{i}", [128, 2 * CH]) for i in range(2)])
            O1 = P.ps(cd, "O1", [128, CH])
            O2 = P.ps(cd, "O2", [128, CH])
            L1 = P.ps(cd, "L1", [128, CH])
            L2 = P.ps(cd, "L2", [128, CH])
            P.memset(ONESF[:], 1.0, w=[ONESF])
            for mi in range(16):
                P.load(MK[:, mi, :], masks[:, mi * CH:(mi + 1) * CH], w=[MK], q="gpsimd")

            def loads_d(h):
                b = h % 2
                P.load(KH[b][:], Kscr[h], w=[KH[b]])
                vsrc = Vscr[:, h * 128:(h + 1) * 128].rearrange("(kb p) e -> p kb e", p=128)
                for q4 in range(4):
                    P.load(VH[b][:, q4 * 16:(q4 + 1) * 16, :], vsrc[:, q4 * 16:(q4 + 1) * 16, :], w=[VH[b]])
                P.load(QH[b][:], Qscr[h], w=[QH[b]])
                P.load(GAH[b][:], GAscr[h * 128:(h + 1) * 128, :], w=[GAH[b]])
                P.load(ZCH[b][:], ZCscr[h * 128:(h + 1) * 128, :], w=[ZCH[b]])

            pending = []

            def run_pending(kb):
                while pending and (kb is None or pending[0][0] <= kb):
                    pending.pop(0)[1]()

            def make_finalize(h, i, Gh, Zh, qs):
                acc, o1s, o2s = ACC[i % 2], O1S.next(), O2S.next()
                att, sq = ATT.next(), SQ16.next()

                def st1():
                    P.copy(o1s[:], O1[:], r=[O1], w=[o1s])
                    P.copy(o2s[:], O2[:], r=[O2], w=[o2s])

                def st2():
                    P.mm(L1[:], ONESF[:], acc[:, 0:CH], False, True, r=[ONESF, acc], w=[L1])
                    P.mm(L2[:], ONESF[:], acc[:, CH:2 * CH], False, True, r=[ONESF, acc], w=[L2])

                def st3():
                    P.recip(R1[:], L1[:], r=[L1], w=[R1])
                    P.recip(R2[:], L2[:], r=[L2], w=[R2])
                    P.tt(AT1[:], o1s[:], R1[:], MUL, r=[o1s, R1], w=[AT1])
                    P.tt(AT2[:], o2s[:], R2[:], MUL, r=[o2s, R2], w=[AT2])
                    P.stt(att[:], AT2[:], NEGLAM, AT1[:], MUL, ADD, r=[AT2, AT1, DV], w=[att])

                def st6():
                    P.act(sq[:], att[:], AF.Square, r=[att], w=[sq])

                ms = []

                def st7():
                    MS = SP.next()
                    ms.append(MS)
                    P.mm(MS[:, 0:CH], ONESB[:], sq[:], True, True, r=[ONESB, sq], w=[MS])

                def st8():
                    P.act(LNd[:], ms[0][:, 0:CH], AF.Ln, r=[ms[0]], w=[LNd], bias=EPS, scale=1.0 / 128)
                    P.act(RSd[:], LNd[:], AF.Exp, r=[LNd], w=[RSd], scale=-0.5)

                def st9():
                    P.tt(YA[:], att[:], RSd[:], MUL, r=[att, RSd], w=[YA])
                    P.stt(ZA[:], YA[:], HNV, Gh[:, qs], MUL, MUL, r=[YA, DV, Gh], w=[ZA])
                    zo = ZO.next()
                    P.tt(zo[:], ZA[:], Zh[:, qs], ADD, r=[ZA, Zh], w=[zo])
                    P.store(Zscr[h * 128:(h + 1) * 128, qs], zo[:], r=[zo], q=STQ)

                st1()
                pending.extend([(0, st2), (1, st3), (3, st6), (4, st7), (5, st8), (6, st9)])

            loads_d(0)
            for h in range(8):
                if h + 1 < 8:
                    loads_d(h + 1)
                b = h % 2
                Kh, Vh, Qh, Gh, Zh = KH[b], VH[b], QH[b], GAH[b], ZCH[b]
                for i in range(NSL):
                    nk = 8 * (i + 1)
                    qs = slice(i * CH, (i + 1) * CH)

                    def smm(kb):
                        Sx = SP.next()
                        ks = slice(kb * 128, (kb + 1) * 128)
                        P.mm(Sx[:, 0:CH], Kh[0:64, ks], Qh[0:64, qs], True, True, r=[Kh, Qh], w=[Sx.s(0)])
                        P.mm(Sx[:, CH:2 * CH], Kh[64:128, ks], Qh[64:128, qs], True, True, r=[Kh, Qh], w=[Sx.s(1)])
                        return Sx

                    Snext = smm(0)
                    for kb in range(nk):
                        Sx = Snext
                        pt = PT.next()
                        P.act(pt[:], Sx[:], AF.Exp, r=[Sx], w=[pt], scale=0.125)
                        if kb >= nk - 8:
                            jj = kb - (nk - 8)
                            midx = (i % 2) * 8 + jj
                            P.tt(pt[:, 0:CH], pt[:, 0:CH], MK[:, midx, :], MUL, r=[pt.s(0), MK], w=[pt.s(0)])
                            P.tt(pt[:, CH:2 * CH], pt[:, CH:2 * CH], MK[:, midx, :], MUL, r=[pt.s(1), MK], w=[pt.s(1)])
                        if kb + 1 < nk:
                            Snext = smm(kb + 1)
                        st, sp_ = (kb == 0), (kb == nk - 1)
                        P.mm(O1[:], Vh[:, kb, :], pt[:, 0:CH], st, sp_, r=[Vh, pt.s(0)], w=[O1])
                        P.mm(O2[:], Vh[:, kb, :], pt[:, CH:2 * CH], st, sp_, r=[Vh, pt.s(1)], w=[O2])
                        acc = ACC[i % 2]
                        if kb % 3 == 2:
                            P.mm(L1[:], ONESB[:], pt[:, 0:CH], kb == 2, False, r=[ONESB, pt.s(0)], w=[L1])
                            P.mm(L2[:], ONESB[:], pt[:, CH:2 * CH], kb == 2, False, r=[ONESB, pt.s(1)], w=[L2])
                        elif kb == 0:
                            P.copy(acc[:], pt[:], r=[pt], w=[acc])
                        else:
                            P.tt(acc[:], acc[:], pt[:], ADD, r=[acc, pt], w=[acc])
                        run_pending(kb)
                    make_finalize(h, i, Gh, Zh, qs)
                run_pending(None)
            P.barrier()
            P.emit()

        if "C" in phases:
          with ExitStack() as cc:
            SC = 256
            WO = P.sb(cc, "WO", [128, 8, 1024], BF16)
            WF1 = P.sb(cc, "WF1", [128, 8, 4096], BF16)
            WF2 = P.sb(cc, "WF2", [128, 32, 1024], BF16)
            load_w(WO, w_o, 1024, 0)
            load_w(WF1, w_ff1, 4096, 0)
            for q4 in range(4):
                for cg in range(2):
                    P.load(WF2[:, q4 * 8:(q4 + 1) * 8, cg * 512:(cg + 1) * 512],
                           fm(w_ff2[q4 * 1024:(q4 + 1) * 1024, cg * 512:(cg + 1) * 512]), w=[WF2], q="gpsimd")
            Z = [P.sb(cc, f"ZC{i}", [128, 8, SC], BF16) for i in range(2)]
            X = P.sb(cc, "XC", [128, 8, SC], F32)
            Y = P.sb(cc, "YC", [128, 8, SC], F32)
            H2 = P.sb(cc, "H2C", [128, 8, SC], BF16)
            F1 = P.sb(cc, "F1C", [128, 32, SC], BF16)
            SQ = P.sb(cc, "SQC", [128, 8, SC], BF16)
            SR = P.sb(cc, "SRC", [128, SC], F32)
            RS = P.sb(cc, "RSC", [128, SC], F32)
            TMPs = Pool([P.sb(cc, f"TMC{i}", [128, SC], F32) for i in range(3)])
            RL = Pool([P.sb(cc, f"RLC{i}", [128, SC], F32) for i in range(3)])
            Q16 = Pool([P.sb(cc, f"Q16C{i}", [128, SC], BF16) for i in range(2)])
            ACC = P.ps(cc, "ACCC", [128, CH])
            SSb = P.ps(cc, "SSbC", [128, CH])
            PP = Pool([P.ps(cc, f"PPC{i}", [128, CH]) for i in range(6)])
            NSC = NSL * CH // SC

            def xcols(s):
                slot, hf = divmod(s, CH // SC)
                o = slot * XW + HALO + hf * SC
                return slice(o, o + SC)

            def load_z(s):
                P.load(Z[s % 2][:], fm(Zscr[:, s * SC:(s + 1) * SC]), w=[Z[s % 2]])

            load_z(0)
            P.load(X[:], fm(xT_own[:, xcols(0)]), w=[X])
            for s in range(NSC):
                if s + 1 < NSC:
                    load_z(s + 1)
                Zs = Z[s % 2]
                for n in range(8):
                    PY = PP.next()
                    for j in range(8):
                        P.mm(PY[:, 0:SC], WO[:, j, n * 128:(n + 1) * 128], Zs[:, j, :], j == 0, j == 7, r=[WO, Zs], w=[PY])
                    P.act(Y[:, n, :], PY[:, 0:SC], AF.Identity, r=[PY], w=[Y.s(n)])
                    q16 = Q16.next()
                    P.act(q16[:], PY[:, 0:SC], AF.Square, r=[PY], w=[q16])
                    P.mm(ACC[:, 0:SC], ONESB[:], q16[:], n == 0, n == 7, r=[ONESB, q16], w=[ACC])
                P.act(SR[:], ACC[:, 0:SC], AF.Sqrt, r=[ACC], w=[SR], bias=EPS, scale=1.0 / D)
                P.recip(RS[:], SR[:], r=[SR], w=[RS])
                for n in range(8):
                    tm = TMPs.next()
                    P.stt(tm[:], Y[:, n, :], GP1(n), RS[:], MUL, MUL, r=[Y.s(n), DV, RS], w=[tm])
                    P.tt(X[:, n, :], X[:, n, :], tm[:], ADD, r=[X.s(n), tm], w=[X.s(n)])
                P.act(SQ[:], X[:], AF.Square, r=[X], w=[SQ])
                for j in range(8):
                    P.mm(SSb[:, 0:SC], ONESB[:], SQ[:, j, :], j == 0, j == 7, r=[ONESB, SQ], w=[SSb])
                P.act(SR[:], SSb[:, 0:SC], AF.Sqrt, r=[SSb], w=[SR], bias=EPS, scale=1.0 / D)
                P.recip(RS[:], SR[:], r=[SR], w=[RS])
                for j in range(8):
                    tm = TMPs.next()
                    P.stt(tm[:], X[:, j, :], A2(j), RS[:], MUL, MUL, r=[X.s(j), DV, RS], w=[tm])
                    P.act(H2[:, j, :], tm[:], AF.Identity, r=[tm, MODV], w=[H2.s(j)], bias=SH2(j), scale=1.0)
                for m in range(32):
                    PF = PP.next()
                    for j in range(8):
                        P.mm(PF[:, 0:SC], WF1[:, j, m * 128:(m + 1) * 128], H2[:, j, :], j == 0, j == 7, r=[WF1, H2.s(j)], w=[PF])
                    rl = RL.next()
                    P.ts(rl[:], PF[:, 0:SC], 0.0, None, MAX, None, r=[PF], w=[rl])
                    P.act(F1[:, m, :], rl[:], AF.Square, r=[rl], w=[F1.s(m)])
                for n in range(8):
                    PO = PP.next()
                    for m in range(32):
                        P.mm(PO[:, 0:SC], WF2[:, m, n * 128:(n + 1) * 128], F1[:, m, :], m == 0, m == 31, r=[WF2, F1.s(m)], w=[PO])
                    P.act(Y[:, n, :], PO[:, 0:SC], AF.Identity, r=[PO], w=[Y.s(n)])
                    q16 = Q16.next()
                    P.act(q16[:], PO[:, 0:SC], AF.Square, r=[PO], w=[q16])
                    P.mm(ACC[:, 0:SC], ONESB[:], q16[:], n == 0, n == 7, r=[ONESB, q16], w=[ACC])
                P.act(SR[:], ACC[:, 0:SC], AF.Sqrt, r=[ACC], w=[SR], bias=EPS, scale=1.0 / D)
                P.recip(RS[:], SR[:], r=[SR], w=[RS])
                for n in range(8):
                    tm = TMPs.next()
                    P.stt(tm[:], Y[:, n, :], GP2(n), RS[:], MUL, MUL, r=[Y.s(n), DV, RS], w=[tm])
                    P.tt(X[:, n, :], X[:, n, :], tm[:], ADD, r=[X.s(n), tm], w=[X.s(n)])
                P.store(fm(outT[:, s * SC:(s + 1) * SC]), X[:], r=[X], q=STQ)
                if s + 1 < NSC:
                    P.load(X[:], fm(xT_own[:, xcols(s + 1)]), w=[X])
            P.barrier()
            P.emit()
    return nc


_DBG = None
STQ = "sync"
import os
NCH_RUN = int(os.environ.get("NCH_RUN", "16"))
A_PARTS = os.environ.get("A_PARTS", "nkv")
DBG_OUT = os.environ.get("DBG_OUT", "1") == "1"


def _fmv(v, n):
    return np.ascontiguousarray(np.asarray(v, np.float32).reshape(n, 128).T)


def _host_constants():
    inv_freq = (np.float32(10000.0) ** (-np.arange(0, 64, 2, dtype=np.float32) / np.float32(64))).astype(np.float32)
    pos = np.arange(T, dtype=np.float32)
    ang = (pos[:, None] * inv_freq[None, :]).astype(np.float32)
    cos = np.cos(ang).astype(np.float32).T
    sin = np.sin(ang).astype(np.float32).T
    cos_all = np.ascontiguousarray(np.tile(cos, (4, 1)))
    sin_all = np.ascontiguousarray(np.tile(sin, (4, 1)))
    rm = np.zeros((128, 128), np.float32)
    for blk in (0, 64):
        for d in range(32):
            rm[blk + d + 32, blk + d] = -1.0
            rm[blk + d, blk + d + 32] = 1.0
    k = np.arange(128)[:, None]
    q = np.arange(CH)[None, :]
    diag = [((j * 128 + k) <= q).astype(np.float32) for j in range(4)]
    ones = [np.ones((128, CH), np.float32)] * 4
    zeros = [np.zeros((128, CH), np.float32)] * 4
    return cos_all, sin_all, rm, diag, ones, zeros


def kernel(x, c, w_ada, b_ada, pre_norm1, post_norm1, w_in, conv_w, conv_b, conv_ln_g, conv_ln_b,
           conv_w_out, conv_b_out, lambda_q1, lambda_k1, lambda_q2, lambda_k2, head_norm, w_o,
           pre_norm2, post_norm2, w_ff1, w_ff2):
    x = np.asarray(x, np.float32)
    c = np.asarray(c, np.float32)
    cos_all, sin_all, rm, diag, ones, zeros = _host_constants()
    w_in0 = np.asarray(w_in, np.float32)[0]
    perm = np.arange(7168)
    for base in (2048, 3072):
        for h in range(8):
            for m in range(2):
                perm[base + h * 128 + m * 64: base + h * 128 + m * 64 + 64] = base + m * 512 + h * 64 + np.arange(64)
    w_in_p = np.ascontiguousarray(w_in0[:, perm])
    w_ada0 = np.ascontiguousarray(np.asarray(w_ada, np.float32)[0])
    w_co0 = np.ascontiguousarray(np.asarray(conv_w_out, np.float32)[0])
    w_o0 = np.ascontiguousarray(np.asarray(w_o, np.float32)[0])
    w_ff10 = np.ascontiguousarray(np.asarray(w_ff1, np.float32)[0])
    w_ff20 = np.ascontiguousarray(np.asarray(w_ff2, np.float32)[0])

    in_maps = []
    for core in range(8):
        b, jh = divmod(core, 2)
        own = OWN[jh]
        xT = np.ascontiguousarray(x[b].T)
        xo = np.zeros((D, NSL, XW), np.float32)
        flag = np.ones((128, NSL), np.float32)
        for i, ch in enumerate(own):
            lo = ch * CH - HALO
            if lo < 0:
                xo[:, i, HALO:] = xT[:, 0:CH]
                flag[:, i] = 0.0
            else:
                xo[:, i, :] = xT[:, lo:lo + XW]
        small = np.zeros((128, NS), np.float32)
        small[:, SM_BADA:SM_BADA + 48] = _fmv(b_ada[0], 48)
        small[:, SM_PN1:SM_PN1 + 8] = _fmv(pre_norm1[0], 8)
        small[:, SM_PO1:SM_PO1 + 8] = _fmv(post_norm1[0], 8)
        small[:, SM_PN2:SM_PN2 + 8] = _fmv(pre_norm2[0], 8)
        small[:, SM_PO2:SM_PO2 + 8] = _fmv(post_norm2[0], 8)
        small[:, SM_CB:SM_CB + 8] = _fmv(conv_b[0], 8)
        small[:, SM_LNG:SM_LNG + 8] = _fmv(conv_ln_g[0], 8)
        small[:, SM_LNB:SM_LNB + 8] = _fmv(conv_ln_b[0], 8)
        small[:, SM_BOUT:SM_BOUT + 8] = _fmv(conv_b_out[0], 8)
        small[:, SM_HN] = np.asarray(head_norm, np.float32)[0]
        small[:, SM_FLAG:SM_FLAG + 8] = flag
        small[:, SM_CT:SM_CT + 16] = np.repeat(_fmv(c[b], 8), 2, axis=1)
        for k, v in enumerate((lambda_q1, lambda_k1, lambda_q2, lambda_k2)):
            small[:, SM_LAM + 64 * k:SM_LAM + 64 * (k + 1)] = np.asarray(v, np.float32)[0][None, :]
        cw = np.asarray(conv_w, np.float32)[0]
        small[:, SM_CW:SM_CW + 248] = cw.T.reshape(8, 128, 31).transpose(1, 0, 2).reshape(128, 248)
        idx = np.concatenate([np.arange(ch * CH, (ch + 1) * CH) for ch in own])
        mk = []
        for par in range(2):
            own_is_2i = (par == 0) if jh == 0 else (par == 1)
            mk += (diag + zeros) if own_is_2i else (ones + diag)
        in_maps.append({
            "xT_all": xT, "xT_own": xo.reshape(D, NSL * XW), "w_ada": w_ada0, "w_in": w_in_p, "w_co": w_co0,
            "w_o": w_o0, "w_ff1": w_ff10, "w_ff2": w_ff20, "small": small,
            "cos_all": cos_all, "sin_all": sin_all,
            "cos_own": np.ascontiguousarray(cos_all[:, idx]), "sin_own": np.ascontiguousarray(sin_all[:, idx]),
            "masks": np.ascontiguousarray(np.concatenate(mk, axis=1)), "rmat": rm,
        })
    if _DBG is not None:
        nc = build_program(_DBG, dbg=DBG_OUT)
        res = run_bass_kernel_spmd(nc, in_maps, core_ids=list(range(8)))
        return res, in_maps
    nc = build_program()
    res = run_bass_kernel_spmd(nc, in_maps, core_ids=list(range(8)))
    out = np.empty((NB, T, D), np.float32)
    for core in range(8):
        b, jh = divmod(core, 2)
        oT = np.asarray(res.results[core]["outT"], np.float32)
        for i, ch in enumerate(OWN[jh]):
            out[b, ch * CH:(ch + 1) * CH, :] = oT[:, i * CH:(i + 1) * CH].T
    return out
```

```python
import numpy as np
import concourse.bass as bass
import concourse.mybir as mybir

F32 = mybir.dt.float32
BF16 = mybir.dt.bfloat16
AF = mybir.ActivationFunctionType
ALU = mybir.AluOpType

ENGS = ("tensor", "vector", "scalar", "gpsimd", "sync")
N_DMA_SEMS = 32
SAME_ENGINE_SYNC = True


class Res:
    __slots__ = ("t", "last_w", "readers", "name", "parent", "kids")

    def __init__(self, t, name, parent=None):
        self.t = t
        self.last_w = None
        self.readers = []
        self.name = name
        self.parent = parent
        self.kids = {}

    def s(self, k):
        if k not in self.kids:
            self.kids[k] = Res(self.t, f"{self.name}.{k}", parent=self)
        return self.kids[k]

    def holders(self):
        h = [self]
        if self.parent is not None:
            h.append(self.parent)
        h.extend(self.kids.values())
        return h

    def __getitem__(self, idx):
        return self.t[idx]


class Op:
    __slots__ = ("eng", "fn", "deps", "signal", "count", "is_dma", "sem", "semval", "barrier", "emitted")

    def __init__(self, eng, fn, is_dma=False):
        self.eng = eng
        self.fn = fn
        self.deps = []
        self.signal = False
        self.count = None
        self.is_dma = is_dma
        self.sem = None
        self.semval = None
        self.barrier = False
        self.emitted = False


class Prog:
    def __init__(self, nc):
        self.nc = nc
        self.ops = {e: [] for e in ENGS}
        self.esem = {e: nc.alloc_semaphore(name=f"s_{e}") for e in ENGS}
        self.ecount = {e: 0 for e in ENGS}
        self.waited = {e: {} for e in ENGS}
        self.nd = 2 * N_DMA_SEMS
        self.dsems = [nc.alloc_semaphore(name=f"d_{i}") for i in range(self.nd)]
        self.dsem_uses = [0] * self.nd
        self.dsem_last = [None] * self.nd
        self.dma_rr = {"sync": 0, "gpsimd": 0}
        self.last_op = {e: None for e in ENGS}
        self.n_ops = 0

    def sb(self, ctx, name, shape, dtype):
        t = ctx.enter_context(self.nc.sbuf_tensor(name, list(shape), dtype))
        return Res(t, name)

    def ps(self, ctx, name, shape, dtype=F32):
        t = ctx.enter_context(self.nc.psum_tensor(name, list(shape), dtype))
        return Res(t, name)

    def _deps(self, op, r, w):
        deps = []
        for x0 in r:
            for x in x0.holders():
                if x.last_w is not None:
                    deps.append(x.last_w)
        for x0 in w:
            for x in x0.holders():
                if x.last_w is not None:
                    deps.append(x.last_w)
                deps.extend(x.readers)
        seen = set()
        for d in deps:
            if id(d) in seen or d is op or d.emitted:
                continue
            seen.add(id(d))
            if (not d.is_dma) and (not op.is_dma) and d.eng == op.eng and (d.eng == "tensor" or not SAME_ENGINE_SYNC):
                continue
            op.deps.append(d)
            if not d.is_dma:
                d.signal = True
        for x in r:
            if op.is_dma:
                x.readers.append(op)
            else:
                x.readers = [o for o in x.readers if o.is_dma or o.eng != op.eng]
                x.readers.append(op)
        for x in w:
            x.last_w = op
            x.readers = []

    def add(self, eng, fn, r=(), w=()):
        op = Op(eng, fn)
        self._deps(op, r, w)
        self.ops[eng].append(op)
        self.last_op[eng] = op
        self.n_ops += 1
        return op

    def dma(self, fn, r=(), w=(), q="sync"):
        op = Op(q, fn, is_dma=True)
        k = self.dma_rr[q] % N_DMA_SEMS + (N_DMA_SEMS if q == "gpsimd" else 0)
        self.dma_rr[q] += 1
        prev = self.dsem_last[k]
        self.dsem_uses[k] += 1
        op.sem = k
        op.semval = 16 * self.dsem_uses[k]
        if prev is not None and not prev.emitted:
            op.deps.append(prev)
        self._deps(op, r, w)
        self.dsem_last[k] = op
        self.ops[q].append(op)
        self.n_ops += 1
        return op

    def barrier(self):
        targets = []
        for e in ENGS:
            lo = self.last_op[e]
            if lo is not None:
                lo.signal = True
                targets.append(lo)
        for k in range(self.nd):
            if self.dsem_last[k] is not None:
                targets.append(self.dsem_last[k])
        for e in ENGS:
            op = Op(e, None)
            op.barrier = True
            op.deps = [t for t in targets if t.is_dma or t.eng != e]
            self.ops[e].append(op)

    def emit(self, name=None):
        nc = self.nc
        for e in ENGS:
            c = self.ecount[e]
            for op in self.ops[e]:
                if op.is_dma or op.barrier:
                    continue
                if op.signal:
                    c += 1
                    op.count = c
            self.ecount[e] = c

        def emit_engine(e, eng):
            waited = self.waited[e]
            for op in self.ops[e]:
                for d in op.deps:
                    if d.is_dma:
                        key, val, sem = ("d", d.sem), d.semval, self.dsems[d.sem]
                    else:
                        assert d.count is not None, "dependency on non-signalling op"
                        key, val, sem = ("e", d.eng), d.count, self.esem[d.eng]
                    if waited.get(key, 0) >= val:
                        continue
                    eng.wait_ge(sem, val)
                    waited[key] = val
                if op.barrier:
                    continue
                ins = op.fn(eng)
                if op.is_dma:
                    ins.then_inc(self.dsems[op.sem], 16)
                elif op.signal:
                    ins.then_inc(self.esem[e], 1)

        with nc.Block() as block:
            @block.tensor
            def _(eng):
                emit_engine("tensor", eng)

            @block.vector
            def _(eng):
                emit_engine("vector", eng)

            @block.scalar
            def _(eng):
                emit_engine("scalar", eng)

            @block.gpsimd
            def _(eng):
                emit_engine("gpsimd", eng)

            @block.sync
            def _(eng):
                emit_engine("sync", eng)
        for e in ENGS:
            for op in self.ops[e]:
                op.emitted = True
        self.ops = {e: [] for e in ENGS}

    def mm(self, out, lhsT, rhs, start, stop, r, w, **kw):
        return self.add("tensor", lambda e: e.matmul(out, lhsT, rhs, start=start, stop=stop, **kw), r=r, w=w)

    def act(self, out, in_, func, r, w, bias=None, scale=None, eng="scalar"):
        kw = {}
        if bias is not None:
            kw["bias"] = bias
        if scale is not None:
            kw["scale"] = scale
        return self.add(eng, lambda e: e.activation(out, in_, func, **kw), r=r, w=w)

    def tt(self, out, in0, in1, op, r, w, eng="vector"):
        return self.add(eng, lambda e: e.tensor_tensor(out, in0, in1, op), r=r, w=w)

    def ts(self, out, in0, s1, s2, op0, op1, r, w, eng="vector"):
        if op1 is None:
            return self.add(eng, lambda e: e.tensor_scalar(out, in0, s1, None, op0), r=r, w=w)
        return self.add(eng, lambda e: e.tensor_scalar(out, in0, s1, s2, op0, op1), r=r, w=w)

    def stt(self, out, in0, scalar, in1, op0, op1, r, w):
        return self.add("vector", lambda e: e.scalar_tensor_tensor(out, in0, scalar, in1, op0, op1), r=r, w=w)

    def copy(self, out, in_, r, w, eng="vector"):
        return self.add(eng, lambda e: e.tensor_copy(out, in_), r=r, w=w)

    def recip(self, out, in_, r, w):
        return self.add("vector", lambda e: e.reciprocal(out, in_), r=r, w=w)

    def memset(self, ap, val, w, eng="vector"):
        return self.add(eng, lambda e: e.memset(ap, val), r=(), w=w)

    def load(self, out, in_, w, r=(), q="sync", **kw):
        return self.dma(lambda e: e.dma_start(out=out, in_=in_, **kw), r=r, w=w, q=q)

    def store(self, out, in_, r, w=(), q="sync", **kw):
        return self.dma(lambda e: e.dma_start(out=out, in_=in_, **kw), r=r, w=w, q=q)


from contextlib import ExitStack
from concourse.bass_utils import run_bass_kernel_spmd

D = 1024
T = 8192
NB = 4
NCH = 16
NSL = 8
CH = 512
HALO = 32
XW = CH + HALO
EPS = 1e-6
LAM_INIT = 0.2
OWN = {0: [0, 3, 4, 7, 8, 11, 12, 15], 1: [1, 2, 5, 6, 9, 10, 13, 14]}

SM_BADA = 0
SM_PN1 = 48
SM_PO1 = 56
SM_PN2 = 64
SM_PO2 = 72
SM_CB = 80
SM_LNG = 88
SM_LNB = 96
SM_BOUT = 104
SM_HN = 112
SM_FLAG = 113
SM_CT = 121
SM_LAM = 137
SM_CW = 393
NS = 393 + 248


class Pool:
    def __init__(self, items):
        self.items = items
        self.i = 0

    def next(self):
        x = self.items[self.i % len(self.items)]
        self.i += 1
        return x


def build_program(phases="ABDC", dbg=False):
    nc = bass.Bass("TRN2", target_bir_lowering=False)

    def din(name, shape, dt=F32):
        return nc.dram_tensor(name, list(shape), dt, kind="ExternalInput").ap()

    xT_all = din("xT_all", [D, T])
    xT_own = din("xT_own", [D, NSL * XW])
    w_ada = din("w_ada", [D, 6 * D])
    w_in = din("w_in", [D, 7 * D])
    w_co = din("w_co", [D, D])
    w_o = din("w_o", [D, D])
    w_ff1 = din("w_ff1", [D, 4 * D])
    w_ff2 = din("w_ff2", [4 * D, D])
    small = din("small", [128, NS])
    cos_all = din("cos_all", [128, T])
    sin_all = din("sin_all", [128, T])
    cos_own = din("cos_own", [128, NSL * CH])
    sin_own = din("sin_own", [128, NSL * CH])
    masks = din("masks", [128, 16 * CH])
    rmat = din("rmat", [128, 128])
    ident = din("ident", [128, 128])
    outT = nc.dram_tensor("outT", [D, NSL * CH], F32, kind="ExternalOutput").ap()

    def dscr(name, shape, dt=BF16):
        return nc.dram_tensor(name, list(shape), dt, kind="ExternalOutput" if dbg else "Internal").ap()

    dbg_small = dscr("dbg_small", [128, 88], F32) if dbg else None

    Kscr = dscr("Kscr", [8, 128, T])
    Vscr = dscr("Vscr", [T, D])
    Qscr = dscr("Qscr", [8, 128, NSL * CH])
    ZCscr = dscr("ZCscr", [D, NSL * CH])
    GAscr = dscr("GAscr", [D, NSL * CH])
    Zscr = dscr("Zscr", [D, NSL * CH])

    P = Prog(nc)
    MUL, ADD, SUB, MAX = ALU.mult, ALU.add, ALU.subtract, ALU.max

    def fm(ap):
        return ap.rearrange("(j p) t -> p j t", p=128)

    with ExitStack() as g:
        SM = P.sb(g, "SM", [128, NS], F32)
        MODV = P.sb(g, "MODV", [128, 48], F32)
        DV = P.sb(g, "DV", [128, 40], F32)
        ONESB = P.sb(g, "ONESB", [128, 128], BF16)
        RM = P.sb(g, "RM", [128, 128], BF16)
        IDENT = P.sb(g, "IDENT", [128, 128], BF16)

        def smc(off, n=1):
            return SM[:, off:off + n]

        SH1 = lambda j: MODV[:, 0 + j:1 + j]
        SH2 = lambda j: MODV[:, 24 + j:25 + j]
        A1 = lambda j: DV[:, 0 + j:1 + j]
        A2 = lambda j: DV[:, 8 + j:9 + j]
        GP1 = lambda j: DV[:, 16 + j:17 + j]
        GP2 = lambda j: DV[:, 24 + j:25 + j]
        NEGLAM = DV[:, 32:33]
        HNV = DV[:, 33:34]

        with ExitStack() as c0:
            WA = [P.sb(c0, f"WA{i}", [128, 8, 768], F32) for i in range(2)]
            CS = P.sb(c0, "CS", [128, 16], F32)
            TMPL = P.sb(c0, "TMPL", [128, 64], F32)
            SS = P.sb(c0, "SSl", [128, 4], F32)
            MODP = P.ps(c0, "MODP", [128, 96], F32)
            P.load(SM[:], small, w=[SM])
            P.load(RM[:], rmat, w=[RM], q="gpsimd")
            P.load(IDENT[:], ident, w=[IDENT], q="gpsimd")
            P.memset(ONESB[:], 1.0, w=[ONESB])
            P.act(CS[:], smc(SM_CT, 16), AF.Silu, r=[SM], w=[CS])
            for gi in range(8):
                Wt = WA[gi % 2]
                P.load(Wt[:], fm(w_ada[:, gi * 768:(gi + 1) * 768]), w=[Wt])
                for nn in range(6):
                    n = gi * 6 + nn
                    for j in range(8):
                        P.mm(MODP[:, 2 * n:2 * n + 2], Wt[:, j, nn * 128:(nn + 1) * 128], CS[:, 2 * j:2 * j + 2],
                             j == 0, j == 7, r=[Wt, CS], w=[MODP])
            modp_v = MODP[:].rearrange("p (n two) -> p n two", two=2)[:, :, 0]
            P.tt(MODV[:], modp_v, smc(SM_BADA, 48), ADD, r=[MODP, SM], w=[MODV])
            P.stt(DV[:, 0:8], MODV[:, 8:16], 1.0, smc(SM_PN1, 8), ADD, MUL, r=[MODV, SM], w=[DV])
            P.stt(DV[:, 8:16], MODV[:, 32:40], 1.0, smc(SM_PN2, 8), ADD, MUL, r=[MODV, SM], w=[DV])
            P.tt(DV[:, 16:24], MODV[:, 16:24], smc(SM_PO1, 8), MUL, r=[MODV, SM], w=[DV])
            P.tt(DV[:, 24:32], MODV[:, 40:48], smc(SM_PO2, 8), MUL, r=[MODV, SM], w=[DV])
            for k in range(2):
                P.tt(TMPL[:], smc(SM_LAM + 128 * k, 64), smc(SM_LAM + 128 * k + 64, 64), MUL, r=[SM], w=[TMPL])
                P.add("vector", lambda e, k=k: e.reduce_sum(SS[:, k:k + 1], TMPL[:], mybir.AxisListType.X),
                      r=[TMPL], w=[SS])
            P.act(SS[:, 2:4], SS[:, 0:2], AF.Exp, r=[SS], w=[SS])
            P.stt(DV[:, 32:33], SS[:, 3:4], -LAM_INIT, SS[:, 2:3], ADD, SUB, r=[SS], w=[DV])
            P.ts(DV[:, 33:34], smc(SM_HN, 1), 1.0 - LAM_INIT, None, MUL, None, r=[SM], w=[DV])
            if dbg:
                P.store(dbg_small[:, 0:48], MODV[:], r=[MODV])
                P.store(dbg_small[:, 48:82], DV[:, 0:34], r=[DV])
            P.barrier()
            P.emit()

        def norm_H(X, H, t0, t1, avec, shvec, SQ, SSb, SR, RS, TMPs):
            n = t1 - t0
            P.act(SQ[:, :, t0:t1], X[:, :, t0:t1], AF.Square, r=[X], w=[SQ])
            for j in range(8):
                P.mm(SSb[:, 0:n], ONESB[:], SQ[:, j, t0:t1], j == 0, j == 7, r=[ONESB, SQ], w=[SSb])
            P.act(SR[:, 0:n], SSb[:, 0:n], AF.Sqrt, r=[SSb], w=[SR], bias=EPS, scale=1.0 / D)
            P.recip(RS[:, 0:n], SR[:, 0:n], r=[SR], w=[RS])
            for j in range(8):
                Tm = TMPs.next()
                P.stt(Tm[:, 0:n], X[:, j, t0:t1], avec(j), RS[:, 0:n], MUL, MUL, r=[X, DV, RS], w=[Tm])
                P.act(H[:, j, t0:t1], Tm[:, 0:n], AF.Identity, r=[Tm, MODV], w=[H.s((j, t0))], bias=shvec(j), scale=1.0)

        def load_w(Wt, src, ncols, c0col=0):
            for cc in range(0, ncols, 512):
                P.load(Wt[:, :, cc:cc + 512], fm(src[:, c0col + cc:c0col + cc + 512]), w=[Wt], q="gpsimd")

        if "A" in phases:
          with ExitStack() as ca:
            WK = P.sb(ca, "WK", [128, 8, 1024], BF16)
            WV = P.sb(ca, "WV", [128, 8, 1024], BF16)
            load_w(WK, w_in, 1024, 3072)
            load_w(WV, w_in, 1024, 4096)
            X = [P.sb(ca, f"XA{i}", [128, 8, CH], F32) for i in range(2)]
            H = [P.sb(ca, f"HA{i}", [128, 8, CH], BF16) for i in range(2)]
            SQ = P.sb(ca, "SQA", [128, 8, CH], BF16)
            SR = P.sb(ca, "SRA", [128, CH], F32)
            RS = P.sb(ca, "RSA", [128, CH], F32)
            TMPs = Pool([P.sb(ca, f"TMA{i}", [128, CH], F32) for i in range(3)])
            COS = [P.sb(ca, f"COSA{i}", [128, CH], F32) for i in range(2)]
            SIN = [P.sb(ca, f"SINA{i}", [128, CH], F32) for i in range(2)]
            KRAW = Pool([P.sb(ca, f"KRAW{i}", [128, CH], BF16) for i in range(2)])
            T1 = Pool([P.sb(ca, f"T1A{i}", [128, CH], F32) for i in range(2)])
            T2 = Pool([P.sb(ca, f"T2A{i}", [128, CH], F32) for i in range(2)])
            KRO = Pool([P.sb(ca, f"KRO{i}", [128, CH], BF16) for i in range(3)])
            VSB = Pool([P.sb(ca, f"VSB{i}", [128, 1024], BF16) for i in range(2)])
            SSb = P.ps(ca, "SSbA", [128, CH])
            PP = Pool([P.ps(ca, f"PPA{i}", [128, CH]) for i in range(6)])

            def loads_a(c):
                P.load(X[c % 2][:], fm(xT_all[:, c * CH:(c + 1) * CH]), w=[X[c % 2]])
                P.load(COS[c % 2][:], cos_all[:, c * CH:(c + 1) * CH], w=[COS[c % 2]])
                P.load(SIN[c % 2][:], sin_all[:, c * CH:(c + 1) * CH], w=[SIN[c % 2]])

            loads_a(0)
            for c in range(NCH_RUN):
                if c + 1 < NCH_RUN:
                    loads_a(c + 1)
                Xc, Hc, Cc, Sc = X[c % 2], H[c % 2], COS[c % 2], SIN[c % 2]
                if "n" in A_PARTS:
                    norm_H(Xc, Hc, 0, CH, A1, SH1, SQ, SSb, SR, RS, TMPs)
                for n in range(8 if "k" in A_PARTS else 0):
                    KP = PP.next()
                    for j in range(8):
                        P.mm(KP[:], WK[:, j, n * 128:(n + 1) * 128], Hc[:, j, :], j == 0, j == 7, r=[WK, Hc.s((j, 0))], w=[KP])
                    Kr = KRAW.next()
                    P.act(Kr[:], KP[:], AF.Identity, r=[KP], w=[Kr])
                    KR = PP.next()
                    P.mm(KR[:], RM[:], Kr[:], True, True, r=[RM, Kr], w=[KR])
                    t1, t2, ko = T1.next(), T2.next(), KRO.next()
                    P.tt(t1[:], Kr[:], Cc[:], MUL, r=[Kr, Cc], w=[t1])
                    P.tt(t2[:], KR[:], Sc[:], MUL, r=[KR, Sc], w=[t2])
                    P.tt(ko[:], t1[:], t2[:], ADD, r=[t1, t2], w=[ko])
                    P.store(Kscr[n, :, c * CH:(c + 1) * CH], ko[:], r=[ko], q=STQ)
                for tb in range(4 if "v" in A_PARTS else 0):
                    Vs = VSB.next()
                    for cg in range(2):
                        VP = PP.next()
                        for j in range(8):
                            P.mm(VP[:], Hc[:, j, tb * 128:(tb + 1) * 128], WV[:, j, cg * 512:(cg + 1) * 512],
                                 j == 0, j == 7, r=[Hc.s((j, 0)), WV], w=[VP])
                        if cg == 0:
                            P.act(Vs[:, 0:512], VP[:], AF.Identity, r=[VP], w=[Vs.s(0)])
                        else:
                            P.copy(Vs[:, 512:1024], VP[:], r=[VP], w=[Vs.s(1)])
                    r0 = (c * 4 + tb) * 128
                    P.store(Vscr[r0:r0 + 128, :], Vs[:], r=[Vs], q=STQ)
            P.barrier()
            P.emit()

        if "B" in phases:
          with ExitStack() as cb:
            WUA = P.sb(cb, "WUA", [128, 8, 1024], BF16)
            WUB = P.sb(cb, "WUB", [128, 8, 1024], BF16)
            WQ = P.sb(cb, "WQ", [128, 8, 1024], BF16)
            WG = P.sb(cb, "WG", [128, 8, 2048], BF16)
            WCO = P.sb(cb, "WCO", [128, 8, 1024], BF16)
            load_w(WUA, w_in, 1024, 0)
            load_w(WUB, w_in, 1024, 1024)
            load_w(WQ, w_in, 1024, 2048)
            load_w(WG, w_in, 2048, 5120)
            load_w(WCO, w_co, 1024, 0)
            X = P.sb(cb, "XB", [128, 8, XW], F32)
            H = P.sb(cb, "HB", [128, 8, XW], BF16)
            U = P.sb(cb, "UB", [128, 8, XW], BF16)
            DGp = Pool([P.sb(cb, f"DG{i}", [128, 128], BF16) for i in range(8)])
            CV = P.sb(cb, "CVB", [128, 8, CH], F32)
            S = P.sb(cb, "SB", [128, 8, XW], BF16)
            SQ = S
            SR = P.sb(cb, "SRB", [128, CH], F32)
            RS = P.sb(cb, "RSB", [128, CH], F32)
            MEAN = P.sb(cb, "MEANB", [128, CH], F32)
            MSQ = SR
            VAR = RS
            NBt = P.sb(cb, "NBB", [128, CH], F32)
            TMPs = Pool([P.sb(cb, f"TMB{i}", [128, CH], F32) for i in range(4)])
            TMP2 = TMPs
            SG = Pool([P.sb(cb, f"SGB{i}", [128, CH], F32) for i in range(2)])
            CB16 = Pool([P.sb(cb, f"CB16{i}", [128, CH], BF16) for i in range(2)])
            CQ16 = Pool([P.sb(cb, f"CQ16{i}", [128, CH], BF16) for i in range(2)])
            OB16 = Pool([P.sb(cb, f"OB16{i}", [128, CH], BF16) for i in range(3)])
            COS = [P.sb(cb, f"COSB{i}", [128, CH], F32) for i in range(2)]
            SIN = [P.sb(cb, f"SINB{i}", [128, CH], F32) for i in range(2)]
            SSb = P.ps(cb, "SSbB", [128, CH])
            ACC2 = P.ps(cb, "ACC2B", [128, CH])
            PP = Pool([P.ps(cb, f"PPB{i}", [128, CH]) for i in range(6)])
            CW = lambda n, j: SM[:, SM_CW + n * 31 + j:SM_CW + n * 31 + j + 1]

            def loads_b_tabs(i):
                P.load(COS[i % 2][:], cos_own[:, i * CH:(i + 1) * CH], w=[COS[i % 2]])
                P.load(SIN[i % 2][:], sin_own[:, i * CH:(i + 1) * CH], w=[SIN[i % 2]])

            P.load(X[:], fm(xT_own[:, 0:XW]), w=[X])
            loads_b_tabs(0)
            for i in range(NSL):
                if i + 1 < NSL:
                    loads_b_tabs(i + 1)
                Cc, Sc = COS[i % 2], SIN[i % 2]
                norm_H(X, H, HALO, XW, A1, SH1, SQ, SSb, SR, RS, TMPs)
                norm_H(X, H, 0, HALO, A1, SH1, SQ, SSb, SR, RS, TMPs)
                if i + 1 < NSL:
                    P.load(X[:], fm(xT_own[:, (i + 1) * XW:(i + 2) * XW]), w=[X])
                for n in range(8):
                    for (t0, t1) in ((HALO, XW), (0, HALO)):
                        nn = t1 - t0
                        PA, PB = PP.next(), PP.next()
                        for j in range(8):
                            P.mm(PA[:, 0:nn], WUA[:, j, n * 128:(n + 1) * 128], H[:, j, t0:t1], j == 0, j == 7,
                                 r=[WUA, H.s((j, t0))], w=[PA])
                        for j in range(8):
                            P.mm(PB[:, 0:nn], WUB[:, j, n * 128:(n + 1) * 128], H[:, j, t0:t1], j == 0, j == 7,
                                 r=[WUB, H.s((j, t0))], w=[PB])
                        sg = SG.next()
                        P.act(sg[:, 0:nn], PB[:, 0:nn], AF.Sigmoid, r=[PB], w=[sg])
                        if t0 == 0:
                            P.stt(U[:, n, t0:t1], PA[:, 0:nn], SM[:, SM_FLAG + i:SM_FLAG + i + 1], sg[:, 0:nn], MUL, MUL,
                                  r=[PA, sg, SM], w=[U.s((n, t0))])
                        else:
                            P.tt(U[:, n, t0:t1], PA[:, 0:nn], sg[:, 0:nn], MUL, r=[PA, sg], w=[U.s((n, t0))])
                for n in range(8):
                    CA = PP.next()
                    for j in range(31):
                        dg = DGp.next()
                        P.act(dg[:], IDENT[:], AF.Identity, r=[IDENT, SM], w=[dg], scale=CW(n, j))
                        P.mm(CA[:], dg[:], U[:, n, 2 + j:2 + j + CH], j == 0, j == 30,
                             r=[dg, U.s((n, 0)), U.s((n, HALO))], w=[CA])
                    P.act(CV[:, n, :], CA[:], AF.Identity, r=[CA, SM], w=[CV.s(n)],
                          bias=SM[:, SM_CB + n:SM_CB + n + 1], scale=1.0)
                for n in range(8):
                    cb16, cq16 = CB16.next(), CQ16.next()
                    P.act(cb16[:], CV[:, n, :], AF.Identity, r=[CV.s(n)], w=[cb16])
                    P.act(cq16[:], CV[:, n, :], AF.Square, r=[CV.s(n)], w=[cq16])
                    P.mm(SSb[:], ONESB[:], cb16[:], n == 0, n == 7, r=[ONESB, cb16], w=[SSb])
                    P.mm(ACC2[:], ONESB[:], cq16[:], n == 0, n == 7, r=[ONESB, cq16], w=[ACC2])
                P.ts(MEAN[:], SSb[:], 1.0 / D, None, MUL, None, r=[SSb], w=[MEAN])
                P.tt(MSQ[:], MEAN[:], MEAN[:], MUL, r=[MEAN], w=[MSQ])
                P.stt(VAR[:], ACC2[:], 1.0 / D, MSQ[:], MUL, SUB, r=[ACC2, MSQ], w=[VAR])
                P.act(SR[:], VAR[:], AF.Sqrt, r=[VAR], w=[SR], bias=EPS, scale=1.0)
                P.recip(RS[:], SR[:], r=[SR], w=[RS])
                P.stt(NBt[:], MEAN[:], -1.0, RS[:], MUL, MUL, r=[MEAN, RS], w=[NBt])
                for n in range(8):
                    ta, tb_ = TMPs.next(), TMP2.next()
                    P.tt(ta[:], CV[:, n, :], RS[:], MUL, r=[CV.s(n), RS], w=[ta])
                    P.tt(tb_[:], ta[:], NBt[:], ADD, r=[ta, NBt], w=[tb_])
                    P.act(S[:, n, 0:CH], tb_[:], AF.Silu, r=[tb_, SM], w=[S.s(n)],
                          bias=SM[:, SM_LNB + n:SM_LNB + n + 1], scale=SM[:, SM_LNG + n:SM_LNG + n + 1])
                for n in range(8):
                    PY, PG = PP.next(), PP.next()
                    for j in range(8):
                        P.mm(PY[:], WCO[:, j, n * 128:(n + 1) * 128], S[:, j, 0:CH], j == 0, j == 7, r=[WCO, S.s(j)], w=[PY])
                    for j in range(8):
                        P.mm(PG[:], WG[:, j, n * 128:(n + 1) * 128], H[:, j, HALO:XW], j == 0, j == 7, r=[WG, H.s((j, HALO))], w=[PG])
                    sg = SG.next()
                    P.act(sg[:], PG[:], AF.Sigmoid, r=[PG], w=[sg])
                    ob = OB16.next()
                    P.stt(ob[:], PY[:], SM[:, SM_BOUT + n:SM_BOUT + n + 1], sg[:], ADD, MUL, r=[PY, sg, SM], w=[ob])
                    P.store(ZCscr[n * 128:(n + 1) * 128, i * CH:(i + 1) * CH], ob[:], r=[ob], q=STQ)
                for n in range(8):
                    PG = PP.next()
                    for j in range(8):
                        P.mm(PG[:], WG[:, j, 1024 + n * 128:1024 + (n + 1) * 128], H[:, j, HALO:XW], j == 0, j == 7,
                             r=[WG, H.s((j, HALO))], w=[PG])
                    ob = OB16.next()
                    P.act(ob[:], PG[:], AF.Sigmoid, r=[PG], w=[ob])
                    P.store(GAscr[n * 128:(n + 1) * 128, i * CH:(i + 1) * CH], ob[:], r=[ob], q=STQ)
                for n in range(8):
                    PQ = PP.next()
                    for j in range(8):
                        P.mm(PQ[:], WQ[:, j, n * 128:(n + 1) * 128], H[:, j, HALO:XW], j == 0, j == 7, r=[WQ, H.s((j, HALO))], w=[PQ])
                    qr = CB16.next()
                    P.act(qr[:], PQ[:], AF.Identity, r=[PQ], w=[qr])
                    PR = PP.next()
                    P.mm(PR[:], RM[:], qr[:], True, True, r=[RM, qr], w=[PR])
                    t1, t2, ob = TMPs.next(), TMP2.next(), OB16.next()
                    P.tt(t1[:], qr[:], Cc[:], MUL, r=[qr, Cc], w=[t1])
                    P.tt(t2[:], PR[:], Sc[:], MUL, r=[PR, Sc], w=[t2])
                    P.tt(ob[:], t1[:], t2[:], ADD, r=[t1, t2], w=[ob])
                    P.store(Qscr[n, :, i * CH:(i + 1) * CH], ob[:], r=[ob], q=STQ)
            P.barrier()
            P.emit()

        if "D" in phases:
          with ExitStack() as cd:
            KH = [P.sb(cd, f"KH{i}", [128, T], BF16) for i in range(2)]
            VH = [P.sb(cd, f"VH{i}", [128, 64, 128], BF16) for i in range(2)]
            QH = [P.sb(cd, f"QH{i}", [128, NSL * CH], BF16) for i in range(2)]
            GAH = [P.sb(cd, f"GAH{i}", [128, NSL * CH], BF16) for i in range(2)]
            ZCH = [P.sb(cd, f"ZCH{i}", [128, NSL * CH], BF16) for i in range(2)]
            MK = P.sb(cd, "MK", [128, 16, CH], BF16)
            ONESF = P.sb(cd, "ONESF", [128, 128], F32)
            PT = Pool([P.sb(cd, f"PT{i}", [128, 2 * CH], BF16) for i in range(6)])
            ACC = [P.sb(cd, f"ACCD{i}", [128, 2 * CH], F32) for i in range(2)]
            R1 = P.sb(cd, "R1", [128, CH], F32)
            R2 = P.sb(cd, "R2", [128, CH], F32)
            O1S = Pool([P.sb(cd, f"O1S{i}", [128, CH], F32) for i in range(2)])
            O2S = Pool([P.sb(cd, f"O2S{i}", [128, CH], F32) for i in range(2)])
            AT1 = P.sb(cd, "AT1", [128, CH], F32)
            AT2 = P.sb(cd, "AT2", [128, CH], F32)
            ATT = Pool([P.sb(cd, f"ATT{i}", [128, CH], F32) for i in range(2)])
            SQ16 = Pool([P.sb(cd, f"SQ16{i}", [128, CH], BF16) for i in range(2)])
            LNd = P.sb(cd, "LNd", [128, CH], F32)
            RSd = P.sb(cd, "RSd", [128, CH], F32)
            YA = P.sb(cd, "YA", [128, CH], F32)
            ZA = P.sb(cd, "ZA", [128, CH], F32)
            ZO = Pool([P.sb(cd, f"ZO{i}", [128, CH], BF16) for i in range(2)])
            SP = Pool([P.ps(cd, f"SP{i}", [128, 2 * CH]) for i in range(2)])
            O1 = P.ps(cd, "O1", [128, CH])
            O2 = P.ps(cd, "O2", [128, CH])
            L1 = P.ps(cd, "L1", [128, CH])
            L2 = P.ps(cd, "L2", [128, CH])
            P.memset(ONESF[:], 1.0, w=[ONESF])
            for mi in range(16):
                P.load(MK[:, mi, :], masks[:, mi * CH:(mi + 1) * CH], w=[MK], q="gpsimd")

            def loads_d(h):
                b = h % 2
                P.load(KH[b][:], Kscr[h], w=[KH[b]])
                vsrc = Vscr[:, h * 128:(h + 1) * 128].rearrange("(kb p) e -> p kb e", p=128)
                for q4 in range(4):
                    P.load(VH[b][:, q4 * 16:(q4 + 1) * 16, :], vsrc[:, q4 * 16:(q4 + 1) * 16, :], w=[VH[b]])
                P.load(QH[b][:], Qscr[h], w=[QH[b]])
                P.load(GAH[b][:], GAscr[h * 128:(h + 1) * 128, :], w=[GAH[b]])
                P.load(ZCH[b][:], ZCscr[h * 128:(h + 1) * 128, :], w=[ZCH[b]])

            pending = []

            def run_pending(kb):
                while pending and (kb is None or pending[0][0] <= kb):
                    pending.pop(0)[1]()

            def make_finalize(h, i, Gh, Zh, qs):
                acc, o1s, o2s = ACC[i % 2], O1S.next(), O2S.next()
                att, sq = ATT.next(), SQ16.next()

                def st1():
                    P.copy(o1s[:], O1[:], r=[O1], w=[o1s])
                    P.copy(o2s[:], O2[:], r=[O2], w=[o2s])

                def st2():
                    P.mm(L1[:], ONESF[:], acc[:, 0:CH], False, True, r=[ONESF, acc], w=[L1])
                    P.mm(L2[:], ONESF[:], acc[:, CH:2 * CH], False, True, r=[ONESF, acc], w=[L2])

                def st3():
                    P.recip(R1[:], L1[:], r=[L1], w=[R1])
                    P.recip(R2[:], L2[:], r=[L2], w=[R2])
                    P.tt(AT1[:], o1s[:], R1[:], MUL, r=[o1s, R1], w=[AT1])
                    P.tt(AT2[:], o2s[:], R2[:], MUL, r=[o2s, R2], w=[AT2])
                    P.stt(att[:], AT2[:], NEGLAM, AT1[:], MUL, ADD, r=[AT2, AT1, DV], w=[att])

                def st6():
                    P.act(sq[:], att[:], AF.Square, r=[att], w=[sq])

                ms = []

                def st7():
                    MS = SP.next()
                    ms.append(MS)
                    P.mm(MS[:, 0:CH], ONESB[:], sq[:], True, True, r=[ONESB, sq], w=[MS])

                def st8():
                    P.act(LNd[:], ms[0][:, 0:CH], AF.Ln, r=[ms[0]], w=[LNd], bias=EPS, scale=1.0 / 128)
                    P.act(RSd[:], LNd[:], AF.Exp, r=[LNd], w=[RSd], scale=-0.5)

                def st9():
                    P.tt(YA[:], att[:], RSd[:], MUL, r=[att, RSd], w=[YA])
                    P.stt(ZA[:], YA[:], HNV, Gh[:, qs], MUL, MUL, r=[YA, DV, Gh], w=[ZA])
                    zo = ZO.next()
                    P.tt(zo[:], ZA[:], Zh[:, qs], ADD, r=[ZA, Zh], w=[zo])
                    P.store(Zscr[h * 128:(h + 1) * 128, qs], zo[:], r=[zo], q=STQ)

                st1()
                pending.extend([(0, st2), (1, st3), (3, st6), (4, st7), (5, st8), (6, st9)])

            loads_d(0)
            for h in range(8):
                if h + 1 < 8:
                    loads_d(h + 1)
                b = h % 2
                Kh, Vh, Qh, Gh, Zh = KH[b], VH[b], QH[b], GAH[b], ZCH[b]
                for i in range(NSL):
                    nk = 8 * (i + 1)
                    qs = slice(i * CH, (i + 1) * CH)

                    def smm(kb):
                        Sx = SP.next()
                        ks = slice(kb * 128, (kb + 1) * 128)
                        P.mm(Sx[:, 0:CH], Kh[0:64, ks], Qh[0:64, qs], True, True, r=[Kh, Qh], w=[Sx.s(0)])
                        P.mm(Sx[:, CH:2 * CH], Kh[64:128, ks], Qh[64:128, qs], True, True, r=[Kh, Qh], w=[Sx.s(1)])
                        return Sx

                    Snext = smm(0)
                    for kb in range(nk):
                        Sx = Snext
                        pt = PT.next()
                        P.act(pt[:], Sx[:], AF.Exp, r=[Sx], w=[pt], scale=0.125)
                        if kb >= nk - 8:
                            jj = kb - (nk - 8)
                            midx = (i % 2) * 8 + jj
                            P.tt(pt[:, 0:CH], pt[:, 0:CH], MK[:, midx, :], MUL, r=[pt.s(0), MK], w=[pt.s(0)])
                            P.tt(pt[:, CH:2 * CH], pt[:, CH:2 * CH], MK[:, midx, :], MUL, r=[pt.s(1), MK], w=[pt.s(1)])
                        if kb + 1 < nk:
                            Snext = smm(kb + 1)
                        st, sp_ = (kb == 0), (kb == nk - 1)
                        P.mm(O1[:], Vh[:, kb, :], pt[:, 0:CH], st, sp_, r=[Vh, pt.s(0)], w=[O1])
                        P.mm(O2[:], Vh[:, kb, :], pt[:, CH:2 * CH], st, sp_, r=[Vh, pt.s(1)], w=[O2])
                        acc = ACC[i % 2]
                        if kb % 3 == 2:
                            P.mm(L1[:], ONESB[:], pt[:, 0:CH], kb == 2, False, r=[ONESB, pt.s(0)], w=[L1])
                            P.mm(L2[:], ONESB[:], pt[:, CH:2 * CH], kb == 2, False, r=[ONESB, pt.s(1)], w=[L2])
                        elif kb == 0:
                            P.copy(acc[:], pt[:], r=[pt], w=[acc])
                        else:
                            P.tt(acc[:], acc[:], pt[:], ADD, r=[acc, pt], w=[acc])
                        run_pending(kb)
                    make_finalize(h, i, Gh, Zh, qs)
                run_pending(None)
            P.barrier()
            P.emit()

        if "C" in phases:
          with ExitStack() as cc:
            SC = 256
            WO = P.sb(cc, "WO", [128, 8, 1024], BF16)
            WF1 = P.sb(cc, "WF1", [128, 8, 4096], BF16)
            WF2 = P.sb(cc, "WF2", [128, 32, 1024], BF16)
            load_w(WO, w_o, 1024, 0)
            load_w(WF1, w_ff1, 4096, 0)
            for q4 in range(4):
                for cg in range(2):
                    P.load(WF2[:, q4 * 8:(q4 + 1) * 8, cg * 512:(cg + 1) * 512],
                           fm(w_ff2[q4 * 1024:(q4 + 1) * 1024, cg * 512:(cg + 1) * 512]), w=[WF2], q="gpsimd")
            Z = [P.sb(cc, f"ZC{i}", [128, 8, SC], BF16) for i in range(2)]
            X = P.sb(cc, "XC", [128, 8, SC], F32)
            Y = P.sb(cc, "YC", [128, 8, SC], F32)
            H2 = P.sb(cc, "H2C", [128, 8, SC], BF16)
            F1 = P.sb(cc, "F1C", [128, 32, SC], BF16)
            SQ = P.sb(cc, "SQC", [128, 8, SC], BF16)
            SR = P.sb(cc, "SRC", [128, SC], F32)
            RS = P.sb(cc, "RSC", [128, SC], F32)
            TMPs = Pool([P.sb(cc, f"TMC{i}", [128, SC], F32) for i in range(3)])
            RL = Pool([P.sb(cc, f"RLC{i}", [128, SC], F32) for i in range(3)])
            Q16 = Pool([P.sb(cc, f"Q16C{i}", [128, SC], BF16) for i in range(2)])
            ACC = P.ps(cc, "ACCC", [128, CH])
            SSb = P.ps(cc, "SSbC", [128, CH])
            PP = Pool([P.ps(cc, f"PPC{i}", [128, CH]) for i in range(6)])
            NSC = NSL * CH // SC

            def xcols(s):
                slot, hf = divmod(s, CH // SC)
                o = slot * XW + HALO + hf * SC
                return slice(o, o + SC)

            def load_z(s):
                P.load(Z[s % 2][:], fm(Zscr[:, s * SC:(s + 1) * SC]), w=[Z[s % 2]])

            load_z(0)
            P.load(X[:], fm(xT_own[:, xcols(0)]), w=[X])
            for s in range(NSC):
                if s + 1 < NSC:
                    load_z(s + 1)
                Zs = Z[s % 2]
                for n in range(8):
                    PY = PP.next()
                    for j in range(8):
                        P.mm(PY[:, 0:SC], WO[:, j, n * 128:(n + 1) * 128], Zs[:, j, :], j == 0, j == 7, r=[WO, Zs], w=[PY])
                    P.act(Y[:, n, :], PY[:, 0:SC], AF.Identity, r=[PY], w=[Y.s(n)])
                    q16 = Q16.next()
                    P.act(q16[:], PY[:, 0:SC], AF.Square, r=[PY], w=[q16])
                    P.mm(ACC[:, 0:SC], ONESB[:], q16[:], n == 0, n == 7, r=[ONESB, q16], w=[ACC])
                P.act(SR[:], ACC[:, 0:SC], AF.Sqrt, r=[ACC], w=[SR], bias=EPS, scale=1.0 / D)
                P.recip(RS[:], SR[:], r=[SR], w=[RS])
                for n in range(8):
                    tm = TMPs.next()
                    P.stt(tm[:], Y[:, n, :], GP1(n), RS[:], MUL, MUL, r=[Y.s(n), DV, RS], w=[tm])
                    P.tt(X[:, n, :], X[:, n, :], tm[:], ADD, r=[X.s(n), tm], w=[X.s(n)])
                P.act(SQ[:], X[:], AF.Square, r=[X], w=[SQ])
                for j in range(8):
                    P.mm(SSb[:, 0:SC], ONESB[:], SQ[:, j, :], j == 0, j == 7, r=[ONESB, SQ], w=[SSb])
                P.act(SR[:], SSb[:, 0:SC], AF.Sqrt, r=[SSb], w=[SR], bias=EPS, scale=1.0 / D)
                P.recip(RS[:], SR[:], r=[SR], w=[RS])
                for j in range(8):
                    tm = TMPs.next()
                    P.stt(tm[:], X[:, j, :], A2(j), RS[:], MUL, MUL, r=[X.s(j), DV, RS], w=[tm])
                    P.act(H2[:, j, :], tm[:], AF.Identity, r=[tm, MODV], w=[H2.s(j)], bias=SH2(j), scale=1.0)
                for m in range(32):
                    PF = PP.next()
                    for j in range(8):
                        P.mm(PF[:, 0:SC], WF1[:, j, m * 128:(m + 1) * 128], H2[:, j, :], j == 0, j == 7, r=[WF1, H2.s(j)], w=[PF])
                    rl = RL.next()
                    P.ts(rl[:], PF[:, 0:SC], 0.0, None, MAX, None, r=[PF], w=[rl])
                    P.act(F1[:, m, :], rl[:], AF.Square, r=[rl], w=[F1.s(m)])
                for n in range(8):
                    PO = PP.next()
                    for m in range(32):
                        P.mm(PO[:, 0:SC], WF2[:, m, n * 128:(n + 1) * 128], F1[:, m, :], m == 0, m == 31, r=[WF2, F1.s(m)], w=[PO])
                    P.act(Y[:, n, :], PO[:, 0:SC], AF.Identity, r=[PO], w=[Y.s(n)])
                    q16 = Q16.next()
                    P.act(q16[:], PO[:, 0:SC], AF.Square, r=[PO], w=[q16])
                    P.mm(ACC[:, 0:SC], ONESB[:], q16[:], n == 0, n == 7, r=[ONESB, q16], w=[ACC])
                P.act(SR[:], ACC[:, 0:SC], AF.Sqrt, r=[ACC], w=[SR], bias=EPS, scale=1.0 / D)
                P.recip(RS[:], SR[:], r=[SR], w=[RS])
                for n in range(8):
                    tm = TMPs.next()
                    P.stt(tm[:], Y[:, n, :], GP2(n), RS[:], MUL, MUL, r=[Y.s(n), DV, RS], w=[tm])
                    P.tt(X[:, n, :], X[:, n, :], tm[:], ADD, r=[X.s(n), tm], w=[X.s(n)])
                P.store(fm(outT[:, s * SC:(s + 1) * SC]), X[:], r=[X], q=STQ)
                if s + 1 < NSC:
                    P.load(X[:], fm(xT_own[:, xcols(s + 1)]), w=[X])
            P.barrier()
            P.emit()
    return nc


_DBG = None
STQ = "sync"
import os
NCH_RUN = int(os.environ.get("NCH_RUN", "16"))
A_PARTS = os.environ.get("A_PARTS", "nkv")
DBG_OUT = os.environ.get("DBG_OUT", "1") == "1"


def _fmv(v, n):
    return np.ascontiguousarray(np.asarray(v, np.float32).reshape(n, 128).T)


def _host_constants():
    inv_freq = (np.float32(10000.0) ** (-np.arange(0, 64, 2, dtype=np.float32) / np.float32(64))).astype(np.float32)
    pos = np.arange(T, dtype=np.float32)
    ang = (pos[:, None] * inv_freq[None, :]).astype(np.float32)
    cos = np.cos(ang).astype(np.float32).T
    sin = np.sin(ang).astype(np.float32).T
    cos_all = np.ascontiguousarray(np.tile(cos, (4, 1)))
    sin_all = np.ascontiguousarray(np.tile(sin, (4, 1)))
    rm = np.zeros((128, 128), np.float32)
    for blk in (0, 64):
        for d in range(32):
            rm[blk + d + 32, blk + d] = -1.0
            rm[blk + d, blk + d + 32] = 1.0
    k = np.arange(128)[:, None]
    q = np.arange(CH)[None, :]
    diag = [((j * 128 + k) <= q).astype(np.float32) for j in range(4)]
    ones = [np.ones((128, CH), np.float32)] * 4
    zeros = [np.zeros((128, CH), np.float32)] * 4
    return cos_all, sin_all, rm, diag, ones, zeros


def kernel(x, c, w_ada, b_ada, pre_norm1, post_norm1, w_in, conv_w, conv_b, conv_ln_g, conv_ln_b,
           conv_w_out, conv_b_out, lambda_q1, lambda_k1, lambda_q2, lambda_k2, head_norm, w_o,
           pre_norm2, post_norm2, w_ff1, w_ff2):
    x = np.asarray(x, np.float32)
    c = np.asarray(c, np.float32)
    cos_all, sin_all, rm, diag, ones, zeros = _host_constants()
    w_in0 = np.asarray(w_in, np.float32)[0]
    perm = np.arange(7168)
    for base in (2048, 3072):
        for h in range(8):
            for m in range(2):
                perm[base + h * 128 + m * 64: base + h * 128 + m * 64 + 64] = base + m * 512 + h * 64 + np.arange(64)
    w_in_p = np.ascontiguousarray(w_in0[:, perm])
    w_ada0 = np.ascontiguousarray(np.asarray(w_ada, np.float32)[0])
    w_co0 = np.ascontiguousarray(np.asarray(conv_w_out, np.float32)[0])
    w_o0 = np.ascontiguousarray(np.asarray(w_o, np.float32)[0])
    w_ff10 = np.ascontiguousarray(np.asarray(w_ff1, np.float32)[0])
    w_ff20 = np.ascontiguousarray(np.asarray(w_ff2, np.float32)[0])

    in_maps = []
    for core in range(8):
        b, jh = divmod(core, 2)
        own = OWN[jh]
        xT = np.ascontiguousarray(x[b].T)
        xo = np.zeros((D, NSL, XW), np.float32)
        flag = np.ones((128, NSL), np.float32)
        for i, ch in enumerate(own):
            lo = ch * CH - HALO
            if lo < 0:
                xo[:, i, HALO:] = xT[:, 0:CH]
                flag[:, i] = 0.0
            else:
                xo[:, i, :] = xT[:, lo:lo + XW]
        small = np.zeros((128, NS), np.float32)
        small[:, SM_BADA:SM_BADA + 48] = _fmv(b_ada[0], 48)
        small[:, SM_PN1:SM_PN1 + 8] = _fmv(pre_norm1[0], 8)
        small[:, SM_PO1:SM_PO1 + 8] = _fmv(post_norm1[0], 8)
        small[:, SM_PN2:SM_PN2 + 8] = _fmv(pre_norm2[0], 8)
        small[:, SM_PO2:SM_PO2 + 8] = _fmv(post_norm2[0], 8)
        small[:, SM_CB:SM_CB + 8] = _fmv(conv_b[0], 8)
        small[:, SM_LNG:SM_LNG + 8] = _fmv(conv_ln_g[0], 8)
        small[:, SM_LNB:SM_LNB + 8] = _fmv(conv_ln_b[0], 8)
        small[:, SM_BOUT:SM_BOUT + 8] = _fmv(conv_b_out[0], 8)
        small[:, SM_HN] = np.asarray(head_norm, np.float32)[0]
        small[:, SM_FLAG:SM_FLAG + 8] = flag
        small[:, SM_CT:SM_CT + 16] = np.repeat(_fmv(c[b], 8), 2, axis=1)
        for k, v in enumerate((lambda_q1, lambda_k1, lambda_q2, lambda_k2)):
            small[:, SM_LAM + 64 * k:SM_LAM + 64 * (k + 1)] = np.asarray(v, np.float32)[0][None, :]
        cw = np.asarray(conv_w, np.float32)[0]
        small[:, SM_CW:SM_CW + 248] = cw.T.reshape(8, 128, 31).transpose(1, 0, 2).reshape(128, 248)
        idx = np.concatenate([np.arange(ch * CH, (ch + 1) * CH) for ch in own])
        mk = []
        for par in range(2):
            own_is_2i = (par == 0) if jh == 0 else (par == 1)
            mk += (diag + zeros) if own_is_2i else (ones + diag)
        in_maps.append({
            "xT_all": xT, "xT_own": xo.reshape(D, NSL * XW), "w_ada": w_ada0, "w_in": w_in_p, "w_co": w_co0,
            "w_o": w_o0, "w_ff1": w_ff10, "w_ff2": w_ff20, "small": small,
            "cos_all": cos_all, "sin_all": sin_all,
            "cos_own": np.ascontiguousarray(cos_all[:, idx]), "sin_own": np.ascontiguousarray(sin_all[:, idx]),
            "masks": np.ascontiguousarray(np.concatenate(mk, axis=1)), "rmat": rm,
            "ident": np.eye(128, dtype=np.float32),
        })
    if _DBG is not None:
        nc = build_program(_DBG, dbg=DBG_OUT)
        res = run_bass_kernel_spmd(nc, in_maps, core_ids=list(range(8)))
        return res, in_maps
    nc = build_program()
    res = run_bass_kernel_spmd(nc, in_maps, core_ids=list(range(8)))
    out = np.empty((NB, T, D), np.float32)
    for core in range(8):
        b, jh = divmod(core, 2)
        oT = np.asarray(res.results[core]["outT"], np.float32)
        for i, ch in enumerate(OWN[jh]):
            out[b, ch * CH:(ch + 1) * CH, :] = oT[:, i * CH:(i + 1) * CH].T
    return out
```

```python
import numpy as np
import concourse.bass as bass
import concourse.mybir as mybir

F32 = mybir.dt.float32
BF16 = mybir.dt.bfloat16
AF = mybir.ActivationFunctionType
ALU = mybir.AluOpType

ENGS = ("tensor", "vector", "scalar", "gpsimd", "sync")
N_DMA_SEMS = 32
SAME_ENGINE_SYNC = True


class Res:
    __slots__ = ("t", "last_w", "readers", "name", "parent", "kids")

    def __init__(self, t, name, parent=None):
        self.t = t
        self.last_w = None
        self.readers = []
        self.name = name
        self.parent = parent
        self.kids = {}

    def s(self, k):
        if k not in self.kids:
            self.kids[k] = Res(self.t, f"{self.name}.{k}", parent=self)
        return self.kids[k]

    def holders(self):
        h = [self]
        if self.parent is not None:
            h.append(self.parent)
        h.extend(self.kids.values())
        return h

    def __getitem__(self, idx):
        return self.t[idx]


class Op:
    __slots__ = ("eng", "fn", "deps", "signal", "count", "is_dma", "sem", "semval", "barrier", "emitted")

    def __init__(self, eng, fn, is_dma=False):
        self.eng = eng
        self.fn = fn
        self.deps = []
        self.signal = False
        self.count = None
        self.is_dma = is_dma
        self.sem = None
        self.semval = None
        self.barrier = False
        self.emitted = False


class Prog:
    def __init__(self, nc):
        self.nc = nc
        self.ops = {e: [] for e in ENGS}
        self.esem = {e: nc.alloc_semaphore(name=f"s_{e}") for e in ENGS}
        self.ecount = {e: 0 for e in ENGS}
        self.waited = {e: {} for e in ENGS}
        self.nd = 2 * N_DMA_SEMS
        self.dsems = [nc.alloc_semaphore(name=f"d_{i}") for i in range(self.nd)]
        self.dsem_uses = [0] * self.nd
        self.dsem_last = [None] * self.nd
        self.dma_rr = {"sync": 0, "gpsimd": 0}
        self.last_op = {e: None for e in ENGS}
        self.n_ops = 0

    def sb(self, ctx, name, shape, dtype):
        t = ctx.enter_context(self.nc.sbuf_tensor(name, list(shape), dtype))
        return Res(t, name)

    def ps(self, ctx, name, shape, dtype=F32):
        t = ctx.enter_context(self.nc.psum_tensor(name, list(shape), dtype))
        return Res(t, name)

    def _deps(self, op, r, w):
        deps = []
        for x0 in r:
            for x in x0.holders():
                if x.last_w is not None:
                    deps.append(x.last_w)
        for x0 in w:
            for x in x0.holders():
                if x.last_w is not None:
                    deps.append(x.last_w)
                deps.extend(x.readers)
        seen = set()
        for d in deps:
            if id(d) in seen or d is op or d.emitted:
                continue
            seen.add(id(d))
            if (not d.is_dma) and (not op.is_dma) and d.eng == op.eng and (d.eng == "tensor" or not SAME_ENGINE_SYNC):
                continue
            op.deps.append(d)
            if not d.is_dma:
                d.signal = True
        for x in r:
            if op.is_dma:
                x.readers.append(op)
            else:
                x.readers = [o for o in x.readers if o.is_dma or o.eng != op.eng]
                x.readers.append(op)
        for x in w:
            x.last_w = op
            x.readers = []

    def add(self, eng, fn, r=(), w=()):
        op = Op(eng, fn)
        self._deps(op, r, w)
        self.ops[eng].append(op)
        self.last_op[eng] = op
        self.n_ops += 1
        return op

    def dma(self, fn, r=(), w=(), q="sync"):
        op = Op(q, fn, is_dma=True)
        k = self.dma_rr[q] % N_DMA_SEMS + (N_DMA_SEMS if q == "gpsimd" else 0)
        self.dma_rr[q] += 1
        prev = self.dsem_last[k]
        self.dsem_uses[k] += 1
        op.sem = k
        op.semval = 16 * self.dsem_uses[k]
        if prev is not None and not prev.emitted:
            op.deps.append(prev)
        self._deps(op, r, w)
        self.dsem_last[k] = op
        self.ops[q].append(op)
        self.n_ops += 1
        return op

    def barrier(self):
        targets = []
        for e in ENGS:
            lo = self.last_op[e]
            if lo is not None:
                lo.signal = True
                targets.append(lo)
        for k in range(self.nd):
            if self.dsem_last[k] is not None:
                targets.append(self.dsem_last[k])
        for e in ENGS:
            op = Op(e, None)
            op.barrier = True
            op.deps = [t for t in targets if t.is_dma or t.eng != e]
            self.ops[e].append(op)

    def emit(self, name=None):
        nc = self.nc
        for e in ENGS:
            c = self.ecount[e]
            for op in self.ops[e]:
                if op.is_dma or op.barrier:
                    continue
                if op.signal:
                    c += 1
                    op.count = c
            self.ecount[e] = c

        def emit_engine(e, eng):
            waited = self.waited[e]
            for op in self.ops[e]:
                for d in op.deps:
                    if d.is_dma:
                        key, val, sem = ("d", d.sem), d.semval, self.dsems[d.sem]
                    else:
                        assert d.count is not None, "dependency on non-signalling op"
                        key, val, sem = ("e", d.eng), d.count, self.esem[d.eng]
                    if waited.get(key, 0) >= val:
                        continue
                    eng.wait_ge(sem, val)
                    waited[key] = val
                if op.barrier:
                    continue
                ins = op.fn(eng)
                if op.is_dma:
                    ins.then_inc(self.dsems[op.sem], 16)
                elif op.signal:
                    ins.then_inc(self.esem[e], 1)

        with nc.Block() as block:
            @block.tensor
            def _(eng):
                emit_engine("tensor", eng)

            @block.vector
            def _(eng):
                emit_engine("vector", eng)

            @block.scalar
            def _(eng):
                emit_engine("scalar", eng)

            @block.gpsimd
            def _(eng):
                emit_engine("gpsimd", eng)

            @block.sync
            def _(eng):
                emit_engine("sync", eng)
        for e in ENGS:
            for op in self.ops[e]:
                op.emitted = True
        self.ops = {e: [] for e in ENGS}

    def mm(self, out, lhsT, rhs, start, stop, r, w, **kw):
        return self.add("tensor", lambda e: e.matmul(out, lhsT, rhs, start=start, stop=stop, **kw), r=r, w=w)

    def act(self, out, in_, func, r, w, bias=None, scale=None, eng="scalar"):
        kw = {}
        if bias is not None:
            kw["bias"] = bias
        if scale is not None:
            kw["scale"] = scale
        return self.add(eng, lambda e: e.activation(out, in_, func, **kw), r=r, w=w)

    def tt(self, out, in0, in1, op, r, w, eng="vector"):
        return self.add(eng, lambda e: e.tensor_tensor(out, in0, in1, op), r=r, w=w)

    def ts(self, out, in0, s1, s2, op0, op1, r, w, eng="vector"):
        if op1 is None:
            return self.add(eng, lambda e: e.tensor_scalar(out, in0, s1, None, op0), r=r, w=w)
        return self.add(eng, lambda e: e.tensor_scalar(out, in0, s1, s2, op0, op1), r=r, w=w)

    def stt(self, out, in0, scalar, in1, op0, op1, r, w):
        return self.add("vector", lambda e: e.scalar_tensor_tensor(out, in0, scalar, in1, op0, op1), r=r, w=w)

    def copy(self, out, in_, r, w, eng="vector"):
        return self.add(eng, lambda e: e.tensor_copy(out, in_), r=r, w=w)

    def recip(self, out, in_, r, w):
        return self.add("vector", lambda e: e.reciprocal(out, in_), r=r, w=w)

    def memset(self, ap, val, w, eng="vector"):
        return self.add(eng, lambda e: e.memset(ap, val), r=(), w=w)

    def load(self, out, in_, w, r=(), q="sync", **kw):
        return self.dma(lambda e: e.dma_start(out=out, in_=in_, **kw), r=r, w=w, q=q)

    def store(self, out, in_, r, w=(), q="sync", **kw):
        return self.dma(lambda e: e.dma_start(out=out, in_=in_, **kw), r=r, w=w, q=q)


from contextlib import ExitStack
from concourse.bass_utils import run_bass_kernel_spmd

D = 1024
T = 8192
NB = 4
NCH = 16
NSL = 8
CH = 512
HALO = 32
XW = CH + HALO
EPS = 1e-6
LAM_INIT = 0.2
OWN = {0: [0, 3, 4, 7, 8, 11, 12, 15], 1: [1, 2, 5, 6, 9, 10, 13, 14]}

SM_BADA = 0
SM_PN1 = 48
SM_PO1 = 56
SM_PN2 = 64
SM_PO2 = 72
SM_CB = 80
SM_LNG = 88
SM_LNB = 96
SM_BOUT = 104
SM_HN = 112
SM_FLAG = 113
SM_CT = 121
SM_LAM = 137
SM_CW = 393
NS = 393 + 248


class Pool:
    def __init__(self, items):
        self.items = items
        self.i = 0

    def next(self):
        x = self.items[self.i % len(self.items)]
        self.i += 1
        return x


def build_program(phases="ABDC", dbg=False):
    nc = bass.Bass("TRN2", target_bir_lowering=False)

    def din(name, shape, dt=F32):
        return nc.dram_tensor(name, list(shape), dt, kind="ExternalInput").ap()

    xT_all = din("xT_all", [D, T])
    xT_own = din("xT_own", [D, NSL * XW])
    w_ada = din("w_ada", [D, 6 * D])
    w_in = din("w_in", [D, 7 * D])
    w_co = din("w_co", [D, D])
    w_o = din("w_o", [D, D])
    w_ff1 = din("w_ff1", [D, 4 * D])
    w_ff2 = din("w_ff2", [4 * D, D])
    small = din("small", [128, NS])
    cos_all = din("cos_all", [128, T])
    sin_all = din("sin_all", [128, T])
    cos_own = din("cos_own", [128, NSL * CH])
    sin_own = din("sin_own", [128, NSL * CH])
    masks = din("masks", [128, 16 * CH])
    rmat = din("rmat", [128, 128])
    ident = din("ident", [128, 128])
    outT = nc.dram_tensor("outT", [D, NSL * CH], F32, kind="ExternalOutput").ap()

    def dscr(name, shape, dt=BF16):
        return nc.dram_tensor(name, list(shape), dt, kind="ExternalOutput" if dbg else "Internal").ap()

    dbg_small = dscr("dbg_small", [128, 88], F32) if dbg else None

    Kscr = dscr("Kscr", [8, 128, T])
    Vscr = dscr("Vscr", [T, D])
    Qscr = dscr("Qscr", [8, 128, NSL * CH])
    ZCscr = dscr("ZCscr", [D, NSL * CH])
    GAscr = dscr("GAscr", [D, NSL * CH])
    Zscr = dscr("Zscr", [D, NSL * CH])

    P = Prog(nc)
    MUL, ADD, SUB, MAX = ALU.mult, ALU.add, ALU.subtract, ALU.max

    def fm(ap):
        return ap.rearrange("(j p) t -> p j t", p=128)

    with ExitStack() as g:
        SM = P.sb(g, "SM", [128, NS], F32)
        MODV = P.sb(g, "MODV", [128, 48], F32)
        DV = P.sb(g, "DV", [128, 40], F32)
        ONESB = P.sb(g, "ONESB", [128, 128], BF16)
        RM = P.sb(g, "RM", [128, 128], BF16)
        IDENT = P.sb(g, "IDENT", [128, 128], BF16)

        def smc(off, n=1):
            return SM[:, off:off + n]

        SH1 = lambda j: MODV[:, 0 + j:1 + j]
        SH2 = lambda j: MODV[:, 24 + j:25 + j]
        A1 = lambda j: DV[:, 0 + j:1 + j]
        A2 = lambda j: DV[:, 8 + j:9 + j]
        GP1 = lambda j: DV[:, 16 + j:17 + j]
        GP2 = lambda j: DV[:, 24 + j:25 + j]
        NEGLAM = DV[:, 32:33]
        HNV = DV[:, 33:34]

        with ExitStack() as c0:
            WA = [P.sb(c0, f"WA{i}", [128, 8, 768], F32) for i in range(2)]
            CS = P.sb(c0, "CS", [128, 16], F32)
            TMPL = P.sb(c0, "TMPL", [128, 64], F32)
            SS = P.sb(c0, "SSl", [128, 4], F32)
            MODP = P.ps(c0, "MODP", [128, 96], F32)
            P.load(SM[:], small, w=[SM])
            P.load(RM[:], rmat, w=[RM], q="gpsimd")
            P.load(IDENT[:], ident, w=[IDENT], q="gpsimd")
            P.memset(ONESB[:], 1.0, w=[ONESB])
            P.act(CS[:], smc(SM_CT, 16), AF.Silu, r=[SM], w=[CS])
            for gi in range(8):
                Wt = WA[gi % 2]
                P.load(Wt[:], fm(w_ada[:, gi * 768:(gi + 1) * 768]), w=[Wt])
                for nn in range(6):
                    n = gi * 6 + nn
                    for j in range(8):
                        P.mm(MODP[:, 2 * n:2 * n + 2], Wt[:, j, nn * 128:(nn + 1) * 128], CS[:, 2 * j:2 * j + 2],
                             j == 0, j == 7, r=[Wt, CS], w=[MODP])
            modp_v = MODP[:].rearrange("p (n two) -> p n two", two=2)[:, :, 0]
            P.tt(MODV[:], modp_v, smc(SM_BADA, 48), ADD, r=[MODP, SM], w=[MODV])
            P.stt(DV[:, 0:8], MODV[:, 8:16], 1.0, smc(SM_PN1, 8), ADD, MUL, r=[MODV, SM], w=[DV])
            P.stt(DV[:, 8:16], MODV[:, 32:40], 1.0, smc(SM_PN2, 8), ADD, MUL, r=[MODV, SM], w=[DV])
            P.tt(DV[:, 16:24], MODV[:, 16:24], smc(SM_PO1, 8), MUL, r=[MODV, SM], w=[DV])
            P.tt(DV[:, 24:32], MODV[:, 40:48], smc(SM_PO2, 8), MUL, r=[MODV, SM], w=[DV])
            for k in range(2):
                P.tt(TMPL[:], smc(SM_LAM + 128 * k, 64), smc(SM_LAM + 128 * k + 64, 64), MUL, r=[SM], w=[TMPL])
                P.add("vector", lambda e, k=k: e.reduce_sum(SS[:, k:k + 1], TMPL[:], mybir.AxisListType.X),
                      r=[TMPL], w=[SS])
            P.act(SS[:, 2:4], SS[:, 0:2], AF.Exp, r=[SS], w=[SS])
            P.stt(DV[:, 32:33], SS[:, 3:4], -LAM_INIT, SS[:, 2:3], ADD, SUB, r=[SS], w=[DV])
            P.ts(DV[:, 33:34], smc(SM_HN, 1), 1.0 - LAM_INIT, None, MUL, None, r=[SM], w=[DV])
            if dbg:
                P.store(dbg_small[:, 0:48], MODV[:], r=[MODV])
                P.store(dbg_small[:, 48:82], DV[:, 0:34], r=[DV])
            P.barrier()
            P.emit()

        def norm_H(X, H, t0, t1, avec, shvec, SQ, SSb, SR, RS, TMPs):
            n = t1 - t0
            P.act(SQ[:, :, t0:t1], X[:, :, t0:t1], AF.Square, r=[X], w=[SQ])
            for j in range(8):
                P.mm(SSb[:, 0:n], ONESB[:], SQ[:, j, t0:t1], j == 0, j == 7, r=[ONESB, SQ], w=[SSb])
            P.act(SR[:, 0:n], SSb[:, 0:n], AF.Sqrt, r=[SSb], w=[SR], bias=EPS, scale=1.0 / D)
            P.recip(RS[:, 0:n], SR[:, 0:n], r=[SR], w=[RS])
            for j in range(8):
                Tm = TMPs.next()
                P.stt(Tm[:, 0:n], X[:, j, t0:t1], avec(j), RS[:, 0:n], MUL, MUL, r=[X, DV, RS], w=[Tm])
                P.act(H[:, j, t0:t1], Tm[:, 0:n], AF.Identity, r=[Tm, MODV], w=[H.s((j, t0))], bias=shvec(j), scale=1.0)

        def load_w(Wt, src, ncols, c0col=0):
            for cc in range(0, ncols, 512):
                P.load(Wt[:, :, cc:cc + 512], fm(src[:, c0col + cc:c0col + cc + 512]), w=[Wt], q="gpsimd")

        if "A" in phases:
          with ExitStack() as ca:
            WK = P.sb(ca, "WK", [128, 8, 1024], BF16)
            WV = P.sb(ca, "WV", [128, 8, 1024], BF16)
            load_w(WK, w_in, 1024, 3072)
            load_w(WV, w_in, 1024, 4096)
            X = [P.sb(ca, f"XA{i}", [128, 8, CH], F32) for i in range(2)]
            H = [P.sb(ca, f"HA{i}", [128, 8, CH], BF16) for i in range(2)]
            SQ = P.sb(ca, "SQA", [128, 8, CH], BF16)
            SR = P.sb(ca, "SRA", [128, CH], F32)
            RS = P.sb(ca, "RSA", [128, CH], F32)
            TMPs = Pool([P.sb(ca, f"TMA{i}", [128, CH], F32) for i in range(3)])
            COS = [P.sb(ca, f"COSA{i}", [128, CH], F32) for i in range(2)]
            SIN = [P.sb(ca, f"SINA{i}", [128, CH], F32) for i in range(2)]
            KRAW = Pool([P.sb(ca, f"KRAW{i}", [128, CH], BF16) for i in range(2)])
            T1 = Pool([P.sb(ca, f"T1A{i}", [128, CH], F32) for i in range(2)])
            T2 = Pool([P.sb(ca, f"T2A{i}", [128, CH], F32) for i in range(2)])
            KRO = Pool([P.sb(ca, f"KRO{i}", [128, CH], BF16) for i in range(3)])
            VSB = Pool([P.sb(ca, f"VSB{i}", [128, 1024], BF16) for i in range(2)])
            SSb = P.ps(ca, "SSbA", [128, CH])
            PP = Pool([P.ps(ca, f"PPA{i}", [128, CH]) for i in range(6)])

            def loads_a(c):
                P.load(X[c % 2][:], fm(xT_all[:, c * CH:(c + 1) * CH]), w=[X[c % 2]])
                P.load(COS[c % 2][:], cos_all[:, c * CH:(c + 1) * CH], w=[COS[c % 2]])
                P.load(SIN[c % 2][:], sin_all[:, c * CH:(c + 1) * CH], w=[SIN[c % 2]])

            loads_a(0)
            for c in range(NCH_RUN):
                if c + 1 < NCH_RUN:
                    loads_a(c + 1)
                Xc, Hc, Cc, Sc = X[c % 2], H[c % 2], COS[c % 2], SIN[c % 2]
                if "n" in A_PARTS:
                    norm_H(Xc, Hc, 0, CH, A1, SH1, SQ, SSb, SR, RS, TMPs)
                for n in range(8 if "k" in A_PARTS else 0):
                    KP = PP.next()
                    for j in range(8):
                        P.mm(KP[:], WK[:, j, n * 128:(n + 1) * 128], Hc[:, j, :], j == 0, j == 7, r=[WK, Hc.s((j, 0))], w=[KP])
                    Kr = KRAW.next()
                    P.act(Kr[:], KP[:], AF.Identity, r=[KP], w=[Kr])
                    KR = PP.next()
                    P.mm(KR[:], RM[:], Kr[:], True, True, r=[RM, Kr], w=[KR])
                    t1, t2, ko = T1.next(), T2.next(), KRO.next()
                    P.tt(t1[:], Kr[:], Cc[:], MUL, r=[Kr, Cc], w=[t1])
                    P.tt(t2[:], KR[:], Sc[:], MUL, r=[KR, Sc], w=[t2])
                    P.tt(ko[:], t1[:], t2[:], ADD, r=[t1, t2], w=[ko])
                    P.store(Kscr[n, :, c * CH:(c + 1) * CH], ko[:], r=[ko], q=STQ)
                for tb in range(4 if "v" in A_PARTS else 0):
                    Vs = VSB.next()
                    for cg in range(2):
                        VP = PP.next()
                        for j in range(8):
                            P.mm(VP[:], Hc[:, j, tb * 128:(tb + 1) * 128], WV[:, j, cg * 512:(cg + 1) * 512],
                                 j == 0, j == 7, r=[Hc.s((j, 0)), WV], w=[VP])
                        if cg == 0:
                            P.act(Vs[:, 0:512], VP[:], AF.Identity, r=[VP], w=[Vs.s(0)])
                        else:
                            P.copy(Vs[:, 512:1024], VP[:], r=[VP], w=[Vs.s(1)])
                    r0 = (c * 4 + tb) * 128
                    P.store(Vscr[r0:r0 + 128, :], Vs[:], r=[Vs], q=STQ)
            P.barrier()
            P.emit()

        if "B" in phases:
          with ExitStack() as cb:
            WUA = P.sb(cb, "WUA", [128, 8, 1024], BF16)
            WUB = P.sb(cb, "WUB", [128, 8, 1024], BF16)
            WQ = P.sb(cb, "WQ", [128, 8, 1024], BF16)
            WG = P.sb(cb, "WG", [128, 8, 2048], BF16)
            WCO = P.sb(cb, "WCO", [128, 8, 1024], BF16)
            load_w(WUA, w_in, 1024, 0)
            load_w(WUB, w_in, 1024, 1024)
            load_w(WQ, w_in, 1024, 2048)
            load_w(WG, w_in, 2048, 5120)
            load_w(WCO, w_co, 1024, 0)
            X = P.sb(cb, "XB", [128, 8, XW], F32)
            H = P.sb(cb, "HB", [128, 8, XW], BF16)
            U = P.sb(cb, "UB", [128, 8, XW], BF16)
            DGp = Pool([P.sb(cb, f"DG{i}", [128, 128], BF16) for i in range(8)])
            CV = P.sb(cb, "CVB", [128, 8, CH], F32)
            S = P.sb(cb, "SB", [128, 8, XW], BF16)
            SQ = S
            SR = P.sb(cb, "SRB", [128, CH], F32)
            RS = P.sb(cb, "RSB", [128, CH], F32)
            MEAN = P.sb(cb, "MEANB", [128, CH], F32)
            MSQ = SR
            VAR = RS
            NBt = P.sb(cb, "NBB", [128, CH], F32)
            TMPs = Pool([P.sb(cb, f"TMB{i}", [128, CH], F32) for i in range(4)])
            TMP2 = TMPs
            SG = Pool([P.sb(cb, f"SGB{i}", [128, CH], F32) for i in range(2)])
            CB16 = Pool([P.sb(cb, f"CB16{i}", [128, CH], BF16) for i in range(2)])
            CQ16 = Pool([P.sb(cb, f"CQ16{i}", [128, CH], BF16) for i in range(2)])
            OB16 = Pool([P.sb(cb, f"OB16{i}", [128, CH], BF16) for i in range(3)])
            COS = [P.sb(cb, f"COSB{i}", [128, CH], F32) for i in range(2)]
            SIN = [P.sb(cb, f"SINB{i}", [128, CH], F32) for i in range(2)]
            SSb = P.ps(cb, "SSbB", [128, CH])
            ACC2 = P.ps(cb, "ACC2B", [128, CH])
            PP = Pool([P.ps(cb, f"PPB{i}", [128, CH]) for i in range(6)])
            CW = lambda n, j: SM[:, SM_CW + n * 31 + j:SM_CW + n * 31 + j + 1]

            def loads_b_tabs(i):
                P.load(COS[i % 2][:], cos_own[:, i * CH:(i + 1) * CH], w=[COS[i % 2]])
                P.load(SIN[i % 2][:], sin_own[:, i * CH:(i + 1) * CH], w=[SIN[i % 2]])

            P.load(X[:], fm(xT_own[:, 0:XW]), w=[X])
            loads_b_tabs(0)
            for i in range(NSL):
                if i + 1 < NSL:
                    loads_b_tabs(i + 1)
                Cc, Sc = COS[i % 2], SIN[i % 2]
                norm_H(X, H, HALO, XW, A1, SH1, SQ, SSb, SR, RS, TMPs)
                norm_H(X, H, 0, HALO, A1, SH1, SQ, SSb, SR, RS, TMPs)
                if i + 1 < NSL:
                    P.load(X[:], fm(xT_own[:, (i + 1) * XW:(i + 2) * XW]), w=[X])
                for n in range(8):
                    for (t0, t1) in ((HALO, XW), (0, HALO)):
                        nn = t1 - t0
                        PA, PB = PP.next(), PP.next()
                        for j in range(8):
                            P.mm(PA[:, 0:nn], WUA[:, j, n * 128:(n + 1) * 128], H[:, j, t0:t1], j == 0, j == 7,
                                 r=[WUA, H.s((j, t0))], w=[PA])
                        for j in range(8):
                            P.mm(PB[:, 0:nn], WUB[:, j, n * 128:(n + 1) * 128], H[:, j, t0:t1], j == 0, j == 7,
                                 r=[WUB, H.s((j, t0))], w=[PB])
                        sg = SG.next()
                        P.act(sg[:, 0:nn], PB[:, 0:nn], AF.Sigmoid, r=[PB], w=[sg])
                        if t0 == 0:
                            P.stt(U[:, n, t0:t1], PA[:, 0:nn], SM[:, SM_FLAG + i:SM_FLAG + i + 1], sg[:, 0:nn], MUL, MUL,
                                  r=[PA, sg, SM], w=[U.s((n, t0))])
                        else:
                            P.tt(U[:, n, t0:t1], PA[:, 0:nn], sg[:, 0:nn], MUL, r=[PA, sg], w=[U.s((n, t0))])
                for n in range(8):
                    CA = PP.next()
                    for j in range(31):
                        dg = DGp.next()
                        if j % 2 == 0:
                            P.act(dg[:], IDENT[:], AF.Identity, r=[IDENT, SM], w=[dg], scale=CW(n, j))
                        else:
                            P.ts(dg[:], IDENT[:], CW(n, j), None, MUL, None, r=[IDENT, SM], w=[dg])
                        P.mm(CA[:], dg[:], U[:, n, 2 + j:2 + j + CH], j == 0, j == 30,
                             r=[dg, U.s((n, 0)), U.s((n, HALO))], w=[CA])
                    P.act(CV[:, n, :], CA[:], AF.Identity, r=[CA, SM], w=[CV.s(n)],
                          bias=SM[:, SM_CB + n:SM_CB + n + 1], scale=1.0)
                for n in range(8):
                    cb16, cq16 = CB16.next(), CQ16.next()
                    P.act(cb16[:], CV[:, n, :], AF.Identity, r=[CV.s(n)], w=[cb16])
                    P.act(cq16[:], CV[:, n, :], AF.Square, r=[CV.s(n)], w=[cq16])
                    P.mm(SSb[:], ONESB[:], cb16[:], n == 0, n == 7, r=[ONESB, cb16], w=[SSb])
                    P.mm(ACC2[:], ONESB[:], cq16[:], n == 0, n == 7, r=[ONESB, cq16], w=[ACC2])
                P.ts(MEAN[:], SSb[:], 1.0 / D, None, MUL, None, r=[SSb], w=[MEAN])
                P.tt(MSQ[:], MEAN[:], MEAN[:], MUL, r=[MEAN], w=[MSQ])
                P.stt(VAR[:], ACC2[:], 1.0 / D, MSQ[:], MUL, SUB, r=[ACC2, MSQ], w=[VAR])
                P.act(SR[:], VAR[:], AF.Sqrt, r=[VAR], w=[SR], bias=EPS, scale=1.0)
                P.recip(RS[:], SR[:], r=[SR], w=[RS])
                P.stt(NBt[:], MEAN[:], -1.0, RS[:], MUL, MUL, r=[MEAN, RS], w=[NBt])
                for n in range(8):
                    ta, tb_ = TMPs.next(), TMP2.next()
                    P.tt(ta[:], CV[:, n, :], RS[:], MUL, r=[CV.s(n), RS], w=[ta])
                    P.tt(tb_[:], ta[:], NBt[:], ADD, r=[ta, NBt], w=[tb_])
                    P.act(S[:, n, 0:CH], tb_[:], AF.Silu, r=[tb_, SM], w=[S.s(n)],
                          bias=SM[:, SM_LNB + n:SM_LNB + n + 1], scale=SM[:, SM_LNG + n:SM_LNG + n + 1])
                for n in range(8):
                    PY, PG = PP.next(), PP.next()
                    for j in range(8):
                        P.mm(PY[:], WCO[:, j, n * 128:(n + 1) * 128], S[:, j, 0:CH], j == 0, j == 7, r=[WCO, S.s(j)], w=[PY])
                    for j in range(8):
                        P.mm(PG[:], WG[:, j, n * 128:(n + 1) * 128], H[:, j, HALO:XW], j == 0, j == 7, r=[WG, H.s((j, HALO))], w=[PG])
                    sg = SG.next()
                    P.act(sg[:], PG[:], AF.Sigmoid, r=[PG], w=[sg])
                    ob = OB16.next()
                    P.stt(ob[:], PY[:], SM[:, SM_BOUT + n:SM_BOUT + n + 1], sg[:], ADD, MUL, r=[PY, sg, SM], w=[ob])
                    P.store(ZCscr[n * 128:(n + 1) * 128, i * CH:(i + 1) * CH], ob[:], r=[ob], q=STQ)
                for n in range(8):
                    PG = PP.next()
                    for j in range(8):
                        P.mm(PG[:], WG[:, j, 1024 + n * 128:1024 + (n + 1) * 128], H[:, j, HALO:XW], j == 0, j == 7,
                             r=[WG, H.s((j, HALO))], w=[PG])
                    ob = OB16.next()
                    P.act(ob[:], PG[:], AF.Sigmoid, r=[PG], w=[ob])
                    P.store(GAscr[n * 128:(n + 1) * 128, i * CH:(i + 1) * CH], ob[:], r=[ob], q=STQ)
                for n in range(8):
                    PQ = PP.next()
                    for j in range(8):
                        P.mm(PQ[:], WQ[:, j, n * 128:(n + 1) * 128], H[:, j, HALO:XW], j == 0, j == 7, r=[WQ, H.s((j, HALO))], w=[PQ])
                    qr = CB16.next()
                    P.act(qr[:], PQ[:], AF.Identity, r=[PQ], w=[qr])
                    PR = PP.next()
                    P.mm(PR[:], RM[:], qr[:], True, True, r=[RM, qr], w=[PR])
                    t1, t2, ob = TMPs.next(), TMP2.next(), OB16.next()
                    P.tt(t1[:], qr[:], Cc[:], MUL, r=[qr, Cc], w=[t1])
                    P.tt(t2[:], PR[:], Sc[:], MUL, r=[PR, Sc], w=[t2])
                    P.tt(ob[:], t1[:], t2[:], ADD, r=[t1, t2], w=[ob])
                    P.store(Qscr[n, :, i * CH:(i + 1) * CH], ob[:], r=[ob], q=STQ)
            P.barrier()
            P.emit()

        if "D" in phases:
          with ExitStack() as cd:
            KH = [P.sb(cd, f"KH{i}", [128, T], BF16) for i in range(2)]
            VH = [P.sb(cd, f"VH{i}", [128, 64, 128], BF16) for i in range(2)]
            QH = [P.sb(cd, f"QH{i}", [128, NSL * CH], BF16) for i in range(2)]
            GAH = [P.sb(cd, f"GAH{i}", [128, NSL * CH], BF16) for i in range(2)]
            ZCH = [P.sb(cd, f"ZCH{i}", [128, NSL * CH], BF16) for i in range(2)]
            MK = P.sb(cd, "MK", [128, 16, CH], BF16)
            ONESF = P.sb(cd, "ONESF", [128, 128], F32)
            PT = Pool([P.sb(cd, f"PT{i}", [128, 2 * CH], BF16) for i in range(6)])
            ACC = [P.sb(cd, f"ACCD{i}", [128, 2 * CH], F32) for i in range(2)]
            R1 = P.sb(cd, "R1", [128, CH], F32)
            R2 = P.sb(cd, "R2", [128, CH], F32)
            O1S = Pool([P.sb(cd, f"O1S{i}", [128, CH], F32) for i in range(2)])
            O2S = Pool([P.sb(cd, f"O2S{i}", [128, CH], F32) for i in range(2)])
            AT1 = P.sb(cd, "AT1", [128, CH], F32)
            AT2 = P.sb(cd, "AT2", [128, CH], F32)
            ATT = Pool([P.sb(cd, f"ATT{i}", [128, CH], F32) for i in range(2)])
            SQ16 = Pool([P.sb(cd, f"SQ16{i}", [128, CH], BF16) for i in range(2)])
            LNd = P.sb(cd, "LNd", [128, CH], F32)
            RSd = P.sb(cd, "RSd", [128, CH], F32)
            YA = P.sb(cd, "YA", [128, CH], F32)
            ZA = P.sb(cd, "ZA", [128, CH], F32)
            ZO = Pool([P.sb(cd, f"ZO{i}", [128, CH], BF16) for i in range(2)])
            SP = Pool([P.ps(cd, f"SP{i}", [128, 2 * CH]) for i in range(2)])
            O1 = P.ps(cd, "O1", [128, CH])
            O2 = P.ps(cd, "O2", [128, CH])
            L1 = P.ps(cd, "L1", [128, CH])
            L2 = P.ps(cd, "L2", [128, CH])
            P.memset(ONESF[:], 1.0, w=[ONESF])
            for mi in range(16):
                P.load(MK[:, mi, :], masks[:, mi * CH:(mi + 1) * CH], w=[MK], q="gpsimd")

            def loads_d(h):
                b = h % 2
                P.load(KH[b][:], Kscr[h], w=[KH[b]])
                vsrc = Vscr[:, h * 128:(h + 1) * 128].rearrange("(kb p) e -> p kb e", p=128)
                for q4 in range(4):
                    P.load(VH[b][:, q4 * 16:(q4 + 1) * 16, :], vsrc[:, q4 * 16:(q4 + 1) * 16, :], w=[VH[b]])
                P.load(QH[b][:], Qscr[h], w=[QH[b]])
                P.load(GAH[b][:], GAscr[h * 128:(h + 1) * 128, :], w=[GAH[b]])
                P.load(ZCH[b][:], ZCscr[h * 128:(h + 1) * 128, :], w=[ZCH[b]])

            pending = []

            def run_pending(kb):
                while pending and (kb is None or pending[0][0] <= kb):
                    pending.pop(0)[1]()

            def make_finalize(h, i, Gh, Zh, qs):
                acc, o1s, o2s = ACC[i % 2], O1S.next(), O2S.next()
                att, sq = ATT.next(), SQ16.next()

                def st1():
                    P.copy(o1s[:], O1[:], r=[O1], w=[o1s])
                    P.copy(o2s[:], O2[:], r=[O2], w=[o2s])

                def st2():
                    P.mm(L1[:], ONESF[:], acc[:, 0:CH], False, True, r=[ONESF, acc], w=[L1])
                    P.mm(L2[:], ONESF[:], acc[:, CH:2 * CH], False, True, r=[ONESF, acc], w=[L2])

                def st3():
                    P.recip(R1[:], L1[:], r=[L1], w=[R1])
                    P.recip(R2[:], L2[:], r=[L2], w=[R2])
                    P.tt(AT1[:], o1s[:], R1[:], MUL, r=[o1s, R1], w=[AT1])
                    P.tt(AT2[:], o2s[:], R2[:], MUL, r=[o2s, R2], w=[AT2])
                    P.stt(att[:], AT2[:], NEGLAM, AT1[:], MUL, ADD, r=[AT2, AT1, DV], w=[att])

                def st6():
                    P.act(sq[:], att[:], AF.Square, r=[att], w=[sq])

                ms = []

                def st7():
                    MS = SP.next()
                    ms.append(MS)
                    P.mm(MS[:, 0:CH], ONESB[:], sq[:], True, True, r=[ONESB, sq], w=[MS])

                def st8():
                    P.act(LNd[:], ms[0][:, 0:CH], AF.Ln, r=[ms[0]], w=[LNd], bias=EPS, scale=1.0 / 128)
                    P.act(RSd[:], LNd[:], AF.Exp, r=[LNd], w=[RSd], scale=-0.5)

                def st9():
                    P.tt(YA[:], att[:], RSd[:], MUL, r=[att, RSd], w=[YA])
                    P.stt(ZA[:], YA[:], HNV, Gh[:, qs], MUL, MUL, r=[YA, DV, Gh], w=[ZA])
                    zo = ZO.next()
                    P.tt(zo[:], ZA[:], Zh[:, qs], ADD, r=[ZA, Zh], w=[zo])
                    P.store(Zscr[h * 128:(h + 1) * 128, qs], zo[:], r=[zo], q=STQ)

                st1()
                pending.extend([(0, st2), (1, st3), (3, st6), (4, st7), (5, st8), (6, st9)])

            loads_d(0)
            for h in range(8):
                if h + 1 < 8:
                    loads_d(h + 1)
                b = h % 2
                Kh, Vh, Qh, Gh, Zh = KH[b], VH[b], QH[b], GAH[b], ZCH[b]
                for i in range(NSL):
                    nk = 8 * (i + 1)
                    qs = slice(i * CH, (i + 1) * CH)

                    def smm(kb):
                        Sx = SP.next()
                        ks = slice(kb * 128, (kb + 1) * 128)
                        P.mm(Sx[:, 0:CH], Kh[0:64, ks], Qh[0:64, qs], True, True, r=[Kh, Qh], w=[Sx.s(0)])
                        P.mm(Sx[:, CH:2 * CH], Kh[64:128, ks], Qh[64:128, qs], True, True, r=[Kh, Qh], w=[Sx.s(1)])
                        return Sx

                    acc = ACC[i % 2]

                    def av(kb, pt):
                        st, sp_ = (kb == 0), (kb == nk - 1)
                        P.mm(O1[:], Vh[:, kb, :], pt[:, 0:CH], st, sp_, r=[Vh, pt.s(0)], w=[O1])
                        P.mm(O2[:], Vh[:, kb, :], pt[:, CH:2 * CH], st, sp_, r=[Vh, pt.s(1)], w=[O2])
                        if kb % 3 == 2:
                            P.mm(L1[:], ONESB[:], pt[:, 0:CH], kb == 2, False, r=[ONESB, pt.s(0)], w=[L1])
                            P.mm(L2[:], ONESB[:], pt[:, CH:2 * CH], kb == 2, False, r=[ONESB, pt.s(1)], w=[L2])

                    Snext = smm(0)
                    prev = None
                    for kb in range(nk):
                        Sx = Snext
                        pt = PT.next()
                        P.act(pt[:], Sx[:], AF.Exp, r=[Sx], w=[pt], scale=0.125)
                        if kb >= nk - 8:
                            jj = kb - (nk - 8)
                            midx = (i % 2) * 8 + jj
                            P.tt(pt[:, 0:CH], pt[:, 0:CH], MK[:, midx, :], MUL, r=[pt.s(0), MK], w=[pt.s(0)])
                            P.tt(pt[:, CH:2 * CH], pt[:, CH:2 * CH], MK[:, midx, :], MUL, r=[pt.s(1), MK], w=[pt.s(1)])
                        if kb % 3 != 2:
                            if kb == 0:
                                P.copy(acc[:], pt[:], r=[pt], w=[acc])
                            else:
                                P.tt(acc[:], acc[:], pt[:], ADD, r=[acc, pt], w=[acc])
                        if kb + 1 < nk:
                            Snext = smm(kb + 1)
                        if prev is not None:
                            av(*prev)
                        prev = (kb, pt)
                        run_pending(kb)
                    av(*prev)
                    make_finalize(h, i, Gh, Zh, qs)
                run_pending(None)
            P.barrier()
            P.emit()

        if "C" in phases:
          with ExitStack() as cc:
            SC = 256
            WO = P.sb(cc, "WO", [128, 8, 1024], BF16)
            WF1 = P.sb(cc, "WF1", [128, 8, 4096], BF16)
            WF2 = P.sb(cc, "WF2", [128, 32, 1024], BF16)
            load_w(WO, w_o, 1024, 0)
            load_w(WF1, w_ff1, 4096, 0)
            for q4 in range(4):
                for cg in range(2):
                    P.load(WF2[:, q4 * 8:(q4 + 1) * 8, cg * 512:(cg + 1) * 512],
                           fm(w_ff2[q4 * 1024:(q4 + 1) * 1024, cg * 512:(cg + 1) * 512]), w=[WF2], q="gpsimd")
            Z = [P.sb(cc, f"ZC{i}", [128, 8, SC], BF16) for i in range(2)]
            X = P.sb(cc, "XC", [128, 8, SC], F32)
            Y = P.sb(cc, "YC", [128, 8, SC], F32)
            H2 = P.sb(cc, "H2C", [128, 8, SC], BF16)
            F1 = P.sb(cc, "F1C", [128, 32, SC], BF16)
            SQ = P.sb(cc, "SQC", [128, 8, SC], BF16)
            SR = P.sb(cc, "SRC", [128, SC], F32)
            RS = P.sb(cc, "RSC", [128, SC], F32)
            TMPs = Pool([P.sb(cc, f"TMC{i}", [128, SC], F32) for i in range(3)])
            RL = Pool([P.sb(cc, f"RLC{i}", [128, SC], F32) for i in range(3)])
            Q16 = Pool([P.sb(cc, f"Q16C{i}", [128, SC], BF16) for i in range(2)])
            ACC = P.ps(cc, "ACCC", [128, CH])
            SSb = P.ps(cc, "SSbC", [128, CH])
            PP = Pool([P.ps(cc, f"PPC{i}", [128, CH]) for i in range(6)])
            NSC = NSL * CH // SC

            def xcols(s):
                slot, hf = divmod(s, CH // SC)
                o = slot * XW + HALO + hf * SC
                return slice(o, o + SC)

            def load_z(s):
                P.load(Z[s % 2][:], fm(Zscr[:, s * SC:(s + 1) * SC]), w=[Z[s % 2]])

            load_z(0)
            P.load(X[:], fm(xT_own[:, xcols(0)]), w=[X])
            for s in range(NSC):
                if s + 1 < NSC:
                    load_z(s + 1)
                Zs = Z[s % 2]
                for n in range(8):
                    PY = PP.next()
                    for j in range(8):
                        P.mm(PY[:, 0:SC], WO[:, j, n * 128:(n + 1) * 128], Zs[:, j, :], j == 0, j == 7, r=[WO, Zs], w=[PY])
                    P.act(Y[:, n, :], PY[:, 0:SC], AF.Identity, r=[PY], w=[Y.s(n)])
                    q16 = Q16.next()
                    P.act(q16[:], PY[:, 0:SC], AF.Square, r=[PY], w=[q16])
                    P.mm(ACC[:, 0:SC], ONESB[:], q16[:], n == 0, n == 7, r=[ONESB, q16], w=[ACC])
                P.act(SR[:], ACC[:, 0:SC], AF.Sqrt, r=[ACC], w=[SR], bias=EPS, scale=1.0 / D)
                P.recip(RS[:], SR[:], r=[SR], w=[RS])
                for n in range(8):
                    tm = TMPs.next()
                    P.stt(tm[:], Y[:, n, :], GP1(n), RS[:], MUL, MUL, r=[Y.s(n), DV, RS], w=[tm])
                    P.tt(X[:, n, :], X[:, n, :], tm[:], ADD, r=[X.s(n), tm], w=[X.s(n)])
                P.act(SQ[:], X[:], AF.Square, r=[X], w=[SQ])
                for j in range(8):
                    P.mm(SSb[:, 0:SC], ONESB[:], SQ[:, j, :], j == 0, j == 7, r=[ONESB, SQ], w=[SSb])
                P.act(SR[:], SSb[:, 0:SC], AF.Sqrt, r=[SSb], w=[SR], bias=EPS, scale=1.0 / D)
                P.recip(RS[:], SR[:], r=[SR], w=[RS])
                for j in range(8):
                    tm = TMPs.next()
                    P.stt(tm[:], X[:, j, :], A2(j), RS[:], MUL, MUL, r=[X.s(j), DV, RS], w=[tm])
                    P.act(H2[:, j, :], tm[:], AF.Identity, r=[tm, MODV], w=[H2.s(j)], bias=SH2(j), scale=1.0)
                for m in range(32):
                    PF = PP.next()
                    for j in range(8):
                        P.mm(PF[:, 0:SC], WF1[:, j, m * 128:(m + 1) * 128], H2[:, j, :], j == 0, j == 7, r=[WF1, H2.s(j)], w=[PF])
                    rl = RL.next()
                    P.ts(rl[:], PF[:, 0:SC], 0.0, None, MAX, None, r=[PF], w=[rl])
                    P.act(F1[:, m, :], rl[:], AF.Square, r=[rl], w=[F1.s(m)])
                for n in range(8):
                    PO = PP.next()
                    for m in range(32):
                        P.mm(PO[:, 0:SC], WF2[:, m, n * 128:(n + 1) * 128], F1[:, m, :], m == 0, m == 31, r=[WF2, F1.s(m)], w=[PO])
                    P.act(Y[:, n, :], PO[:, 0:SC], AF.Identity, r=[PO], w=[Y.s(n)])
                    q16 = Q16.next()
                    P.act(q16[:], PO[:, 0:SC], AF.Square, r=[PO], w=[q16])
                    P.mm(ACC[:, 0:SC], ONESB[:], q16[:], n == 0, n == 7, r=[ONESB, q16], w=[ACC])
                P.act(SR[:], ACC[:, 0:SC], AF.Sqrt, r=[ACC], w=[SR], bias=EPS, scale=1.0 / D)
                P.recip(RS[:], SR[:], r=[SR], w=[RS])
                for n in range(8):
                    tm = TMPs.next()
                    P.stt(tm[:], Y[:, n, :], GP2(n), RS[:], MUL, MUL, r=[Y.s(n), DV, RS], w=[tm])
                    P.tt(X[:, n, :], X[:, n, :], tm[:], ADD, r=[X.s(n), tm], w=[X.s(n)])
                P.store(fm(outT[:, s * SC:(s + 1) * SC]), X[:], r=[X], q=STQ)
                if s + 1 < NSC:
                    P.load(X[:], fm(xT_own[:, xcols(s + 1)]), w=[X])
            P.barrier()
            P.emit()
    return nc


_DBG = None
STQ = "sync"
import os
NCH_RUN = int(os.environ.get("NCH_RUN", "16"))
A_PARTS = os.environ.get("A_PARTS", "nkv")
DBG_OUT = os.environ.get("DBG_OUT", "1") == "1"


def _fmv(v, n):
    return np.ascontiguousarray(np.asarray(v, np.float32).reshape(n, 128).T)


def _host_constants():
    inv_freq = (np.float32(10000.0) ** (-np.arange(0, 64, 2, dtype=np.float32) / np.float32(64))).astype(np.float32)
    pos = np.arange(T, dtype=np.float32)
    ang = (pos[:, None] * inv_freq[None, :]).astype(np.float32)
    cos = np.cos(ang).astype(np.float32).T
    sin = np.sin(ang).astype(np.float32).T
    cos_all = np.ascontiguousarray(np.tile(cos, (4, 1)))
    sin_all = np.ascontiguousarray(np.tile(sin, (4, 1)))
    rm = np.zeros((128, 128), np.float32)
    for blk in (0, 64):
        for d in range(32):
            rm[blk + d + 32, blk + d] = -1.0
            rm[blk + d, blk + d + 32] = 1.0
    k = np.arange(128)[:, None]
    q = np.arange(CH)[None, :]
    diag = [((j * 128 + k) <= q).astype(np.float32) for j in range(4)]
    ones = [np.ones((128, CH), np.float32)] * 4
    zeros = [np.zeros((128, CH), np.float32)] * 4
    return cos_all, sin_all, rm, diag, ones, zeros


def kernel(x, c, w_ada, b_ada, pre_norm1, post_norm1, w_in, conv_w, conv_b, conv_ln_g, conv_ln_b,
           conv_w_out, conv_b_out, lambda_q1, lambda_k1, lambda_q2, lambda_k2, head_norm, w_o,
           pre_norm2, post_norm2, w_ff1, w_ff2):
    x = np.asarray(x, np.float32)
    c = np.asarray(c, np.float32)
    cos_all, sin_all, rm, diag, ones, zeros = _host_constants()
    w_in0 = np.asarray(w_in, np.float32)[0]
    perm = np.arange(7168)
    for base in (2048, 3072):
        for h in range(8):
            for m in range(2):
                perm[base + h * 128 + m * 64: base + h * 128 + m * 64 + 64] = base + m * 512 + h * 64 + np.arange(64)
    w_in_p = np.ascontiguousarray(w_in0[:, perm])
    w_ada0 = np.ascontiguousarray(np.asarray(w_ada, np.float32)[0])
    w_co0 = np.ascontiguousarray(np.asarray(conv_w_out, np.float32)[0])
    w_o0 = np.ascontiguousarray(np.asarray(w_o, np.float32)[0])
    w_ff10 = np.ascontiguousarray(np.asarray(w_ff1, np.float32)[0])
    w_ff20 = np.ascontiguousarray(np.asarray(w_ff2, np.float32)[0])

    in_maps = []
    for core in range(8):
        b, jh = divmod(core, 2)
        own = OWN[jh]
        xT = np.ascontiguousarray(x[b].T)
        xo = np.zeros((D, NSL, XW), np.float32)
        flag = np.ones((128, NSL), np.float32)
        for i, ch in enumerate(own):
            lo = ch * CH - HALO
            if lo < 0:
                xo[:, i, HALO:] = xT[:, 0:CH]
                flag[:, i] = 0.0
            else:
                xo[:, i, :] = xT[:, lo:lo + XW]
        small = np.zeros((128, NS), np.float32)
        small[:, SM_BADA:SM_BADA + 48] = _fmv(b_ada[0], 48)
        small[:, SM_PN1:SM_PN1 + 8] = _fmv(pre_norm1[0], 8)
        small[:, SM_PO1:SM_PO1 + 8] = _fmv(post_norm1[0], 8)
        small[:, SM_PN2:SM_PN2 + 8] = _fmv(pre_norm2[0], 8)
        small[:, SM_PO2:SM_PO2 + 8] = _fmv(post_norm2[0], 8)
        small[:, SM_CB:SM_CB + 8] = _fmv(conv_b[0], 8)
        small[:, SM_LNG:SM_LNG + 8] = _fmv(conv_ln_g[0], 8)
        small[:, SM_LNB:SM_LNB + 8] = _fmv(conv_ln_b[0], 8)
        small[:, SM_BOUT:SM_BOUT + 8] = _fmv(conv_b_out[0], 8)
        small[:, SM_HN] = np.asarray(head_norm, np.float32)[0]
        small[:, SM_FLAG:SM_FLAG + 8] = flag
        small[:, SM_CT:SM_CT + 16] = np.repeat(_fmv(c[b], 8), 2, axis=1)
        for k, v in enumerate((lambda_q1, lambda_k1, lambda_q2, lambda_k2)):
            small[:, SM_LAM + 64 * k:SM_LAM + 64 * (k + 1)] = np.asarray(v, np.float32)[0][None, :]
        cw = np.asarray(conv_w, np.float32)[0]
        small[:, SM_CW:SM_CW + 248] = cw.T.reshape(8, 128, 31).transpose(1, 0, 2).reshape(128, 248)
        idx = np.concatenate([np.arange(ch * CH, (ch + 1) * CH) for ch in own])
        mk = []
        for par in range(2):
            own_is_2i = (par == 0) if jh == 0 else (par == 1)
            mk += (diag + zeros) if own_is_2i else (ones + diag)
        in_maps.append({
            "xT_all": xT, "xT_own": xo.reshape(D, NSL * XW), "w_ada": w_ada0, "w_in": w_in_p, "w_co": w_co0,
            "w_o": w_o0, "w_ff1": w_ff10, "w_ff2": w_ff20, "small": small,
            "cos_all": cos_all, "sin_all": sin_all,
            "cos_own": np.ascontiguousarray(cos_all[:, idx]), "sin_own": np.ascontiguousarray(sin_all[:, idx]),
            "masks": np.ascontiguousarray(np.concatenate(mk, axis=1)), "rmat": rm,
            "ident": np.eye(128, dtype=np.float32),
        })
    if _DBG is not None:
        nc = build_program(_DBG, dbg=DBG_OUT)
        res = run_bass_kernel_spmd(nc, in_maps, core_ids=list(range(8)))
        return res, in_maps
    nc = build_program()
    res = run_bass_kernel_spmd(nc, in_maps, core_ids=list(range(8)))
    out = np.empty((NB, T, D), np.float32)
    for core in range(8):
        b, jh = divmod(core, 2)
        oT = np.asarray(res.results[core]["outT"], np.float32)
        for i, ch in enumerate(OWN[jh]):
            out[b, ch * CH:(ch + 1) * CH, :] = oT[:, i * CH:(i + 1) * CH].T
    return out
```

```python
import numpy as np
import concourse.bass as bass
import concourse.mybir as mybir

F32 = mybir.dt.float32
BF16 = mybir.dt.bfloat16
AF = mybir.ActivationFunctionType
ALU = mybir.AluOpType

ENGS = ("tensor", "vector", "scalar", "gpsimd", "sync")
N_DMA_SEMS = 32
SAME_ENGINE_SYNC = True


class Res:
    __slots__ = ("t", "last_w", "readers", "name", "parent", "kids")

    def __init__(self, t, name, parent=None):
        self.t = t
        self.last_w = None
        self.readers = []
        self.name = name
        self.parent = parent
        self.kids = {}

    def s(self, k):
        if k not in self.kids:
            self.kids[k] = Res(self.t, f"{self.name}.{k}", parent=self)
        return self.kids[k]

    def holders(self):
        h = [self]
        if self.parent is not None:
            h.append(self.parent)
        h.extend(self.kids.values())
        return h

    def __getitem__(self, idx):
        return self.t[idx]


class Op:
    __slots__ = ("eng", "fn", "deps", "signal", "count", "is_dma", "sem", "semval", "barrier", "emitted")

    def __init__(self, eng, fn, is_dma=False):
        self.eng = eng
        self.fn = fn
        self.deps = []
        self.signal = False
        self.count = None
        self.is_dma = is_dma
        self.sem = None
        self.semval = None
        self.barrier = False
        self.emitted = False


class Prog:
    def __init__(self, nc):
        self.nc = nc
        self.ops = {e: [] for e in ENGS}
        self.esem = {e: nc.alloc_semaphore(name=f"s_{e}") for e in ENGS}
        self.ecount = {e: 0 for e in ENGS}
        self.waited = {e: {} for e in ENGS}
        self.nd = 2 * N_DMA_SEMS
        self.dsems = [nc.alloc_semaphore(name=f"d_{i}") for i in range(self.nd)]
        self.dsem_uses = [0] * self.nd
        self.dsem_last = [None] * self.nd
        self.dma_rr = {"sync": 0, "gpsimd": 0}
        self.last_op = {e: None for e in ENGS}
        self.n_ops = 0

    def sb(self, ctx, name, shape, dtype):
        t = ctx.enter_context(self.nc.sbuf_tensor(name, list(shape), dtype))
        return Res(t, name)

    def ps(self, ctx, name, shape, dtype=F32):
        t = ctx.enter_context(self.nc.psum_tensor(name, list(shape), dtype))
        return Res(t, name)

    def _deps(self, op, r, w):
        deps = []
        for x0 in r:
            for x in x0.holders():
                if x.last_w is not None:
                    deps.append(x.last_w)
        for x0 in w:
            for x in x0.holders():
                if x.last_w is not None:
                    deps.append(x.last_w)
                deps.extend(x.readers)
        seen = set()
        for d in deps:
            if id(d) in seen or d is op or d.emitted:
                continue
            seen.add(id(d))
            if (not d.is_dma) and (not op.is_dma) and d.eng == op.eng and (d.eng == "tensor" or not SAME_ENGINE_SYNC):
                continue
            op.deps.append(d)
            if not d.is_dma:
                d.signal = True
        for x in r:
            if op.is_dma:
                x.readers.append(op)
            else:
                x.readers = [o for o in x.readers if o.is_dma or o.eng != op.eng]
                x.readers.append(op)
        for x in w:
            x.last_w = op
            x.readers = []

    def add(self, eng, fn, r=(), w=()):
        op = Op(eng, fn)
        self._deps(op, r, w)
        self.ops[eng].append(op)
        self.last_op[eng] = op
        self.n_ops += 1
        return op

    def dma(self, fn, r=(), w=(), q="sync"):
        op = Op(q, fn, is_dma=True)
        k = self.dma_rr[q] % N_DMA_SEMS + (N_DMA_SEMS if q == "gpsimd" else 0)
        self.dma_rr[q] += 1
        prev = self.dsem_last[k]
        self.dsem_uses[k] += 1
        op.sem = k
        op.semval = 16 * self.dsem_uses[k]
        if prev is not None and not prev.emitted:
            op.deps.append(prev)
        self._deps(op, r, w)
        self.dsem_last[k] = op
        self.ops[q].append(op)
        self.n_ops += 1
        return op

    def barrier(self):
        targets = []
        for e in ENGS:
            lo = self.last_op[e]
            if lo is not None:
                lo.signal = True
                targets.append(lo)
        for k in range(self.nd):
            if self.dsem_last[k] is not None:
                targets.append(self.dsem_last[k])
        for e in ENGS:
            op = Op(e, None)
            op.barrier = True
            op.deps = [t for t in targets if t.is_dma or t.eng != e]
            self.ops[e].append(op)

    def emit(self, name=None):
        nc = self.nc
        for e in ENGS:
            c = self.ecount[e]
            for op in self.ops[e]:
                if op.is_dma or op.barrier:
                    continue
                if op.signal:
                    c += 1
                    op.count = c
            self.ecount[e] = c

        def emit_engine(e, eng):
            waited = self.waited[e]
            for op in self.ops[e]:
                for d in op.deps:
                    if d.is_dma:
                        key, val, sem = ("d", d.sem), d.semval, self.dsems[d.sem]
                    else:
                        assert d.count is not None, "dependency on non-signalling op"
                        key, val, sem = ("e", d.eng), d.count, self.esem[d.eng]
                    if waited.get(key, 0) >= val:
                        continue
                    eng.wait_ge(sem, val)
                    waited[key] = val
                if op.barrier:
                    continue
                ins = op.fn(eng)
                if op.is_dma:
                    ins.then_inc(self.dsems[op.sem], 16)
                elif op.signal:
                    ins.then_inc(self.esem[e], 1)

        with nc.Block() as block:
            @block.tensor
            def _(eng):
                emit_engine("tensor", eng)

            @block.vector
            def _(eng):
                emit_engine("vector", eng)

            @block.scalar
            def _(eng):
                emit_engine("scalar", eng)

            @block.gpsimd
            def _(eng):
                emit_engine("gpsimd", eng)

            @block.sync
            def _(eng):
                emit_engine("sync", eng)
        for e in ENGS:
            for op in self.ops[e]:
                op.emitted = True
        self.ops = {e: [] for e in ENGS}

    def mm(self, out, lhsT, rhs, start, stop, r, w, **kw):
        return self.add("tensor", lambda e: e.matmul(out, lhsT, rhs, start=start, stop=stop, **kw), r=r, w=w)

    def act(self, out, in_, func, r, w, bias=None, scale=None, eng="scalar"):
        kw = {}
        if bias is not None:
            kw["bias"] = bias
        if scale is not None:
            kw["scale"] = scale
        return self.add(eng, lambda e: e.activation(out, in_, func, **kw), r=r, w=w)

    def tt(self, out, in0, in1, op, r, w, eng="vector"):
        return self.add(eng, lambda e: e.tensor_tensor(out, in0, in1, op), r=r, w=w)

    def ts(self, out, in0, s1, s2, op0, op1, r, w, eng="vector"):
        if op1 is None:
            return self.add(eng, lambda e: e.tensor_scalar(out, in0, s1, None, op0), r=r, w=w)
        return self.add(eng, lambda e: e.tensor_scalar(out, in0, s1, s2, op0, op1), r=r, w=w)

    def stt(self, out, in0, scalar, in1, op0, op1, r, w):
        return self.add("vector", lambda e: e.scalar_tensor_tensor(out, in0, scalar, in1, op0, op1), r=r, w=w)

    def copy(self, out, in_, r, w, eng="vector"):
        return self.add(eng, lambda e: e.tensor_copy(out, in_), r=r, w=w)

    def recip(self, out, in_, r, w):
        return self.add("vector", lambda e: e.reciprocal(out, in_), r=r, w=w)

    def memset(self, ap, val, w, eng="vector"):
        return self.add(eng, lambda e: e.memset(ap, val), r=(), w=w)

    def load(self, out, in_, w, r=(), q="sync", **kw):
        return self.dma(lambda e: e.dma_start(out=out, in_=in_, **kw), r=r, w=w, q=q)

    def store(self, out, in_, r, w=(), q="sync", **kw):
        return self.dma(lambda e: e.dma_start(out=out, in_=in_, **kw), r=r, w=w, q=q)


from contextlib import ExitStack
from concourse.bass_utils import run_bass_kernel_spmd

D = 1024
T = 8192
NB = 4
NCH = 16
NSL = 8
CH = 512
HALO = 32
XW = CH + HALO
EPS = 1e-6
LAM_INIT = 0.2
OWN = {0: [0, 3, 4, 7, 8, 11, 12, 15], 1: [1, 2, 5, 6, 9, 10, 13, 14]}

SM_BADA = 0
SM_PN1 = 48
SM_PO1 = 56
SM_PN2 = 64
SM_PO2 = 72
SM_CB = 80
SM_LNG = 88
SM_LNB = 96
SM_BOUT = 104
SM_HN = 112
SM_FLAG = 113
SM_CT = 121
SM_LAM = 137
SM_CW = 393
NS = 393 + 248


class Pool:
    def __init__(self, items):
        self.items = items
        self.i = 0

    def next(self):
        x = self.items[self.i % len(self.items)]
        self.i += 1
        return x


def build_program(phases="ABDC", dbg=False):
    nc = bass.Bass("TRN2", target_bir_lowering=False)

    def din(name, shape, dt=F32):
        return nc.dram_tensor(name, list(shape), dt, kind="ExternalInput").ap()

    xT_all = din("xT_all", [D, T])
    xT_own = din("xT_own", [D, NSL * XW])
    w_ada = din("w_ada", [D, 6 * D])
    w_in = din("w_in", [D, 7 * D])
    w_co = din("w_co", [D, D])
    w_o = din("w_o", [D, D])
    w_ff1 = din("w_ff1", [D, 4 * D])
    w_ff2 = din("w_ff2", [4 * D, D])
    small = din("small", [128, NS])
    cos_all = din("cos_all", [128, T])
    sin_all = din("sin_all", [128, T])
    cos_own = din("cos_own", [128, NSL * CH])
    sin_own = din("sin_own", [128, NSL * CH])
    masks = din("masks", [128, 16 * CH])
    rmat = din("rmat", [128, 128])
    ident = din("ident", [128, 128])
    outT = nc.dram_tensor("outT", [D, NSL * CH], F32, kind="ExternalOutput").ap()

    def dscr(name, shape, dt=BF16):
        return nc.dram_tensor(name, list(shape), dt, kind="ExternalOutput" if dbg else "Internal").ap()

    dbg_small = dscr("dbg_small", [128, 88], F32) if dbg else None

    Kscr = dscr("Kscr", [8, 128, T])
    Vscr = dscr("Vscr", [T, D])
    Qscr = dscr("Qscr", [8, 128, NSL * CH])
    ZCscr = dscr("ZCscr", [D, NSL * CH])
    GAscr = dscr("GAscr", [D, NSL * CH])
    Zscr = dscr("Zscr", [D, NSL * CH])

    P = Prog(nc)
    MUL, ADD, SUB, MAX = ALU.mult, ALU.add, ALU.subtract, ALU.max

    def fm(ap):
        return ap.rearrange("(j p) t -> p j t", p=128)

    with ExitStack() as g:
        SM = P.sb(g, "SM", [128, NS], F32)
        MODV = P.sb(g, "MODV", [128, 48], F32)
        DV = P.sb(g, "DV", [128, 40], F32)
        ONESB = P.sb(g, "ONESB", [128, 128], BF16)
        RM = P.sb(g, "RM", [128, 128], BF16)
        IDENT = P.sb(g, "IDENT", [128, 128], BF16)

        def smc(off, n=1):
            return SM[:, off:off + n]

        SH1 = lambda j: MODV[:, 0 + j:1 + j]
        SH2 = lambda j: MODV[:, 24 + j:25 + j]
        A1 = lambda j: DV[:, 0 + j:1 + j]
        A2 = lambda j: DV[:, 8 + j:9 + j]
        GP1 = lambda j: DV[:, 16 + j:17 + j]
        GP2 = lambda j: DV[:, 24 + j:25 + j]
        NEGLAM = DV[:, 32:33]
        HNV = DV[:, 33:34]

        with ExitStack() as c0:
            WA = [P.sb(c0, f"WA{i}", [128, 8, 768], F32) for i in range(2)]
            CS = P.sb(c0, "CS", [128, 16], F32)
            TMPL = P.sb(c0, "TMPL", [128, 64], F32)
            SS = P.sb(c0, "SSl", [128, 4], F32)
            MODP = P.ps(c0, "MODP", [128, 96], F32)
            P.load(SM[:], small, w=[SM])
            P.load(RM[:], rmat, w=[RM], q="gpsimd")
            P.load(IDENT[:], ident, w=[IDENT], q="gpsimd")
            P.memset(ONESB[:], 1.0, w=[ONESB])
            P.act(CS[:], smc(SM_CT, 16), AF.Silu, r=[SM], w=[CS])
            for gi in range(8):
                Wt = WA[gi % 2]
                P.load(Wt[:], fm(w_ada[:, gi * 768:(gi + 1) * 768]), w=[Wt])
                for nn in range(6):
                    n = gi * 6 + nn
                    for j in range(8):
                        P.mm(MODP[:, 2 * n:2 * n + 2], Wt[:, j, nn * 128:(nn + 1) * 128], CS[:, 2 * j:2 * j + 2],
                             j == 0, j == 7, r=[Wt, CS], w=[MODP])
            modp_v = MODP[:].rearrange("p (n two) -> p n two", two=2)[:, :, 0]
            P.tt(MODV[:], modp_v, smc(SM_BADA, 48), ADD, r=[MODP, SM], w=[MODV])
            P.stt(DV[:, 0:8], MODV[:, 8:16], 1.0, smc(SM_PN1, 8), ADD, MUL, r=[MODV, SM], w=[DV])
            P.stt(DV[:, 8:16], MODV[:, 32:40], 1.0, smc(SM_PN2, 8), ADD, MUL, r=[MODV, SM], w=[DV])
            P.tt(DV[:, 16:24], MODV[:, 16:24], smc(SM_PO1, 8), MUL, r=[MODV, SM], w=[DV])
            P.tt(DV[:, 24:32], MODV[:, 40:48], smc(SM_PO2, 8), MUL, r=[MODV, SM], w=[DV])
            for k in range(2):
                P.tt(TMPL[:], smc(SM_LAM + 128 * k, 64), smc(SM_LAM + 128 * k + 64, 64), MUL, r=[SM], w=[TMPL])
                P.add("vector", lambda e, k=k: e.reduce_sum(SS[:, k:k + 1], TMPL[:], mybir.AxisListType.X),
                      r=[TMPL], w=[SS])
            P.act(SS[:, 2:4], SS[:, 0:2], AF.Exp, r=[SS], w=[SS])
            P.stt(DV[:, 32:33], SS[:, 3:4], -LAM_INIT, SS[:, 2:3], ADD, SUB, r=[SS], w=[DV])
            P.ts(DV[:, 33:34], smc(SM_HN, 1), 1.0 - LAM_INIT, None, MUL, None, r=[SM], w=[DV])
            if dbg:
                P.store(dbg_small[:, 0:48], MODV[:], r=[MODV])
                P.store(dbg_small[:, 48:82], DV[:, 0:34], r=[DV])
            P.barrier()
            P.emit()

        def norm_H(X, H, t0, t1, avec, shvec, SQ, SSb, SR, RS, TMPs):
            n = t1 - t0
            P.act(SQ[:, :, t0:t1], X[:, :, t0:t1], AF.Square, r=[X], w=[SQ])
            for j in range(8):
                P.mm(SSb[:, 0:n], ONESB[:], SQ[:, j, t0:t1], j == 0, j == 7, r=[ONESB, SQ], w=[SSb])
            P.act(SR[:, 0:n], SSb[:, 0:n], AF.Sqrt, r=[SSb], w=[SR], bias=EPS, scale=1.0 / D)
            P.recip(RS[:, 0:n], SR[:, 0:n], r=[SR], w=[RS])
            for j in range(8):
                Tm = TMPs.next()
                P.stt(Tm[:, 0:n], X[:, j, t0:t1], avec(j), RS[:, 0:n], MUL, MUL, r=[X, DV, RS], w=[Tm])
                P.act(H[:, j, t0:t1], Tm[:, 0:n], AF.Identity, r=[Tm, MODV], w=[H.s((j, t0))], bias=shvec(j), scale=1.0)

        def load_w(Wt, src, ncols, c0col=0):
            for cc in range(0, ncols, 512):
                P.load(Wt[:, :, cc:cc + 512], fm(src[:, c0col + cc:c0col + cc + 512]), w=[Wt], q="gpsimd")

        if "A" in phases:
          with ExitStack() as ca:
            WK = P.sb(ca, "WK", [128, 8, 1024], BF16)
            WV = P.sb(ca, "WV", [128, 8, 1024], BF16)
            load_w(WK, w_in, 1024, 3072)
            load_w(WV, w_in, 1024, 4096)
            X = [P.sb(ca, f"XA{i}", [128, 8, CH], F32) for i in range(2)]
            H = [P.sb(ca, f"HA{i}", [128, 8, CH], BF16) for i in range(2)]
            SQ = P.sb(ca, "SQA", [128, 8, CH], BF16)
            SR = P.sb(ca, "SRA", [128, CH], F32)
            RS = P.sb(ca, "RSA", [128, CH], F32)
            TMPs = Pool([P.sb(ca, f"TMA{i}", [128, CH], F32) for i in range(3)])
            COS = [P.sb(ca, f"COSA{i}", [128, CH], F32) for i in range(2)]
            SIN = [P.sb(ca, f"SINA{i}", [128, CH], F32) for i in range(2)]
            KRAW = Pool([P.sb(ca, f"KRAW{i}", [128, CH], BF16) for i in range(2)])
            T1 = Pool([P.sb(ca, f"T1A{i}", [128, CH], F32) for i in range(2)])
            T2 = Pool([P.sb(ca, f"T2A{i}", [128, CH], F32) for i in range(2)])
            KRO = Pool([P.sb(ca, f"KRO{i}", [128, CH], BF16) for i in range(3)])
            VSB = Pool([P.sb(ca, f"VSB{i}", [128, 1024], BF16) for i in range(2)])
            SSb = P.ps(ca, "SSbA", [128, CH])
            PP = Pool([P.ps(ca, f"PPA{i}", [128, CH]) for i in range(6)])

            def loads_a(c):
                P.load(X[c % 2][:], fm(xT_all[:, c * CH:(c + 1) * CH]), w=[X[c % 2]])
                P.load(COS[c % 2][:], cos_all[:, c * CH:(c + 1) * CH], w=[COS[c % 2]])
                P.load(SIN[c % 2][:], sin_all[:, c * CH:(c + 1) * CH], w=[SIN[c % 2]])

            loads_a(0)
            for c in range(NCH_RUN):
                if c + 1 < NCH_RUN:
                    loads_a(c + 1)
                Xc, Hc, Cc, Sc = X[c % 2], H[c % 2], COS[c % 2], SIN[c % 2]
                if "n" in A_PARTS:
                    norm_H(Xc, Hc, 0, CH, A1, SH1, SQ, SSb, SR, RS, TMPs)
                for n in range(8 if "k" in A_PARTS else 0):
                    KP = PP.next()
                    for j in range(8):
                        P.mm(KP[:], WK[:, j, n * 128:(n + 1) * 128], Hc[:, j, :], j == 0, j == 7, r=[WK, Hc.s((j, 0))], w=[KP])
                    Kr = KRAW.next()
                    P.act(Kr[:], KP[:], AF.Identity, r=[KP], w=[Kr])
                    KR = PP.next()
                    P.mm(KR[:], RM[:], Kr[:], True, True, r=[RM, Kr], w=[KR])
                    t1, t2, ko = T1.next(), T2.next(), KRO.next()
                    P.tt(t1[:], Kr[:], Cc[:], MUL, r=[Kr, Cc], w=[t1])
                    P.tt(t2[:], KR[:], Sc[:], MUL, r=[KR, Sc], w=[t2])
                    P.tt(ko[:], t1[:], t2[:], ADD, r=[t1, t2], w=[ko])
                    P.store(Kscr[n, :, c * CH:(c + 1) * CH], ko[:], r=[ko], q=STQ)
                for tb in range(4 if "v" in A_PARTS else 0):
                    Vs = VSB.next()
                    for cg in range(2):
                        VP = PP.next()
                        for j in range(8):
                            P.mm(VP[:], Hc[:, j, tb * 128:(tb + 1) * 128], WV[:, j, cg * 512:(cg + 1) * 512],
                                 j == 0, j == 7, r=[Hc.s((j, 0)), WV], w=[VP])
                        if cg == 0:
                            P.act(Vs[:, 0:512], VP[:], AF.Identity, r=[VP], w=[Vs.s(0)])
                        else:
                            P.copy(Vs[:, 512:1024], VP[:], r=[VP], w=[Vs.s(1)])
                    r0 = (c * 4 + tb) * 128
                    P.store(Vscr[r0:r0 + 128, :], Vs[:], r=[Vs], q=STQ)
            P.barrier()
            P.emit()

        if "B" in phases:
          with ExitStack() as cb:
            WUA = P.sb(cb, "WUA", [128, 8, 1024], BF16)
            WUB = P.sb(cb, "WUB", [128, 8, 1024], BF16)
            WQ = P.sb(cb, "WQ", [128, 8, 1024], BF16)
            WG = P.sb(cb, "WG", [128, 8, 2048], BF16)
            WCO = P.sb(cb, "WCO", [128, 8, 1024], BF16)
            load_w(WUA, w_in, 1024, 0)
            load_w(WUB, w_in, 1024, 1024)
            load_w(WQ, w_in, 1024, 2048)
            load_w(WG, w_in, 2048, 5120)
            load_w(WCO, w_co, 1024, 0)
            X = P.sb(cb, "XB", [128, 8, XW], F32)
            H = P.sb(cb, "HB", [128, 8, XW], BF16)
            U = P.sb(cb, "UB", [128, 8, XW], BF16)
            DGp = Pool([P.sb(cb, f"DG{i}", [128, 128], BF16) for i in range(8)])
            CV = P.sb(cb, "CVB", [128, 8, CH], F32)
            S = P.sb(cb, "SB", [128, 8, XW], BF16)
            SQ = S
            SR = P.sb(cb, "SRB", [128, CH], F32)
            RS = P.sb(cb, "RSB", [128, CH], F32)
            MEAN = P.sb(cb, "MEANB", [128, CH], F32)
            MSQ = SR
            VAR = RS
            NBt = P.sb(cb, "NBB", [128, CH], F32)
            TMPs = Pool([P.sb(cb, f"TMB{i}", [128, CH], F32) for i in range(4)])
            TMP2 = TMPs
            SG = Pool([P.sb(cb, f"SGB{i}", [128, CH], F32) for i in range(2)])
            CB16 = Pool([P.sb(cb, f"CB16{i}", [128, CH], BF16) for i in range(2)])
            CQ16 = Pool([P.sb(cb, f"CQ16{i}", [128, CH], BF16) for i in range(2)])
            OB16 = Pool([P.sb(cb, f"OB16{i}", [128, CH], BF16) for i in range(3)])
            COS = [P.sb(cb, f"COSB{i}", [128, CH], F32) for i in range(2)]
            SIN = [P.sb(cb, f"SINB{i}", [128, CH], F32) for i in range(2)]
            SSb = P.ps(cb, "SSbB", [128, CH])
            ACC2 = P.ps(cb, "ACC2B", [128, CH])
            PP = Pool([P.ps(cb, f"PPB{i}", [128, CH]) for i in range(6)])
            CW = lambda n, j: SM[:, SM_CW + n * 31 + j:SM_CW + n * 31 + j + 1]

            def loads_b_tabs(i):
                P.load(COS[i % 2][:], cos_own[:, i * CH:(i + 1) * CH], w=[COS[i % 2]])
                P.load(SIN[i % 2][:], sin_own[:, i * CH:(i + 1) * CH], w=[SIN[i % 2]])

            P.load(X[:], fm(xT_own[:, 0:XW]), w=[X])
            loads_b_tabs(0)
            for i in range(NSL):
                if i + 1 < NSL:
                    loads_b_tabs(i + 1)
                Cc, Sc = COS[i % 2], SIN[i % 2]
                norm_H(X, H, HALO, XW, A1, SH1, SQ, SSb, SR, RS, TMPs)
                norm_H(X, H, 0, HALO, A1, SH1, SQ, SSb, SR, RS, TMPs)
                if i + 1 < NSL:
                    P.load(X[:], fm(xT_own[:, (i + 1) * XW:(i + 2) * XW]), w=[X])
                for n in range(8):
                    for (t0, t1) in ((HALO, XW), (0, HALO)):
                        nn = t1 - t0
                        PA, PB = PP.next(), PP.next()
                        for j in range(8):
                            P.mm(PA[:, 0:nn], WUA[:, j, n * 128:(n + 1) * 128], H[:, j, t0:t1], j == 0, j == 7,
                                 r=[WUA, H.s((j, t0))], w=[PA])
                        for j in range(8):
                            P.mm(PB[:, 0:nn], WUB[:, j, n * 128:(n + 1) * 128], H[:, j, t0:t1], j == 0, j == 7,
                                 r=[WUB, H.s((j, t0))], w=[PB])
                        sg = SG.next()
                        P.act(sg[:, 0:nn], PB[:, 0:nn], AF.Sigmoid, r=[PB], w=[sg])
                        if t0 == 0:
                            P.stt(U[:, n, t0:t1], PA[:, 0:nn], SM[:, SM_FLAG + i:SM_FLAG + i + 1], sg[:, 0:nn], MUL, MUL,
                                  r=[PA, sg, SM], w=[U.s((n, t0))])
                        else:
                            P.tt(U[:, n, t0:t1], PA[:, 0:nn], sg[:, 0:nn], MUL, r=[PA, sg], w=[U.s((n, t0))])
                for n in range(8):
                    CA = PP.next()
                    for j in range(31):
                        dg = DGp.next()
                        if j % 2 == 0:
                            P.act(dg[:], IDENT[:], AF.Identity, r=[IDENT, SM], w=[dg], scale=CW(n, j))
                        else:
                            P.ts(dg[:], IDENT[:], CW(n, j), None, MUL, None, r=[IDENT, SM], w=[dg])
                        P.mm(CA[:], dg[:], U[:, n, 2 + j:2 + j + CH], j == 0, j == 30,
                             r=[dg, U.s((n, 0)), U.s((n, HALO))], w=[CA])
                    P.act(CV[:, n, :], CA[:], AF.Identity, r=[CA, SM], w=[CV.s(n)],
                          bias=SM[:, SM_CB + n:SM_CB + n + 1], scale=1.0)
                for n in range(8):
                    cb16, cq16 = CB16.next(), CQ16.next()
                    P.act(cb16[:], CV[:, n, :], AF.Identity, r=[CV.s(n)], w=[cb16])
                    P.act(cq16[:], CV[:, n, :], AF.Square, r=[CV.s(n)], w=[cq16])
                    P.mm(SSb[:], ONESB[:], cb16[:], n == 0, n == 7, r=[ONESB, cb16], w=[SSb])
                    P.mm(ACC2[:], ONESB[:], cq16[:], n == 0, n == 7, r=[ONESB, cq16], w=[ACC2])
                P.ts(MEAN[:], SSb[:], 1.0 / D, None, MUL, None, r=[SSb], w=[MEAN])
                P.tt(MSQ[:], MEAN[:], MEAN[:], MUL, r=[MEAN], w=[MSQ])
                P.stt(VAR[:], ACC2[:], 1.0 / D, MSQ[:], MUL, SUB, r=[ACC2, MSQ], w=[VAR])
                P.act(SR[:], VAR[:], AF.Sqrt, r=[VAR], w=[SR], bias=EPS, scale=1.0)
                P.recip(RS[:], SR[:], r=[SR], w=[RS])
                P.stt(NBt[:], MEAN[:], -1.0, RS[:], MUL, MUL, r=[MEAN, RS], w=[NBt])
                for n in range(8):
                    ta, tb_ = TMPs.next(), TMP2.next()
                    P.tt(ta[:], CV[:, n, :], RS[:], MUL, r=[CV.s(n), RS], w=[ta])
                    P.tt(tb_[:], ta[:], NBt[:], ADD, r=[ta, NBt], w=[tb_])
                    P.act(S[:, n, 0:CH], tb_[:], AF.Silu, r=[tb_, SM], w=[S.s(n)],
                          bias=SM[:, SM_LNB + n:SM_LNB + n + 1], scale=SM[:, SM_LNG + n:SM_LNG + n + 1])
                for n in range(8):
                    PY, PG = PP.next(), PP.next()
                    for j in range(8):
                        P.mm(PY[:], WCO[:, j, n * 128:(n + 1) * 128], S[:, j, 0:CH], j == 0, j == 7, r=[WCO, S.s(j)], w=[PY])
                    for j in range(8):
                        P.mm(PG[:], WG[:, j, n * 128:(n + 1) * 128], H[:, j, HALO:XW], j == 0, j == 7, r=[WG, H.s((j, HALO))], w=[PG])
                    sg = SG.next()
                    P.act(sg[:], PG[:], AF.Sigmoid, r=[PG], w=[sg])
                    ob = OB16.next()
                    P.stt(ob[:], PY[:], SM[:, SM_BOUT + n:SM_BOUT + n + 1], sg[:], ADD, MUL, r=[PY, sg, SM], w=[ob])
                    P.store(ZCscr[n * 128:(n + 1) * 128, i * CH:(i + 1) * CH], ob[:], r=[ob], q=STQ)
                for n in range(8):
                    PG = PP.next()
                    for j in range(8):
                        P.mm(PG[:], WG[:, j, 1024 + n * 128:1024 + (n + 1) * 128], H[:, j, HALO:XW], j == 0, j == 7,
                             r=[WG, H.s((j, HALO))], w=[PG])
                    ob = OB16.next()
                    P.act(ob[:], PG[:], AF.Sigmoid, r=[PG], w=[ob])
                    P.store(GAscr[n * 128:(n + 1) * 128, i * CH:(i + 1) * CH], ob[:], r=[ob], q=STQ)
                for n in range(8):
                    PQ = PP.next()
                    for j in range(8):
                        P.mm(PQ[:], WQ[:, j, n * 128:(n + 1) * 128], H[:, j, HALO:XW], j == 0, j == 7, r=[WQ, H.s((j, HALO))], w=[PQ])
                    qr = CB16.next()
                    P.act(qr[:], PQ[:], AF.Identity, r=[PQ], w=[qr])
                    PR = PP.next()
                    P.mm(PR[:], RM[:], qr[:], True, True, r=[RM, qr], w=[PR])
                    t1, t2, ob = TMPs.next(), TMP2.next(), OB16.next()
                    P.tt(t1[:], qr[:], Cc[:], MUL, r=[qr, Cc], w=[t1])
                    P.tt(t2[:], PR[:], Sc[:], MUL, r=[PR, Sc], w=[t2])
                    P.tt(ob[:], t1[:], t2[:], ADD, r=[t1, t2], w=[ob])
                    P.store(Qscr[n, :, i * CH:(i + 1) * CH], ob[:], r=[ob], q=STQ)
            P.barrier()
            P.emit()

        if "D" in phases:
          with ExitStack() as cd:
            KH = [P.sb(cd, f"KH{i}", [128, T], BF16) for i in range(2)]
            VH = [P.sb(cd, f"VH{i}", [128, 64, 128], BF16) for i in range(2)]
            QH = [P.sb(cd, f"QH{i}", [128, NSL * CH], BF16) for i in range(2)]
            GAH = [P.sb(cd, f"GAH{i}", [128, NSL * CH], BF16) for i in range(2)]
            ZCH = [P.sb(cd, f"ZCH{i}", [128, NSL * CH], BF16) for i in range(2)]
            MK = P.sb(cd, "MK", [128, 16, CH], BF16)
            ONESF = P.sb(cd, "ONESF", [128, 128], F32)
            PT = Pool([P.sb(cd, f"PT{i}", [128, 2 * CH], BF16) for i in range(10)])
            ACC = [P.sb(cd, f"ACCD{i}", [128, 2 * CH], F32) for i in range(2)]
            R1 = P.sb(cd, "R1", [128, CH], F32)
            R2 = P.sb(cd, "R2", [128, CH], F32)
            O1S = Pool([P.sb(cd, f"O1S{i}", [128, CH], F32) for i in range(2)])
            O2S = Pool([P.sb(cd, f"O2S{i}", [128, CH], F32) for i in range(2)])
            AT1 = P.sb(cd, "AT1", [128, CH], F32)
            AT2 = P.sb(cd, "AT2", [128, CH], F32)
            ATT = Pool([P.sb(cd, f"ATT{i}", [128, CH], F32) for i in range(2)])
            SQ16 = Pool([P.sb(cd, f"SQ16{i}", [128, CH], BF16) for i in range(2)])
            LNd = P.sb(cd, "LNd", [128, CH], F32)
            RSd = P.sb(cd, "RSd", [128, CH], F32)
            YA = P.sb(cd, "YA", [128, CH], F32)
            ZA = P.sb(cd, "ZA", [128, CH], F32)
            ZO = Pool([P.sb(cd, f"ZO{i}", [128, CH], BF16) for i in range(2)])
            SP = Pool([P.ps(cd, f"SP{i}", [128, 2 * CH]) for i in range(2)])
            O1 = P.ps(cd, "O1", [128, CH])
            O2 = P.ps(cd, "O2", [128, CH])
            L1 = P.ps(cd, "L1", [128, CH])
            L2 = P.ps(cd, "L2", [128, CH])
            P.memset(ONESF[:], 1.0, w=[ONESF])
            for mi in range(16):
                P.load(MK[:, mi, :], masks[:, mi * CH:(mi + 1) * CH], w=[MK], q="gpsimd")

            def loads_d(h):
                b = h % 2
                P.load(KH[b][:], Kscr[h], w=[KH[b]])
                vsrc = Vscr[:, h * 128:(h + 1) * 128].rearrange("(kb p) e -> p kb e", p=128)
                for q4 in range(4):
                    P.load(VH[b][:, q4 * 16:(q4 + 1) * 16, :], vsrc[:, q4 * 16:(q4 + 1) * 16, :], w=[VH[b]])
                P.load(QH[b][:], Qscr[h], w=[QH[b]])
                P.load(GAH[b][:], GAscr[h * 128:(h + 1) * 128, :], w=[GAH[b]])
                P.load(ZCH[b][:], ZCscr[h * 128:(h + 1) * 128, :], w=[ZCH[b]])

            pending = []

            def run_pending(kb):
                while pending and (kb is None or pending[0][0] <= kb):
                    pending.pop(0)[1]()

            def make_finalize(h, i, Gh, Zh, qs):
                acc, o1s, o2s = ACC[i % 2], O1S.next(), O2S.next()
                att, sq = ATT.next(), SQ16.next()

                def st1():
                    P.copy(o1s[:], O1[:], r=[O1], w=[o1s])
                    P.copy(o2s[:], O2[:], r=[O2], w=[o2s])

                def st2():
                    P.mm(L1[:], ONESF[:], acc[:, 0:CH], False, True, r=[ONESF, acc], w=[L1])
                    P.mm(L2[:], ONESF[:], acc[:, CH:2 * CH], False, True, r=[ONESF, acc], w=[L2])

                def st3():
                    P.recip(R1[:], L1[:], r=[L1], w=[R1])
                    P.recip(R2[:], L2[:], r=[L2], w=[R2])
                    P.tt(AT1[:], o1s[:], R1[:], MUL, r=[o1s, R1], w=[AT1])
                    P.tt(AT2[:], o2s[:], R2[:], MUL, r=[o2s, R2], w=[AT2])
                    P.stt(att[:], AT2[:], NEGLAM, AT1[:], MUL, ADD, r=[AT2, AT1, DV], w=[att])

                def st6():
                    P.act(sq[:], att[:], AF.Square, r=[att], w=[sq])

                ms = []

                def st7():
                    MS = SP.next()
                    ms.append(MS)
                    P.mm(MS[:, 0:CH], ONESB[:], sq[:], True, True, r=[ONESB, sq], w=[MS])

                def st8():
                    P.act(LNd[:], ms[0][:, 0:CH], AF.Ln, r=[ms[0]], w=[LNd], bias=EPS, scale=1.0 / 128)
                    P.act(RSd[:], LNd[:], AF.Exp, r=[LNd], w=[RSd], scale=-0.5)

                def st9():
                    P.tt(YA[:], att[:], RSd[:], MUL, r=[att, RSd], w=[YA])
                    P.stt(ZA[:], YA[:], HNV, Gh[:, qs], MUL, MUL, r=[YA, DV, Gh], w=[ZA])
                    zo = ZO.next()
                    P.tt(zo[:], ZA[:], Zh[:, qs], ADD, r=[ZA, Zh], w=[zo])
                    P.store(Zscr[h * 128:(h + 1) * 128, qs], zo[:], r=[zo], q=STQ)

                st1()
                pending.extend([(0, st2), (1, st3), (3, st6), (4, st7), (5, st8), (6, st9)])

            loads_d(0)
            for h in range(8):
                if h + 1 < 8:
                    loads_d(h + 1)
                b = h % 2
                Kh, Vh, Qh, Gh, Zh = KH[b], VH[b], QH[b], GAH[b], ZCH[b]
                for i in range(NSL):
                    nk = 8 * (i + 1)
                    qs = slice(i * CH, (i + 1) * CH)

                    def smm(kb):
                        Sx = SP.next()
                        ks = slice(kb * 128, (kb + 1) * 128)
                        P.mm(Sx[:, 0:CH], Kh[0:64, ks], Qh[0:64, qs], True, True, r=[Kh, Qh], w=[Sx.s(0)])
                        P.mm(Sx[:, CH:2 * CH], Kh[64:128, ks], Qh[64:128, qs], True, True, r=[Kh, Qh], w=[Sx.s(1)])
                        return Sx

                    acc = ACC[i % 2]

                    def av(kb, pt):
                        st, sp_ = (kb == 0), (kb == nk - 1)
                        P.mm(O1[:], Vh[:, kb, :], pt[:, 0:CH], st, sp_, r=[Vh, pt.s(0)], w=[O1])
                        P.mm(O2[:], Vh[:, kb, :], pt[:, CH:2 * CH], st, sp_, r=[Vh, pt.s(1)], w=[O2])
                        if kb % 5 in (2, 4):
                            P.mm(L1[:], ONESB[:], pt[:, 0:CH], kb == 2, False, r=[ONESB, pt.s(0)], w=[L1])
                            P.mm(L2[:], ONESB[:], pt[:, CH:2 * CH], kb == 2, False, r=[ONESB, pt.s(1)], w=[L2])

                    Snext = smm(0)
                    prev = None
                    for kb in range(nk):
                        Sx = Snext
                        pt = PT.next()
                        P.act(pt[:], Sx[:], AF.Exp, r=[Sx], w=[pt], scale=0.125)
                        if kb >= nk - 8:
                            jj = kb - (nk - 8)
                            midx = (i % 2) * 8 + jj
                            P.tt(pt[:, 0:CH], pt[:, 0:CH], MK[:, midx, :], MUL, r=[pt.s(0), MK], w=[pt.s(0)])
                            P.tt(pt[:, CH:2 * CH], pt[:, CH:2 * CH], MK[:, midx, :], MUL, r=[pt.s(1), MK], w=[pt.s(1)])
                        if kb % 5 not in (2, 4):
                            if kb == 0:
                                P.copy(acc[:], pt[:], r=[pt], w=[acc])
                            else:
                                P.tt(acc[:], acc[:], pt[:], ADD, r=[acc, pt], w=[acc])
                        if kb + 1 < nk:
                            Snext = smm(kb + 1)
                        if prev is not None:
                            av(*prev)
                        prev = (kb, pt)
                        run_pending(kb)
                    av(*prev)
                    make_finalize(h, i, Gh, Zh, qs)
                run_pending(None)
            P.barrier()
            P.emit()

        if "C" in phases:
          with ExitStack() as cc:
            SC = 256
            WO = P.sb(cc, "WO", [128, 8, 1024], BF16)
            WF1 = P.sb(cc, "WF1", [128, 8, 4096], BF16)
            WF2 = P.sb(cc, "WF2", [128, 32, 1024], BF16)
            load_w(WO, w_o, 1024, 0)
            load_w(WF1, w_ff1, 4096, 0)
            for q4 in range(4):
                for cg in range(2):
                    P.load(WF2[:, q4 * 8:(q4 + 1) * 8, cg * 512:(cg + 1) * 512],
                           fm(w_ff2[q4 * 1024:(q4 + 1) * 1024, cg * 512:(cg + 1) * 512]), w=[WF2], q="gpsimd")
            Z = [P.sb(cc, f"ZC{i}", [128, 8, SC], BF16) for i in range(2)]
            X = P.sb(cc, "XC", [128, 8, SC], F32)
            Y = P.sb(cc, "YC", [128, 8, SC], F32)
            H2 = P.sb(cc, "H2C", [128, 8, SC], BF16)
            F1 = P.sb(cc, "F1C", [128, 32, SC], BF16)
            SQ = P.sb(cc, "SQC", [128, 8, SC], BF16)
            SR = P.sb(cc, "SRC", [128, SC], F32)
            RS = P.sb(cc, "RSC", [128, SC], F32)
            TMPs = Pool([P.sb(cc, f"TMC{i}", [128, SC], F32) for i in range(3)])
            RL = Pool([P.sb(cc, f"RLC{i}", [128, SC], F32) for i in range(3)])
            Q16 = Pool([P.sb(cc, f"Q16C{i}", [128, SC], BF16) for i in range(2)])
            ACC = P.ps(cc, "ACCC", [128, CH])
            SSb = P.ps(cc, "SSbC", [128, CH])
            PP = Pool([P.ps(cc, f"PPC{i}", [128, CH]) for i in range(6)])
            NSC = NSL * CH // SC

            def xcols(s):
                slot, hf = divmod(s, CH // SC)
                o = slot * XW + HALO + hf * SC
                return slice(o, o + SC)

            def load_z(s):
                P.load(Z[s % 2][:], fm(Zscr[:, s * SC:(s + 1) * SC]), w=[Z[s % 2]])

            load_z(0)
            P.load(X[:], fm(xT_own[:, xcols(0)]), w=[X])
            for s in range(NSC):
                if s + 1 < NSC:
                    load_z(s + 1)
                Zs = Z[s % 2]
                for n in range(8):
                    PY = PP.next()
                    for j in range(8):
                        P.mm(PY[:, 0:SC], WO[:, j, n * 128:(n + 1) * 128], Zs[:, j, :], j == 0, j == 7, r=[WO, Zs], w=[PY])
                    P.act(Y[:, n, :], PY[:, 0:SC], AF.Identity, r=[PY], w=[Y.s(n)])
                    q16 = Q16.next()
                    P.act(q16[:], PY[:, 0:SC], AF.Square, r=[PY], w=[q16])
                    P.mm(ACC[:, 0:SC], ONESB[:], q16[:], n == 0, n == 7, r=[ONESB, q16], w=[ACC])
                P.act(SR[:], ACC[:, 0:SC], AF.Sqrt, r=[ACC], w=[SR], bias=EPS, scale=1.0 / D)
                P.recip(RS[:], SR[:], r=[SR], w=[RS])
                for n in range(8):
                    tm = TMPs.next()
                    P.stt(tm[:], Y[:, n, :], GP1(n), RS[:], MUL, MUL, r=[Y.s(n), DV, RS], w=[tm])
                    P.tt(X[:, n, :], X[:, n, :], tm[:], ADD, r=[X.s(n), tm], w=[X.s(n)])
                P.act(SQ[:], X[:], AF.Square, r=[X], w=[SQ])
                for j in range(8):
                    P.mm(SSb[:, 0:SC], ONESB[:], SQ[:, j, :], j == 0, j == 7, r=[ONESB, SQ], w=[SSb])
                P.act(SR[:], SSb[:, 0:SC], AF.Sqrt, r=[SSb], w=[SR], bias=EPS, scale=1.0 / D)
                P.recip(RS[:], SR[:], r=[SR], w=[RS])
                for j in range(8):
                    tm = TMPs.next()
                    P.stt(tm[:], X[:, j, :], A2(j), RS[:], MUL, MUL, r=[X.s(j), DV, RS], w=[tm])
                    P.act(H2[:, j, :], tm[:], AF.Identity, r=[tm, MODV], w=[H2.s(j)], bias=SH2(j), scale=1.0)
                for m in range(32):
                    PF = PP.next()
                    for j in range(8):
                        P.mm(PF[:, 0:SC], WF1[:, j, m * 128:(m + 1) * 128], H2[:, j, :], j == 0, j == 7, r=[WF1, H2.s(j)], w=[PF])
                    rl = RL.next()
                    P.ts(rl[:], PF[:, 0:SC], 0.0, None, MAX, None, r=[PF], w=[rl])
                    P.act(F1[:, m, :], rl[:], AF.Square, r=[rl], w=[F1.s(m)])
                for n in range(8):
                    PO = PP.next()
                    for m in range(32):
                        P.mm(PO[:, 0:SC], WF2[:, m, n * 128:(n + 1) * 128], F1[:, m, :], m == 0, m == 31, r=[WF2, F1.s(m)], w=[PO])
                    P.act(Y[:, n, :], PO[:, 0:SC], AF.Identity, r=[PO], w=[Y.s(n)])
                    q16 = Q16.next()
                    P.act(q16[:], PO[:, 0:SC], AF.Square, r=[PO], w=[q16])
                    P.mm(ACC[:, 0:SC], ONESB[:], q16[:], n == 0, n == 7, r=[ONESB, q16], w=[ACC])
                P.act(SR[:], ACC[:, 0:SC], AF.Sqrt, r=[ACC], w=[SR], bias=EPS, scale=1.0 / D)
                P.recip(RS[:], SR[:], r=[SR], w=[RS])
                for n in range(8):
                    tm = TMPs.next()
                    P.stt(tm[:], Y[:, n, :], GP2(n), RS[:], MUL, MUL, r=[Y.s(n), DV, RS], w=[tm])
                    P.tt(X[:, n, :], X[:, n, :], tm[:], ADD, r=[X.s(n), tm], w=[X.s(n)])
                P.store(fm(outT[:, s * SC:(s + 1) * SC]), X[:], r=[X], q=STQ)
                if s + 1 < NSC:
                    P.load(X[:], fm(xT_own[:, xcols(s + 1)]), w=[X])
            P.barrier()
            P.emit()
    return nc


_DBG = None
STQ = "sync"
import os
NCH_RUN = int(os.environ.get("NCH_RUN", "16"))
A_PARTS = os.environ.get("A_PARTS", "nkv")
DBG_OUT = os.environ.get("DBG_OUT", "1") == "1"


def _fmv(v, n):
    return np.ascontiguousarray(np.asarray(v, np.float32).reshape(n, 128).T)


def _host_constants():
    inv_freq = (np.float32(10000.0) ** (-np.arange(0, 64, 2, dtype=np.float32) / np.float32(64))).astype(np.float32)
    pos = np.arange(T, dtype=np.float32)
    ang = (pos[:, None] * inv_freq[None, :]).astype(np.float32)
    cos = np.cos(ang).astype(np.float32).T
    sin = np.sin(ang).astype(np.float32).T
    cos_all = np.ascontiguousarray(np.tile(cos, (4, 1)))
    sin_all = np.ascontiguousarray(np.tile(sin, (4, 1)))
    rm = np.zeros((128, 128), np.float32)
    for blk in (0, 64):
        for d in range(32):
            rm[blk + d + 32, blk + d] = -1.0
            rm[blk + d, blk + d + 32] = 1.0
    k = np.arange(128)[:, None]
    q = np.arange(CH)[None, :]
    diag = [((j * 128 + k) <= q).astype(np.float32) for j in range(4)]
    ones = [np.ones((128, CH), np.float32)] * 4
    zeros = [np.zeros((128, CH), np.float32)] * 4
    return cos_all, sin_all, rm, diag, ones, zeros


def kernel(x, c, w_ada, b_ada, pre_norm1, post_norm1, w_in, conv_w, conv_b, conv_ln_g, conv_ln_b,
           conv_w_out, conv_b_out, lambda_q1, lambda_k1, lambda_q2, lambda_k2, head_norm, w_o,
           pre_norm2, post_norm2, w_ff1, w_ff2):
    x = np.asarray(x, np.float32)
    c = np.asarray(c, np.float32)
    cos_all, sin_all, rm, diag, ones, zeros = _host_constants()
    w_in0 = np.asarray(w_in, np.float32)[0]
    perm = np.arange(7168)
    for base in (2048, 3072):
        for h in range(8):
            for m in range(2):
                perm[base + h * 128 + m * 64: base + h * 128 + m * 64 + 64] = base + m * 512 + h * 64 + np.arange(64)
    w_in_p = np.ascontiguousarray(w_in0[:, perm])
    w_ada0 = np.ascontiguousarray(np.asarray(w_ada, np.float32)[0])
    w_co0 = np.ascontiguousarray(np.asarray(conv_w_out, np.float32)[0])
    w_o0 = np.ascontiguousarray(np.asarray(w_o, np.float32)[0])
    w_ff10 = np.ascontiguousarray(np.asarray(w_ff1, np.float32)[0])
    w_ff20 = np.ascontiguousarray(np.asarray(w_ff2, np.float32)[0])

    in_maps = []
    for core in range(8):
        b, jh = divmod(core, 2)
        own = OWN[jh]
        xT = np.ascontiguousarray(x[b].T)
        xo = np.zeros((D, NSL, XW), np.float32)
        flag = np.ones((128, NSL), np.float32)
        for i, ch in enumerate(own):
            lo = ch * CH - HALO
            if lo < 0:
                xo[:, i, HALO:] = xT[:, 0:CH]
                flag[:, i] = 0.0
            else:
                xo[:, i, :] = xT[:, lo:lo + XW]
        small = np.zeros((128, NS), np.float32)
        small[:, SM_BADA:SM_BADA + 48] = _fmv(b_ada[0], 48)
        small[:, SM_PN1:SM_PN1 + 8] = _fmv(pre_norm1[0], 8)
        small[:, SM_PO1:SM_PO1 + 8] = _fmv(post_norm1[0], 8)
        small[:, SM_PN2:SM_PN2 + 8] = _fmv(pre_norm2[0], 8)
        small[:, SM_PO2:SM_PO2 + 8] = _fmv(post_norm2[0], 8)
        small[:, SM_CB:SM_CB + 8] = _fmv(conv_b[0], 8)
        small[:, SM_LNG:SM_LNG + 8] = _fmv(conv_ln_g[0], 8)
        small[:, SM_LNB:SM_LNB + 8] = _fmv(conv_ln_b[0], 8)
        small[:, SM_BOUT:SM_BOUT + 8] = _fmv(conv_b_out[0], 8)
        small[:, SM_HN] = np.asarray(head_norm, np.float32)[0]
        small[:, SM_FLAG:SM_FLAG + 8] = flag
        small[:, SM_CT:SM_CT + 16] = np.repeat(_fmv(c[b], 8), 2, axis=1)
        for k, v in enumerate((lambda_q1, lambda_k1, lambda_q2, lambda_k2)):
            small[:, SM_LAM + 64 * k:SM_LAM + 64 * (k + 1)] = np.asarray(v, np.float32)[0][None, :]
        cw = np.asarray(conv_w, np.float32)[0]
        small[:, SM_CW:SM_CW + 248] = cw.T.reshape(8, 128, 31).transpose(1, 0, 2).reshape(128, 248)
        idx = np.concatenate([np.arange(ch * CH, (ch + 1) * CH) for ch in own])
        mk = []
        for par in range(2):
            own_is_2i = (par == 0) if jh == 0 else (par == 1)
            mk += (diag + zeros) if own_is_2i else (ones + diag)
        in_maps.append({
            "xT_all": xT, "xT_own": xo.reshape(D, NSL * XW), "w_ada": w_ada0, "w_in": w_in_p, "w_co": w_co0,
            "w_o": w_o0, "w_ff1": w_ff10, "w_ff2": w_ff20, "small": small,
            "cos_all": cos_all, "sin_all": sin_all,
            "cos_own": np.ascontiguousarray(cos_all[:, idx]), "sin_own": np.ascontiguousarray(sin_all[:, idx]),
            "masks": np.ascontiguousarray(np.concatenate(mk, axis=1)), "rmat": rm,
            "ident": np.eye(128, dtype=np.float32),
        })
    if _DBG is not None:
        nc = build_program(_DBG, dbg=DBG_OUT)
        res = run_bass_kernel_spmd(nc, in_maps, core_ids=list(range(8)))
        return res, in_maps
    nc = build_program()
    res = run_bass_kernel_spmd(nc, in_maps, core_ids=list(range(8)))
    out = np.empty((NB, T, D), np.float32)
    for core in range(8):
        b, jh = divmod(core, 2)
        oT = np.asarray(res.results[core]["outT"], np.float32)
        for i, ch in enumerate(OWN[jh]):
            out[b, ch * CH:(ch + 1) * CH, :] = oT[:, i * CH:(i + 1) * CH].T
    return out
```

```python
import numpy as np
import concourse.bass as bass
import concourse.mybir as mybir

F32 = mybir.dt.float32
BF16 = mybir.dt.bfloat16
AF = mybir.ActivationFunctionType
ALU = mybir.AluOpType

ENGS = ("tensor", "vector", "scalar", "gpsimd", "sync")
N_DMA_SEMS = 32
SAME_ENGINE_SYNC = True


class Res:
    __slots__ = ("t", "last_w", "readers", "name", "parent", "kids")

    def __init__(self, t, name, parent=None):
        self.t = t
        self.last_w = None
        self.readers = []
        self.name = name
        self.parent = parent
        self.kids = {}

    def s(self, k):
        if k not in self.kids:
            self.kids[k] = Res(self.t, f"{self.name}.{k}", parent=self)
        return self.kids[k]

    def holders(self):
        h = [self]
        if self.parent is not None:
            h.append(self.parent)
        h.extend(self.kids.values())
        return h

    def __getitem__(self, idx):
        return self.t[idx]


class Op:
    __slots__ = ("eng", "fn", "deps", "signal", "count", "is_dma", "sem", "semval", "barrier", "emitted")

    def __init__(self, eng, fn, is_dma=False):
        self.eng = eng
        self.fn = fn
        self.deps = []
        self.signal = False
        self.count = None
        self.is_dma = is_dma
        self.sem = None
        self.semval = None
        self.barrier = False
        self.emitted = False


class Prog:
    def __init__(self, nc):
        self.nc = nc
        self.ops = {e: [] for e in ENGS}
        self.esem = {e: nc.alloc_semaphore(name=f"s_{e}") for e in ENGS}
        self.ecount = {e: 0 for e in ENGS}
        self.waited = {e: {} for e in ENGS}
        self.nd = 2 * N_DMA_SEMS
        self.dsems = [nc.alloc_semaphore(name=f"d_{i}") for i in range(self.nd)]
        self.dsem_uses = [0] * self.nd
        self.dsem_last = [None] * self.nd
        self.dma_rr = {"sync": 0, "gpsimd": 0}
        self.last_op = {e: None for e in ENGS}
        self.n_ops = 0

    def sb(self, ctx, name, shape, dtype):
        t = ctx.enter_context(self.nc.sbuf_tensor(name, list(shape), dtype))
        return Res(t, name)

    def ps(self, ctx, name, shape, dtype=F32):
        t = ctx.enter_context(self.nc.psum_tensor(name, list(shape), dtype))
        return Res(t, name)

    def _deps(self, op, r, w):
        deps = []
        for x0 in r:
            for x in x0.holders():
                if x.last_w is not None:
                    deps.append(x.last_w)
        for x0 in w:
            for x in x0.holders():
                if x.last_w is not None:
                    deps.append(x.last_w)
                deps.extend(x.readers)
        seen = set()
        for d in deps:
            if id(d) in seen or d is op or d.emitted:
                continue
            seen.add(id(d))
            if (not d.is_dma) and (not op.is_dma) and d.eng == op.eng and (d.eng == "tensor" or not SAME_ENGINE_SYNC):
                continue
            op.deps.append(d)
            if not d.is_dma:
                d.signal = True
        for x in r:
            if op.is_dma:
                x.readers.append(op)
            else:
                x.readers = [o for o in x.readers if o.is_dma or o.eng != op.eng]
                x.readers.append(op)
        for x in w:
            x.last_w = op
            x.readers = []

    def add(self, eng, fn, r=(), w=()):
        op = Op(eng, fn)
        self._deps(op, r, w)
        self.ops[eng].append(op)
        self.last_op[eng] = op
        self.n_ops += 1
        return op

    def dma(self, fn, r=(), w=(), q="sync"):
        op = Op(q, fn, is_dma=True)
        k = self.dma_rr[q] % N_DMA_SEMS + (N_DMA_SEMS if q == "gpsimd" else 0)
        self.dma_rr[q] += 1
        prev = self.dsem_last[k]
        self.dsem_uses[k] += 1
        op.sem = k
        op.semval = 16 * self.dsem_uses[k]
        if prev is not None and not prev.emitted:
            op.deps.append(prev)
        self._deps(op, r, w)
        self.dsem_last[k] = op
        self.ops[q].append(op)
        self.n_ops += 1
        return op

    def barrier(self):
        targets = []
        for e in ENGS:
            lo = self.last_op[e]
            if lo is not None:
                lo.signal = True
                targets.append(lo)
        for k in range(self.nd):
            if self.dsem_last[k] is not None:
                targets.append(self.dsem_last[k])
        for e in ENGS:
            op = Op(e, None)
            op.barrier = True
            op.deps = [t for t in targets if t.is_dma or t.eng != e]
            self.ops[e].append(op)

    def emit(self, name=None):
        nc = self.nc
        for e in ENGS:
            c = self.ecount[e]
            for op in self.ops[e]:
                if op.is_dma or op.barrier:
                    continue
                if op.signal:
                    c += 1
                    op.count = c
            self.ecount[e] = c

        def emit_engine(e, eng):
            waited = self.waited[e]
            for op in self.ops[e]:
                for d in op.deps:
                    if d.is_dma:
                        key, val, sem = ("d", d.sem), d.semval, self.dsems[d.sem]
                    else:
                        assert d.count is not None, "dependency on non-signalling op"
                        key, val, sem = ("e", d.eng), d.count, self.esem[d.eng]
                    if waited.get(key, 0) >= val:
                        continue
                    eng.wait_ge(sem, val)
                    waited[key] = val
                if op.barrier:
                    continue
                ins = op.fn(eng)
                if op.is_dma:
                    ins.then_inc(self.dsems[op.sem], 16)
                elif op.signal:
                    ins.then_inc(self.esem[e], 1)

        with nc.Block() as block:
            @block.tensor
            def _(eng):
                emit_engine("tensor", eng)

            @block.vector
            def _(eng):
                emit_engine("vector", eng)

            @block.scalar
            def _(eng):
                emit_engine("scalar", eng)

            @block.gpsimd
            def _(eng):
                emit_engine("gpsimd", eng)

            @block.sync
            def _(eng):
                emit_engine("sync", eng)
        for e in ENGS:
            for op in self.ops[e]:
                op.emitted = True
        self.ops = {e: [] for e in ENGS}

    def mm(self, out, lhsT, rhs, start, stop, r, w, **kw):
        return self.add("tensor", lambda e: e.matmul(out, lhsT, rhs, start=start, stop=stop, **kw), r=r, w=w)

    def act(self, out, in_, func, r, w, bias=None, scale=None, eng="scalar"):
        kw = {}
        if bias is not None:
            kw["bias"] = bias
        if scale is not None:
            kw["scale"] = scale
        return self.add(eng, lambda e: e.activation(out, in_, func, **kw), r=r, w=w)

    def tt(self, out, in0, in1, op, r, w, eng="vector"):
        return self.add(eng, lambda e: e.tensor_tensor(out, in0, in1, op), r=r, w=w)

    def ts(self, out, in0, s1, s2, op0, op1, r, w, eng="vector"):
        if op1 is None:
            return self.add(eng, lambda e: e.tensor_scalar(out, in0, s1, None, op0), r=r, w=w)
        return self.add(eng, lambda e: e.tensor_scalar(out, in0, s1, s2, op0, op1), r=r, w=w)

    def stt(self, out, in0, scalar, in1, op0, op1, r, w):
        return self.add("vector", lambda e: e.scalar_tensor_tensor(out, in0, scalar, in1, op0, op1), r=r, w=w)

    def copy(self, out, in_, r, w, eng="vector"):
        return self.add(eng, lambda e: e.tensor_copy(out, in_), r=r, w=w)

    def recip(self, out, in_, r, w):
        return self.add("vector", lambda e: e.reciprocal(out, in_), r=r, w=w)

    def memset(self, ap, val, w, eng="vector"):
        return self.add(eng, lambda e: e.memset(ap, val), r=(), w=w)

    def load(self, out, in_, w, r=(), q="sync", **kw):
        return self.dma(lambda e: e.dma_start(out=out, in_=in_, **kw), r=r, w=w, q=q)

    def store(self, out, in_, r, w=(), q="sync", **kw):
        return self.dma(lambda e: e.dma_start(out=out, in_=in_, **kw), r=r, w=w, q=q)


from contextlib import ExitStack
from concourse.bass_utils import run_bass_kernel_spmd

D = 1024
T = 8192
NB = 4
NCH = 16
NSL = 8
CH = 512
HALO = 32
XW = CH + HALO
EPS = 1e-6
LAM_INIT = 0.2
OWN = {0: [0, 3, 4, 7, 8, 11, 12, 15], 1: [1, 2, 5, 6, 9, 10, 13, 14]}

SM_BADA = 0
SM_PN1 = 48
SM_PO1 = 56
SM_PN2 = 64
SM_PO2 = 72
SM_CB = 80
SM_LNG = 88
SM_LNB = 96
SM_BOUT = 104
SM_HN = 112
SM_FLAG = 113
SM_CT = 121
SM_LAM = 137
SM_CW = 393
NS = 393 + 248


class Pool:
    def __init__(self, items):
        self.items = items
        self.i = 0

    def next(self):
        x = self.items[self.i % len(self.items)]
        self.i += 1
        return x


def build_program(phases="ABDC", dbg=False):
    nc = bass.Bass("TRN2", target_bir_lowering=False)

    def din(name, shape, dt=F32):
        return nc.dram_tensor(name, list(shape), dt, kind="ExternalInput").ap()

    xT_all = din("xT_all", [D, T])
    xT_own = din("xT_own", [D, NSL * XW])
    w_ada = din("w_ada", [D, 6 * D])
    w_in = din("w_in", [D, 7 * D])
    w_co = din("w_co", [D, D])
    w_o = din("w_o", [D, D])
    w_ff1 = din("w_ff1", [D, 4 * D])
    w_ff2 = din("w_ff2", [4 * D, D])
    small = din("small", [128, NS])
    cos_all = din("cos_all", [128, T])
    sin_all = din("sin_all", [128, T])
    cos_own = din("cos_own", [128, NSL * CH])
    sin_own = din("sin_own", [128, NSL * CH])
    masks = din("masks", [128, 16 * CH])
    rmat = din("rmat", [128, 128])
    ident = din("ident", [128, 128])
    outT = nc.dram_tensor("outT", [D, NSL * CH], F32, kind="ExternalOutput").ap()

    def dscr(name, shape, dt=BF16):
        return nc.dram_tensor(name, list(shape), dt, kind="ExternalOutput" if dbg else "Internal").ap()

    dbg_small = dscr("dbg_small", [128, 88], F32) if dbg else None

    Kscr = dscr("Kscr", [8, 128, T])
    Vscr = dscr("Vscr", [T, D])
    Qscr = dscr("Qscr", [8, 128, NSL * CH])
    ZCscr = dscr("ZCscr", [D, NSL * CH])
    GAscr = dscr("GAscr", [D, NSL * CH])
    Zscr = dscr("Zscr", [D, NSL * CH])

    P = Prog(nc)
    MUL, ADD, SUB, MAX = ALU.mult, ALU.add, ALU.subtract, ALU.max

    def fm(ap):
        return ap.rearrange("(j p) t -> p j t", p=128)

    with ExitStack() as g:
        SM = P.sb(g, "SM", [128, NS], F32)
        MODV = P.sb(g, "MODV", [128, 48], F32)
        DV = P.sb(g, "DV", [128, 40], F32)
        ONESB = P.sb(g, "ONESB", [128, 128], BF16)
        RM = P.sb(g, "RM", [128, 128], BF16)
        IDENT = P.sb(g, "IDENT", [128, 128], BF16)

        def smc(off, n=1):
            return SM[:, off:off + n]

        SH1 = lambda j: MODV[:, 0 + j:1 + j]
        SH2 = lambda j: MODV[:, 24 + j:25 + j]
        A1 = lambda j: DV[:, 0 + j:1 + j]
        A2 = lambda j: DV[:, 8 + j:9 + j]
        GP1 = lambda j: DV[:, 16 + j:17 + j]
        GP2 = lambda j: DV[:, 24 + j:25 + j]
        NEGLAM = DV[:, 32:33]
        HNV = DV[:, 33:34]

        with ExitStack() as c0:
            WA = [P.sb(c0, f"WA{i}", [128, 8, 768], F32) for i in range(2)]
            CS = P.sb(c0, "CS", [128, 16], F32)
            TMPL = P.sb(c0, "TMPL", [128, 64], F32)
            SS = P.sb(c0, "SSl", [128, 4], F32)
            MODP = P.ps(c0, "MODP", [128, 96], F32)
            P.load(SM[:], small, w=[SM])
            P.load(RM[:], rmat, w=[RM], q="gpsimd")
            P.load(IDENT[:], ident, w=[IDENT], q="gpsimd")
            P.memset(ONESB[:], 1.0, w=[ONESB])
            P.act(CS[:], smc(SM_CT, 16), AF.Silu, r=[SM], w=[CS])
            for gi in range(8):
                Wt = WA[gi % 2]
                P.load(Wt[:], fm(w_ada[:, gi * 768:(gi + 1) * 768]), w=[Wt])
                for nn in range(6):
                    n = gi * 6 + nn
                    for j in range(8):
                        P.mm(MODP[:, 2 * n:2 * n + 2], Wt[:, j, nn * 128:(nn + 1) * 128], CS[:, 2 * j:2 * j + 2],
                             j == 0, j == 7, r=[Wt, CS], w=[MODP])
            modp_v = MODP[:].rearrange("p (n two) -> p n two", two=2)[:, :, 0]
            P.tt(MODV[:], modp_v, smc(SM_BADA, 48), ADD, r=[MODP, SM], w=[MODV])
            P.stt(DV[:, 0:8], MODV[:, 8:16], 1.0, smc(SM_PN1, 8), ADD, MUL, r=[MODV, SM], w=[DV])
            P.stt(DV[:, 8:16], MODV[:, 32:40], 1.0, smc(SM_PN2, 8), ADD, MUL, r=[MODV, SM], w=[DV])
            P.tt(DV[:, 16:24], MODV[:, 16:24], smc(SM_PO1, 8), MUL, r=[MODV, SM], w=[DV])
            P.tt(DV[:, 24:32], MODV[:, 40:48], smc(SM_PO2, 8), MUL, r=[MODV, SM], w=[DV])
            for k in range(2):
                P.tt(TMPL[:], smc(SM_LAM + 128 * k, 64), smc(SM_LAM + 128 * k + 64, 64), MUL, r=[SM], w=[TMPL])
                P.add("vector", lambda e, k=k: e.reduce_sum(SS[:, k:k + 1], TMPL[:], mybir.AxisListType.X),
                      r=[TMPL], w=[SS])
            P.act(SS[:, 2:4], SS[:, 0:2], AF.Exp, r=[SS], w=[SS])
            P.stt(DV[:, 32:33], SS[:, 3:4], -LAM_INIT, SS[:, 2:3], ADD, SUB, r=[SS], w=[DV])
            P.ts(DV[:, 33:34], smc(SM_HN, 1), 1.0 - LAM_INIT, None, MUL, None, r=[SM], w=[DV])
            if dbg:
                P.store(dbg_small[:, 0:48], MODV[:], r=[MODV])
                P.store(dbg_small[:, 48:82], DV[:, 0:34], r=[DV])
            P.barrier()
            P.emit()

        def norm_H(X, H, t0, t1, avec, shvec, SQ, SSb, SR, RS, TMPs):
            n = t1 - t0
            P.act(SQ[:, :, t0:t1], X[:, :, t0:t1], AF.Square, r=[X], w=[SQ])
            for j in range(8):
                P.mm(SSb[:, 0:n], ONESB[:], SQ[:, j, t0:t1], j == 0, j == 7, r=[ONESB, SQ], w=[SSb])
            P.act(SR[:, 0:n], SSb[:, 0:n], AF.Ln, r=[SSb], w=[SR], bias=EPS, scale=1.0 / D)
            P.act(RS[:, 0:n], SR[:, 0:n], AF.Exp, r=[SR], w=[RS], scale=-0.5)
            for j in range(8):
                Tm = TMPs.next()
                P.stt(Tm[:, 0:n], X[:, j, t0:t1], avec(j), RS[:, 0:n], MUL, MUL, r=[X, DV, RS], w=[Tm])
                P.act(H[:, j, t0:t1], Tm[:, 0:n], AF.Identity, r=[Tm, MODV], w=[H.s((j, t0))], bias=shvec(j), scale=1.0)

        def load_w(Wt, src, ncols, c0col=0):
            for cc in range(0, ncols, 512):
                P.load(Wt[:, :, cc:cc + 512], fm(src[:, c0col + cc:c0col + cc + 512]), w=[Wt], q="gpsimd")

        if "A" in phases:
          with ExitStack() as ca:
            WK = P.sb(ca, "WK", [128, 8, 1024], BF16)
            WV = P.sb(ca, "WV", [128, 8, 1024], BF16)
            load_w(WK, w_in, 1024, 3072)
            load_w(WV, w_in, 1024, 4096)
            X = [P.sb(ca, f"XA{i}", [128, 8, CH], F32) for i in range(2)]
            H = [P.sb(ca, f"HA{i}", [128, 8, CH], BF16) for i in range(2)]
            SQ = P.sb(ca, "SQA", [128, 8, CH], BF16)
            SR = P.sb(ca, "SRA", [128, CH], F32)
            RS = P.sb(ca, "RSA", [128, CH], F32)
            TMPs = Pool([P.sb(ca, f"TMA{i}", [128, CH], F32) for i in range(3)])
            COS = [P.sb(ca, f"COSA{i}", [128, CH], F32) for i in range(2)]
            SIN = [P.sb(ca, f"SINA{i}", [128, CH], F32) for i in range(2)]
            KRAW = Pool([P.sb(ca, f"KRAW{i}", [128, CH], BF16) for i in range(2)])
            T1 = Pool([P.sb(ca, f"T1A{i}", [128, CH], F32) for i in range(2)])
            T2 = Pool([P.sb(ca, f"T2A{i}", [128, CH], F32) for i in range(2)])
            KRO = Pool([P.sb(ca, f"KRO{i}", [128, CH], BF16) for i in range(3)])
            VSB = Pool([P.sb(ca, f"VSB{i}", [128, 1024], BF16) for i in range(2)])
            SSb = P.ps(ca, "SSbA", [128, CH])
            PP = Pool([P.ps(ca, f"PPA{i}", [128, CH]) for i in range(6)])

            def loads_a(c):
                P.load(X[c % 2][:], fm(xT_all[:, c * CH:(c + 1) * CH]), w=[X[c % 2]])
                P.load(COS[c % 2][:], cos_all[:, c * CH:(c + 1) * CH], w=[COS[c % 2]])
                P.load(SIN[c % 2][:], sin_all[:, c * CH:(c + 1) * CH], w=[SIN[c % 2]])

            loads_a(0)
            for c in range(NCH_RUN):
                if c + 1 < NCH_RUN:
                    loads_a(c + 1)
                Xc, Hc, Cc, Sc = X[c % 2], H[c % 2], COS[c % 2], SIN[c % 2]
                if "n" in A_PARTS:
                    norm_H(Xc, Hc, 0, CH, A1, SH1, SQ, SSb, SR, RS, TMPs)
                for n in range(8 if "k" in A_PARTS else 0):
                    KP = PP.next()
                    for j in range(8):
                        P.mm(KP[:], WK[:, j, n * 128:(n + 1) * 128], Hc[:, j, :], j == 0, j == 7, r=[WK, Hc.s((j, 0))], w=[KP])
                    Kr = KRAW.next()
                    P.act(Kr[:], KP[:], AF.Identity, r=[KP], w=[Kr])
                    KR = PP.next()
                    P.mm(KR[:], RM[:], Kr[:], True, True, r=[RM, Kr], w=[KR])
                    t1, t2, ko = T1.next(), T2.next(), KRO.next()
                    P.tt(t1[:], Kr[:], Cc[:], MUL, r=[Kr, Cc], w=[t1])
                    P.tt(t2[:], KR[:], Sc[:], MUL, r=[KR, Sc], w=[t2])
                    P.tt(ko[:], t1[:], t2[:], ADD, r=[t1, t2], w=[ko])
                    P.store(Kscr[n, :, c * CH:(c + 1) * CH], ko[:], r=[ko], q=STQ)
                for tb in range(4 if "v" in A_PARTS else 0):
                    Vs = VSB.next()
                    for cg in range(2):
                        VP = PP.next()
                        for j in range(8):
                            P.mm(VP[:], Hc[:, j, tb * 128:(tb + 1) * 128], WV[:, j, cg * 512:(cg + 1) * 512],
                                 j == 0, j == 7, r=[Hc.s((j, 0)), WV], w=[VP])
                        if cg == 0:
                            P.act(Vs[:, 0:512], VP[:], AF.Identity, r=[VP], w=[Vs.s(0)])
                        else:
                            P.copy(Vs[:, 512:1024], VP[:], r=[VP], w=[Vs.s(1)])
                    r0 = (c * 4 + tb) * 128
                    P.store(Vscr[r0:r0 + 128, :], Vs[:], r=[Vs], q=STQ)
            P.barrier()
            P.emit()

        if "B" in phases:
          with ExitStack() as cb:
            WUA = P.sb(cb, "WUA", [128, 8, 1024], BF16)
            WUB = P.sb(cb, "WUB", [128, 8, 1024], BF16)
            WQ = P.sb(cb, "WQ", [128, 8, 1024], BF16)
            WG = P.sb(cb, "WG", [128, 8, 2048], BF16)
            WCO = P.sb(cb, "WCO", [128, 8, 1024], BF16)
            load_w(WUA, w_in, 1024, 0)
            load_w(WUB, w_in, 1024, 1024)
            load_w(WQ, w_in, 1024, 2048)
            load_w(WG, w_in, 2048, 5120)
            load_w(WCO, w_co, 1024, 0)
            X = P.sb(cb, "XB", [128, 8, XW], F32)
            H = P.sb(cb, "HB", [128, 8, XW], BF16)
            U = P.sb(cb, "UB", [128, 8, XW], BF16)
            DGp = Pool([P.sb(cb, f"DG{i}", [128, 128], BF16) for i in range(8)])
            CV = P.sb(cb, "CVB", [128, 8, CH], F32)
            S = P.sb(cb, "SB", [128, 8, XW], BF16)
            SQ = S
            SR = P.sb(cb, "SRB", [128, CH], F32)
            RS = P.sb(cb, "RSB", [128, CH], F32)
            MEAN = P.sb(cb, "MEANB", [128, CH], F32)
            MSQ = SR
            VAR = RS
            NBt = P.sb(cb, "NBB", [128, CH], F32)
            TMPs = Pool([P.sb(cb, f"TMB{i}", [128, CH], F32) for i in range(4)])
            TMP2 = TMPs
            SG = Pool([P.sb(cb, f"SGB{i}", [128, CH], F32) for i in range(2)])
            CB16 = Pool([P.sb(cb, f"CB16{i}", [128, CH], BF16) for i in range(2)])
            CQ16 = Pool([P.sb(cb, f"CQ16{i}", [128, CH], BF16) for i in range(2)])
            OB16 = Pool([P.sb(cb, f"OB16{i}", [128, CH], BF16) for i in range(3)])
            COS = [P.sb(cb, f"COSB{i}", [128, CH], F32) for i in range(2)]
            SIN = [P.sb(cb, f"SINB{i}", [128, CH], F32) for i in range(2)]
            SSb = P.ps(cb, "SSbB", [128, CH])
            ACC2 = P.ps(cb, "ACC2B", [128, CH])
            PP = Pool([P.ps(cb, f"PPB{i}", [128, CH]) for i in range(6)])
            CW = lambda n, j: SM[:, SM_CW + n * 31 + j:SM_CW + n * 31 + j + 1]

            def loads_b_tabs(i):
                P.load(COS[i % 2][:], cos_own[:, i * CH:(i + 1) * CH], w=[COS[i % 2]])
                P.load(SIN[i % 2][:], sin_own[:, i * CH:(i + 1) * CH], w=[SIN[i % 2]])

            P.load(X[:], fm(xT_own[:, 0:XW]), w=[X])
            loads_b_tabs(0)
            for i in range(NSL):
                if i + 1 < NSL:
                    loads_b_tabs(i + 1)
                Cc, Sc = COS[i % 2], SIN[i % 2]
                norm_H(X, H, HALO, XW, A1, SH1, SQ, SSb, SR, RS, TMPs)
                norm_H(X, H, 0, HALO, A1, SH1, SQ, SSb, SR, RS, TMPs)
                if i + 1 < NSL:
                    P.load(X[:], fm(xT_own[:, (i + 1) * XW:(i + 2) * XW]), w=[X])
                for n in range(8):
                    for (t0, t1) in ((HALO, XW), (0, HALO)):
                        nn = t1 - t0
                        PA, PB = PP.next(), PP.next()
                        for j in range(8):
                            P.mm(PA[:, 0:nn], WUA[:, j, n * 128:(n + 1) * 128], H[:, j, t0:t1], j == 0, j == 7,
                                 r=[WUA, H.s((j, t0))], w=[PA])
                        for j in range(8):
                            P.mm(PB[:, 0:nn], WUB[:, j, n * 128:(n + 1) * 128], H[:, j, t0:t1], j == 0, j == 7,
                                 r=[WUB, H.s((j, t0))], w=[PB])
                        sg = SG.next()
                        P.act(sg[:, 0:nn], PB[:, 0:nn], AF.Sigmoid, r=[PB], w=[sg])
                        if t0 == 0:
                            P.stt(U[:, n, t0:t1], PA[:, 0:nn], SM[:, SM_FLAG + i:SM_FLAG + i + 1], sg[:, 0:nn], MUL, MUL,
                                  r=[PA, sg, SM], w=[U.s((n, t0))])
                        else:
                            P.tt(U[:, n, t0:t1], PA[:, 0:nn], sg[:, 0:nn], MUL, r=[PA, sg], w=[U.s((n, t0))])
                for n in range(8):
                    CA = PP.next()
                    for j in range(31):
                        dg = DGp.next()
                        if j % 2 == 0:
                            P.act(dg[:], IDENT[:], AF.Identity, r=[IDENT, SM], w=[dg], scale=CW(n, j))
                        else:
                            P.ts(dg[:], IDENT[:], CW(n, j), None, MUL, None, r=[IDENT, SM], w=[dg])
                        P.mm(CA[:], dg[:], U[:, n, 2 + j:2 + j + CH], j == 0, j == 30,
                             r=[dg, U.s((n, 0)), U.s((n, HALO))], w=[CA])
                    P.act(CV[:, n, :], CA[:], AF.Identity, r=[CA, SM], w=[CV.s(n)],
                          bias=SM[:, SM_CB + n:SM_CB + n + 1], scale=1.0)
                for n in range(8):
                    cb16, cq16 = CB16.next(), CQ16.next()
                    P.act(cb16[:], CV[:, n, :], AF.Identity, r=[CV.s(n)], w=[cb16])
                    P.act(cq16[:], CV[:, n, :], AF.Square, r=[CV.s(n)], w=[cq16])
                    P.mm(SSb[:], ONESB[:], cb16[:], n == 0, n == 7, r=[ONESB, cb16], w=[SSb])
                    P.mm(ACC2[:], ONESB[:], cq16[:], n == 0, n == 7, r=[ONESB, cq16], w=[ACC2])
                P.ts(MEAN[:], SSb[:], 1.0 / D, None, MUL, None, r=[SSb], w=[MEAN])
                P.tt(MSQ[:], MEAN[:], MEAN[:], MUL, r=[MEAN], w=[MSQ])
                P.stt(VAR[:], ACC2[:], 1.0 / D, MSQ[:], MUL, SUB, r=[ACC2, MSQ], w=[VAR])
                P.act(SR[:], VAR[:], AF.Ln, r=[VAR], w=[SR], bias=EPS, scale=1.0)
                P.act(RS[:], SR[:], AF.Exp, r=[SR], w=[RS], scale=-0.5)
                P.stt(NBt[:], MEAN[:], -1.0, RS[:], MUL, MUL, r=[MEAN, RS], w=[NBt])
                for n in range(8):
                    ta, tb_ = TMPs.next(), TMP2.next()
                    P.tt(ta[:], CV[:, n, :], RS[:], MUL, r=[CV.s(n), RS], w=[ta])
                    P.tt(tb_[:], ta[:], NBt[:], ADD, r=[ta, NBt], w=[tb_])
                    P.act(S[:, n, 0:CH], tb_[:], AF.Silu, r=[tb_, SM], w=[S.s(n)],
                          bias=SM[:, SM_LNB + n:SM_LNB + n + 1], scale=SM[:, SM_LNG + n:SM_LNG + n + 1])
                for n in range(8):
                    PY, PG = PP.next(), PP.next()
                    for j in range(8):
                        P.mm(PY[:], WCO[:, j, n * 128:(n + 1) * 128], S[:, j, 0:CH], j == 0, j == 7, r=[WCO, S.s(j)], w=[PY])
                    for j in range(8):
                        P.mm(PG[:], WG[:, j, n * 128:(n + 1) * 128], H[:, j, HALO:XW], j == 0, j == 7, r=[WG, H.s((j, HALO))], w=[PG])
                    sg = SG.next()
                    P.act(sg[:], PG[:], AF.Sigmoid, r=[PG], w=[sg])
                    ob = OB16.next()
                    P.stt(ob[:], PY[:], SM[:, SM_BOUT + n:SM_BOUT + n + 1], sg[:], ADD, MUL, r=[PY, sg, SM], w=[ob])
                    P.store(ZCscr[n * 128:(n + 1) * 128, i * CH:(i + 1) * CH], ob[:], r=[ob], q=STQ)
                for n in range(8):
                    PG = PP.next()
                    for j in range(8):
                        P.mm(PG[:], WG[:, j, 1024 + n * 128:1024 + (n + 1) * 128], H[:, j, HALO:XW], j == 0, j == 7,
                             r=[WG, H.s((j, HALO))], w=[PG])
                    ob = OB16.next()
                    P.act(ob[:], PG[:], AF.Sigmoid, r=[PG], w=[ob])
                    P.store(GAscr[n * 128:(n + 1) * 128, i * CH:(i + 1) * CH], ob[:], r=[ob], q=STQ)
                for n in range(8):
                    PQ = PP.next()
                    for j in range(8):
                        P.mm(PQ[:], WQ[:, j, n * 128:(n + 1) * 128], H[:, j, HALO:XW], j == 0, j == 7, r=[WQ, H.s((j, HALO))], w=[PQ])
                    qr = CB16.next()
                    P.act(qr[:], PQ[:], AF.Identity, r=[PQ], w=[qr])
                    PR = PP.next()
                    P.mm(PR[:], RM[:], qr[:], True, True, r=[RM, qr], w=[PR])
                    t1, t2, ob = TMPs.next(), TMP2.next(), OB16.next()
                    P.tt(t1[:], qr[:], Cc[:], MUL, r=[qr, Cc], w=[t1])
                    P.tt(t2[:], PR[:], Sc[:], MUL, r=[PR, Sc], w=[t2])
                    P.tt(ob[:], t1[:], t2[:], ADD, r=[t1, t2], w=[ob])
                    P.store(Qscr[n, :, i * CH:(i + 1) * CH], ob[:], r=[ob], q=STQ)
            P.barrier()
            P.emit()

        if "D" in phases:
          with ExitStack() as cd:
            KH = [P.sb(cd, f"KH{i}", [128, T], BF16) for i in range(2)]
            VH = [P.sb(cd, f"VH{i}", [128, 64, 128], BF16) for i in range(2)]
            QH = [P.sb(cd, f"QH{i}", [128, NSL * CH], BF16) for i in range(2)]
            GAH = [P.sb(cd, f"GAH{i}", [128, NSL * CH], BF16) for i in range(2)]
            ZCH = [P.sb(cd, f"ZCH{i}", [128, NSL * CH], BF16) for i in range(2)]
            MK = P.sb(cd, "MK", [128, 16, CH], BF16)
            ONESF = P.sb(cd, "ONESF", [128, 128], F32)
            PT = Pool([P.sb(cd, f"PT{i}", [128, 2 * CH], BF16) for i in range(10)])
            ACC = [P.sb(cd, f"ACCD{i}", [128, 2 * CH], F32) for i in range(2)]
            R1 = P.sb(cd, "R1", [128, CH], F32)
            R2 = P.sb(cd, "R2", [128, CH], F32)
            O1S = Pool([P.sb(cd, f"O1S{i}", [128, CH], F32) for i in range(2)])
            O2S = Pool([P.sb(cd, f"O2S{i}", [128, CH], F32) for i in range(2)])
            AT1 = P.sb(cd, "AT1", [128, CH], F32)
            AT2 = P.sb(cd, "AT2", [128, CH], F32)
            ATT = Pool([P.sb(cd, f"ATT{i}", [128, CH], F32) for i in range(2)])
            SQ16 = Pool([P.sb(cd, f"SQ16{i}", [128, CH], BF16) for i in range(2)])
            LNd = P.sb(cd, "LNd", [128, CH], F32)
            RSd = P.sb(cd, "RSd", [128, CH], F32)
            YA = P.sb(cd, "YA", [128, CH], F32)
            ZA = P.sb(cd, "ZA", [128, CH], F32)
            ZO = Pool([P.sb(cd, f"ZO{i}", [128, CH], BF16) for i in range(2)])
            SP = Pool([P.ps(cd, f"SP{i}", [128, 2 * CH]) for i in range(2)])
            O1 = P.ps(cd, "O1", [128, CH])
            O2 = P.ps(cd, "O2", [128, CH])
            L1 = P.ps(cd, "L1", [128, CH])
            L2 = P.ps(cd, "L2", [128, CH])
            P.memset(ONESF[:], 1.0, w=[ONESF])
            for mi in range(16):
                P.load(MK[:, mi, :], masks[:, mi * CH:(mi + 1) * CH], w=[MK], q="gpsimd")

            def loads_d(h):
                b = h % 2
                P.load(KH[b][:], Kscr[h], w=[KH[b]])
                vsrc = Vscr[:, h * 128:(h + 1) * 128].rearrange("(kb p) e -> p kb e", p=128)
                for q4 in range(4):
                    P.load(VH[b][:, q4 * 16:(q4 + 1) * 16, :], vsrc[:, q4 * 16:(q4 + 1) * 16, :], w=[VH[b]])
                P.load(QH[b][:], Qscr[h], w=[QH[b]])
                P.load(GAH[b][:], GAscr[h * 128:(h + 1) * 128, :], w=[GAH[b]])
                P.load(ZCH[b][:], ZCscr[h * 128:(h + 1) * 128, :], w=[ZCH[b]])

            pending = []

            def run_pending(kb):
                while pending and (kb is None or pending[0][0] <= kb):
                    pending.pop(0)[1]()

            def make_finalize(h, i, Gh, Zh, qs):
                acc, o1s, o2s = ACC[i % 2], O1S.next(), O2S.next()
                att, sq = ATT.next(), SQ16.next()

                def st1():
                    P.copy(o1s[:], O1[:], r=[O1], w=[o1s])
                    P.copy(o2s[:], O2[:], r=[O2], w=[o2s])

                def st2():
                    P.mm(L1[:], ONESF[:], acc[:, 0:CH], False, True, r=[ONESF, acc], w=[L1])
                    P.mm(L2[:], ONESF[:], acc[:, CH:2 * CH], False, True, r=[ONESF, acc], w=[L2])

                def st3():
                    P.act(R1[:], L1[:], AF.Ln, r=[L1], w=[R1])
                    P.act(R1[:], R1[:], AF.Exp, r=[R1], w=[R1], scale=-1.0)
                    P.act(R2[:], L2[:], AF.Ln, r=[L2], w=[R2])
                    P.act(R2[:], R2[:], AF.Exp, r=[R2], w=[R2], scale=-1.0)
                    P.tt(AT1[:], o1s[:], R1[:], MUL, r=[o1s, R1], w=[AT1])
                    P.tt(AT2[:], o2s[:], R2[:], MUL, r=[o2s, R2], w=[AT2])
                    P.stt(att[:], AT2[:], NEGLAM, AT1[:], MUL, ADD, r=[AT2, AT1, DV], w=[att])

                def st6():
                    P.act(sq[:], att[:], AF.Square, r=[att], w=[sq])

                ms = []

                def st7():
                    MS = SP.next()
                    ms.append(MS)
                    P.mm(MS[:, 0:CH], ONESB[:], sq[:], True, True, r=[ONESB, sq], w=[MS])

                def st8():
                    P.act(LNd[:], ms[0][:, 0:CH], AF.Ln, r=[ms[0]], w=[LNd], bias=EPS, scale=1.0 / 128)
                    P.act(RSd[:], LNd[:], AF.Exp, r=[LNd], w=[RSd], scale=-0.5)

                def st9():
                    P.tt(YA[:], att[:], RSd[:], MUL, r=[att, RSd], w=[YA])
                    P.stt(ZA[:], YA[:], HNV, Gh[:, qs], MUL, MUL, r=[YA, DV, Gh], w=[ZA])
                    zo = ZO.next()
                    P.tt(zo[:], ZA[:], Zh[:, qs], ADD, r=[ZA, Zh], w=[zo])
                    P.store(Zscr[h * 128:(h + 1) * 128, qs], zo[:], r=[zo], q=STQ)

                st1()
                pending.extend([(0, st2), (1, st3), (3, st6), (4, st7), (5, st8), (6, st9)])

            loads_d(0)
            for h in range(8):
                if h + 1 < 8:
                    loads_d(h + 1)
                b = h % 2
                Kh, Vh, Qh, Gh, Zh = KH[b], VH[b], QH[b], GAH[b], ZCH[b]
                for i in range(NSL):
                    nk = 8 * (i + 1)
                    qs = slice(i * CH, (i + 1) * CH)

                    def smm(kb):
                        Sx = SP.next()
                        ks = slice(kb * 128, (kb + 1) * 128)
                        P.mm(Sx[:, 0:CH], Kh[0:64, ks], Qh[0:64, qs], True, True, r=[Kh, Qh], w=[Sx.s(0)])
                        P.mm(Sx[:, CH:2 * CH], Kh[64:128, ks], Qh[64:128, qs], True, True, r=[Kh, Qh], w=[Sx.s(1)])
                        return Sx

                    acc = ACC[i % 2]

                    def av(kb, pt):
                        st, sp_ = (kb == 0), (kb == nk - 1)
                        P.mm(O1[:], Vh[:, kb, :], pt[:, 0:CH], st, sp_, r=[Vh, pt.s(0)], w=[O1])
                        P.mm(O2[:], Vh[:, kb, :], pt[:, CH:2 * CH], st, sp_, r=[Vh, pt.s(1)], w=[O2])
                        if kb % 5 in (2, 4):
                            P.mm(L1[:], ONESB[:], pt[:, 0:CH], kb == 2, False, r=[ONESB, pt.s(0)], w=[L1])
                            P.mm(L2[:], ONESB[:], pt[:, CH:2 * CH], kb == 2, False, r=[ONESB, pt.s(1)], w=[L2])

                    Snext = smm(0)
                    prev = None
                    for kb in range(nk):
                        Sx = Snext
                        pt = PT.next()
                        P.act(pt[:], Sx[:], AF.Exp, r=[Sx], w=[pt], scale=0.125)
                        if kb >= nk - 8:
                            jj = kb - (nk - 8)
                            midx = (i % 2) * 8 + jj
                            P.tt(pt[:, 0:CH], pt[:, 0:CH], MK[:, midx, :], MUL, r=[pt.s(0), MK], w=[pt.s(0)])
                            P.tt(pt[:, CH:2 * CH], pt[:, CH:2 * CH], MK[:, midx, :], MUL, r=[pt.s(1), MK], w=[pt.s(1)])
                        if kb % 5 not in (2, 4):
                            if kb == 0:
                                P.copy(acc[:], pt[:], r=[pt], w=[acc])
                            else:
                                P.tt(acc[:], acc[:], pt[:], ADD, r=[acc, pt], w=[acc])
                        if kb + 1 < nk:
                            Snext = smm(kb + 1)
                        if prev is not None:
                            av(*prev)
                        prev = (kb, pt)
                        run_pending(kb)
                    av(*prev)
                    make_finalize(h, i, Gh, Zh, qs)
                run_pending(None)
            P.barrier()
            P.emit()

        if "C" in phases:
          with ExitStack() as cc:
            SC = 256
            WO = P.sb(cc, "WO", [128, 8, 1024], BF16)
            WF1 = P.sb(cc, "WF1", [128, 8, 4096], BF16)
            WF2 = P.sb(cc, "WF2", [128, 32, 1024], BF16)
            load_w(WO, w_o, 1024, 0)
            load_w(WF1, w_ff1, 4096, 0)
            for q4 in range(4):
                for cg in range(2):
                    P.load(WF2[:, q4 * 8:(q4 + 1) * 8, cg * 512:(cg + 1) * 512],
                           fm(w_ff2[q4 * 1024:(q4 + 1) * 1024, cg * 512:(cg + 1) * 512]), w=[WF2], q="gpsimd")
            Z = [P.sb(cc, f"ZC{i}", [128, 8, SC], BF16) for i in range(2)]
            X = P.sb(cc, "XC", [128, 8, SC], F32)
            Y = P.sb(cc, "YC", [128, 8, SC], F32)
            H2 = P.sb(cc, "H2C", [128, 8, SC], BF16)
            F1 = P.sb(cc, "F1C", [128, 32, SC], BF16)
            SQ = P.sb(cc, "SQC", [128, 8, SC], BF16)
            SR = P.sb(cc, "SRC", [128, SC], F32)
            RS = P.sb(cc, "RSC", [128, SC], F32)
            TMPs = Pool([P.sb(cc, f"TMC{i}", [128, SC], F32) for i in range(3)])
            RL = Pool([P.sb(cc, f"RLC{i}", [128, SC], F32) for i in range(3)])
            Q16 = Pool([P.sb(cc, f"Q16C{i}", [128, SC], BF16) for i in range(2)])
            ACC = P.ps(cc, "ACCC", [128, CH])
            SSb = P.ps(cc, "SSbC", [128, CH])
            PP = Pool([P.ps(cc, f"PPC{i}", [128, CH]) for i in range(6)])
            NSC = NSL * CH // SC

            def xcols(s):
                slot, hf = divmod(s, CH // SC)
                o = slot * XW + HALO + hf * SC
                return slice(o, o + SC)

            def load_z(s):
                P.load(Z[s % 2][:], fm(Zscr[:, s * SC:(s + 1) * SC]), w=[Z[s % 2]])

            load_z(0)
            P.load(X[:], fm(xT_own[:, xcols(0)]), w=[X])
            for s in range(NSC):
                if s + 1 < NSC:
                    load_z(s + 1)
                Zs = Z[s % 2]
                for n in range(8):
                    PY = PP.next()
                    for j in range(8):
                        P.mm(PY[:, 0:SC], WO[:, j, n * 128:(n + 1) * 128], Zs[:, j, :], j == 0, j == 7, r=[WO, Zs], w=[PY])
                    P.act(Y[:, n, :], PY[:, 0:SC], AF.Identity, r=[PY], w=[Y.s(n)])
                    q16 = Q16.next()
                    P.act(q16[:], PY[:, 0:SC], AF.Square, r=[PY], w=[q16])
                    P.mm(ACC[:, 0:SC], ONESB[:], q16[:], n == 0, n == 7, r=[ONESB, q16], w=[ACC])
                P.act(SR[:], ACC[:, 0:SC], AF.Ln, r=[ACC], w=[SR], bias=EPS, scale=1.0 / D)
                P.act(RS[:], SR[:], AF.Exp, r=[SR], w=[RS], scale=-0.5)
                for n in range(8):
                    tm = TMPs.next()
                    P.stt(tm[:], Y[:, n, :], GP1(n), RS[:], MUL, MUL, r=[Y.s(n), DV, RS], w=[tm])
                    P.tt(X[:, n, :], X[:, n, :], tm[:], ADD, r=[X.s(n), tm], w=[X.s(n)])
                P.act(SQ[:], X[:], AF.Square, r=[X], w=[SQ])
                for j in range(8):
                    P.mm(SSb[:, 0:SC], ONESB[:], SQ[:, j, :], j == 0, j == 7, r=[ONESB, SQ], w=[SSb])
                P.act(SR[:], SSb[:, 0:SC], AF.Ln, r=[SSb], w=[SR], bias=EPS, scale=1.0 / D)
                P.act(RS[:], SR[:], AF.Exp, r=[SR], w=[RS], scale=-0.5)
                for j in range(8):
                    tm = TMPs.next()
                    P.stt(tm[:], X[:, j, :], A2(j), RS[:], MUL, MUL, r=[X.s(j), DV, RS], w=[tm])
                    P.act(H2[:, j, :], tm[:], AF.Identity, r=[tm, MODV], w=[H2.s(j)], bias=SH2(j), scale=1.0)
                for m in range(32):
                    PF = PP.next()
                    for j in range(8):
                        P.mm(PF[:, 0:SC], WF1[:, j, m * 128:(m + 1) * 128], H2[:, j, :], j == 0, j == 7, r=[WF1, H2.s(j)], w=[PF])
                    rl = RL.next()
                    P.ts(rl[:], PF[:, 0:SC], 0.0, None, MAX, None, r=[PF], w=[rl])
                    P.act(F1[:, m, :], rl[:], AF.Square, r=[rl], w=[F1.s(m)])
                for n in range(8):
                    PO = PP.next()
                    for m in range(32):
                        P.mm(PO[:, 0:SC], WF2[:, m, n * 128:(n + 1) * 128], F1[:, m, :], m == 0, m == 31, r=[WF2, F1.s(m)], w=[PO])
                    P.act(Y[:, n, :], PO[:, 0:SC], AF.Identity, r=[PO], w=[Y.s(n)])
                    q16 = Q16.next()
                    P.act(q16[:], PO[:, 0:SC], AF.Square, r=[PO], w=[q16])
                    P.mm(ACC[:, 0:SC], ONESB[:], q16[:], n == 0, n == 7, r=[ONESB, q16], w=[ACC])
                P.act(SR[:], ACC[:, 0:SC], AF.Ln, r=[ACC], w=[SR], bias=EPS, scale=1.0 / D)
                P.act(RS[:], SR[:], AF.Exp, r=[SR], w=[RS], scale=-0.5)
                for n in range(8):
                    tm = TMPs.next()
                    P.stt(tm[:], Y[:, n, :], GP2(n), RS[:], MUL, MUL, r=[Y.s(n), DV, RS], w=[tm])
                    P.tt(X[:, n, :], X[:, n, :], tm[:], ADD, r=[X.s(n), tm], w=[X.s(n)])
                P.store(fm(outT[:, s * SC:(s + 1) * SC]), X[:], r=[X], q=STQ)
                if s + 1 < NSC:
                    P.load(X[:], fm(xT_own[:, xcols(s + 1)]), w=[X])
            P.barrier()
            P.emit()
    return nc


_DBG = None
STQ = "sync"
import os
NCH_RUN = int(os.environ.get("NCH_RUN", "16"))
A_PARTS = os.environ.get("A_PARTS", "nkv")
DBG_OUT = os.environ.get("DBG_OUT", "1") == "1"


def _fmv(v, n):
    return np.ascontiguousarray(np.asarray(v, np.float32).reshape(n, 128).T)


def _host_constants():
    inv_freq = (np.float32(10000.0) ** (-np.arange(0, 64, 2, dtype=np.float32) / np.float32(64))).astype(np.float32)
    pos = np.arange(T, dtype=np.float32)
    ang = (pos[:, None] * inv_freq[None, :]).astype(np.float32)
    cos = np.cos(ang).astype(np.float32).T
    sin = np.sin(ang).astype(np.float32).T
    cos_all = np.ascontiguousarray(np.tile(cos, (4, 1)))
    sin_all = np.ascontiguousarray(np.tile(sin, (4, 1)))
    rm = np.zeros((128, 128), np.float32)
    for blk in (0, 64):
        for d in range(32):
            rm[blk + d + 32, blk + d] = -1.0
            rm[blk + d, blk + d + 32] = 1.0
    k = np.arange(128)[:, None]
    q = np.arange(CH)[None, :]
    diag = [((j * 128 + k) <= q).astype(np.float32) for j in range(4)]
    ones = [np.ones((128, CH), np.float32)] * 4
    zeros = [np.zeros((128, CH), np.float32)] * 4
    return cos_all, sin_all, rm, diag, ones, zeros


def kernel(x, c, w_ada, b_ada, pre_norm1, post_norm1, w_in, conv_w, conv_b, conv_ln_g, conv_ln_b,
           conv_w_out, conv_b_out, lambda_q1, lambda_k1, lambda_q2, lambda_k2, head_norm, w_o,
           pre_norm2, post_norm2, w_ff1, w_ff2):
    x = np.asarray(x, np.float32)
    c = np.asarray(c, np.float32)
    cos_all, sin_all, rm, diag, ones, zeros = _host_constants()
    w_in0 = np.asarray(w_in, np.float32)[0]
    perm = np.arange(7168)
    for base in (2048, 3072):
        for h in range(8):
            for m in range(2):
                perm[base + h * 128 + m * 64: base + h * 128 + m * 64 + 64] = base + m * 512 + h * 64 + np.arange(64)
    w_in_p = np.ascontiguousarray(w_in0[:, perm])
    w_ada0 = np.ascontiguousarray(np.asarray(w_ada, np.float32)[0])
    w_co0 = np.ascontiguousarray(np.asarray(conv_w_out, np.float32)[0])
    w_o0 = np.ascontiguousarray(np.asarray(w_o, np.float32)[0])
    w_ff10 = np.ascontiguousarray(np.asarray(w_ff1, np.float32)[0])
    w_ff20 = np.ascontiguousarray(np.asarray(w_ff2, np.float32)[0])

    in_maps = []
    for core in range(8):
        b, jh = divmod(core, 2)
        own = OWN[jh]
        xT = np.ascontiguousarray(x[b].T)
        xo = np.zeros((D, NSL, XW), np.float32)
        flag = np.ones((128, NSL), np.float32)
        for i, ch in enumerate(own):
            lo = ch * CH - HALO
            if lo < 0:
                xo[:, i, HALO:] = xT[:, 0:CH]
                flag[:, i] = 0.0
            else:
                xo[:, i, :] = xT[:, lo:lo + XW]
        small = np.zeros((128, NS), np.float32)
        small[:, SM_BADA:SM_BADA + 48] = _fmv(b_ada[0], 48)
        small[:, SM_PN1:SM_PN1 + 8] = _fmv(pre_norm1[0], 8)
        small[:, SM_PO1:SM_PO1 + 8] = _fmv(post_norm1[0], 8)
        small[:, SM_PN2:SM_PN2 + 8] = _fmv(pre_norm2[0], 8)
        small[:, SM_PO2:SM_PO2 + 8] = _fmv(post_norm2[0], 8)
        small[:, SM_CB:SM_CB + 8] = _fmv(conv_b[0], 8)
        small[:, SM_LNG:SM_LNG + 8] = _fmv(conv_ln_g[0], 8)
        small[:, SM_LNB:SM_LNB + 8] = _fmv(conv_ln_b[0], 8)
        small[:, SM_BOUT:SM_BOUT + 8] = _fmv(conv_b_out[0], 8)
        small[:, SM_HN] = np.asarray(head_norm, np.float32)[0]
        small[:, SM_FLAG:SM_FLAG + 8] = flag
        small[:, SM_CT:SM_CT + 16] = np.repeat(_fmv(c[b], 8), 2, axis=1)
        for k, v in enumerate((lambda_q1, lambda_k1, lambda_q2, lambda_k2)):
            small[:, SM_LAM + 64 * k:SM_LAM + 64 * (k + 1)] = np.asarray(v, np.float32)[0][None, :]
        cw = np.asarray(conv_w, np.float32)[0]
        small[:, SM_CW:SM_CW + 248] = cw.T.reshape(8, 128, 31).transpose(1, 0, 2).reshape(128, 248)
        idx = np.concatenate([np.arange(ch * CH, (ch + 1) * CH) for ch in own])
        mk = []
        for par in range(2):
            own_is_2i = (par == 0) if jh == 0 else (par == 1)
            mk += (diag + zeros) if own_is_2i else (ones + diag)
        in_maps.append({
            "xT_all": xT, "xT_own": xo.reshape(D, NSL * XW), "w_ada": w_ada0, "w_in": w_in_p, "w_co": w_co0,
            "w_o": w_o0, "w_ff1": w_ff10, "w_ff2": w_ff20, "small": small,
            "cos_all": cos_all, "sin_all": sin_all,
            "cos_own": np.ascontiguousarray(cos_all[:, idx]), "sin_own": np.ascontiguousarray(sin_all[:, idx]),
            "masks": np.ascontiguousarray(np.concatenate(mk, axis=1)), "rmat": rm,
            "ident": np.eye(128, dtype=np.float32),
        })
    if _DBG is not None:
        nc = build_program(_DBG, dbg=DBG_OUT)
        res = run_bass_kernel_spmd(nc, in_maps, core_ids=list(range(8)))
        return res, in_maps
    nc = build_program()
    res = run_bass_kernel_spmd(nc, in_maps, core_ids=list(range(8)))
    out = np.empty((NB, T, D), np.float32)
    for core in range(8):
        b, jh = divmod(core, 2)
        oT = np.asarray(res.results[core]["outT"], np.float32)
        for i, ch in enumerate(OWN[jh]):
            out[b, ch * CH:(ch + 1) * CH, :] = oT[:, i * CH:(i + 1) * CH].T
    return out
```
